# Optimizing a Trainium2 kernel written in Bass

```python
import jax, jax.numpy as jnp
from jax import lax
import numpy as np

D_MODEL = 1024
BATCH = 16
SEQ = 2048
DEPTH = 2

CTX_LEN = 256
GRID_W = 64
HEAD_DIM = 64
N_Q_HEADS = 4
N_KV_HEADS = 2
Q_PER_KV = N_Q_HEADS // N_KV_HEADS
Q_W = N_Q_HEADS * HEAD_DIM
KV_W = N_KV_HEADS * HEAD_DIM
FOURIER_GROUPS = 4
FOURIER_GROUP_W = 64
F_W = FOURIER_GROUPS * FOURIER_GROUP_W
N_BRANCH = 4
BRANCH_W = 256
IN_W = 3 * (Q_W + 2 * KV_W) + F_W + N_BRANCH * D_MODEL
Q_BLOCK = 128
WINDOW = 128
NA_KH_MAX = 8
NA_KW = 16
ROPE_THETA = 10000.0
N_EXPERTS = 16
EXPERT_FF = 1024
EC_CAPACITY = 2
N_MOD = 6
DN_ALPHA = (2 * DEPTH) ** 0.25
DN_BETA = (8 * DEPTH) ** -0.25
LN_EPS = 1e-6
RMS_EPS = 1e-6
NEG_INF = -1e30

kernel_name = "hybrid_gated_mixers_ec_moe_dit"


def layer_norm(h, g, b):
    hf = h.astype(jnp.float32)
    mu = jnp.mean(hf, -1, keepdims=True)
    var = jnp.mean(jnp.square(hf - mu), -1, keepdims=True)
    return ((hf - mu) * lax.rsqrt(var + LN_EPS)).astype(h.dtype) * g + b


def rms_norm(h, g):
    hf = h.astype(jnp.float32)
    return (hf * lax.rsqrt(jnp.mean(hf * hf, -1, keepdims=True) + RMS_EPS)).astype(h.dtype) * g


def heads(t, n):
    return t.reshape(t.shape[:-1] + (n, HEAD_DIM))


def group_q(q):
    return q.reshape(q.shape[:-2] + (N_KV_HEADS, Q_PER_KV, HEAD_DIM))


def flat_heads(o):
    return o.reshape(o.shape[:2] + (Q_W,))


def split_columns(p):
    widths = (Q_W, KV_W, KV_W, F_W, Q_W, KV_W, KV_W, Q_W, KV_W, KV_W)
    offs, o = [], 0
    for w in widths:
        o += w
        offs.append(o)
    return jnp.split(p, offs, axis=-1)


def axial_rope_tables(n_tokens, dtype):
    t = jnp.arange(n_tokens)
    axis_dim = HEAD_DIM // 2
    inv = ROPE_THETA ** (-jnp.arange(0, axis_dim, 2, dtype=jnp.float32) / axis_dim)
    out = []
    for pos in (t // GRID_W, t % GRID_W):
        ang = pos.astype(jnp.float32)[:, None, None] * inv
        out.append(jnp.cos(ang).astype(dtype))
        out.append(jnp.sin(ang).astype(dtype))
    return tuple(out)


def rotate(t, cos, sin):
    h = t.shape[-1] // 2
    t1, t2 = t[..., :h], t[..., h:]
    return jnp.concatenate([t1 * cos - t2 * sin, t1 * sin + t2 * cos], -1)


def axial_rope(t, rope):
    cos_r, sin_r, cos_c, sin_c = rope
    half = HEAD_DIM // 2
    return jnp.concatenate([rotate(t[..., :half], cos_r, sin_r),
                            rotate(t[..., half:], cos_c, sin_c)], -1)


def neighbourhood_tables(n_tokens):
    rows = n_tokens // GRID_W
    kh = min(NA_KH_MAX, rows)
    t = jnp.arange(n_tokens)
    r, col = t // GRID_W, t % GRID_W
    r0 = jnp.clip(r - kh // 2, 0, rows - kh)
    c0 = jnp.clip(col - NA_KW // 2, 0, GRID_W - NA_KW)
    kr = r0[:, None, None] + jnp.arange(kh)[None, :, None]
    kc = c0[:, None, None] + jnp.arange(NA_KW)[None, None, :]
    shape = (n_tokens, kh, NA_KW)
    idx = jnp.broadcast_to(kr * GRID_W + kc, shape).reshape(n_tokens, -1)
    off_r = jnp.broadcast_to(kr - r[:, None, None] + NA_KH_MAX - 1, shape).reshape(n_tokens, -1)
    off_c = jnp.broadcast_to(kc - col[:, None, None] + NA_KW - 1, shape).reshape(n_tokens, -1)
    return idx, off_r, off_c


def gqa_softmax(q, k, v, bias=None, sink=None):
    s = jnp.einsum("bqhgd,bkhd->bhgqk", q, k).astype(jnp.float32) * HEAD_DIM ** -0.5
    if bias is not None:
        s = s + bias
    if sink is not None:
        sk = jnp.broadcast_to(sink.astype(jnp.float32)[None, :, :, None, None], s.shape[:-1] + (1,))
        p = jax.nn.softmax(jnp.concatenate([s, sk], -1), axis=-1)[..., :-1]
    else:
        p = jax.nn.softmax(s, axis=-1)
    return jnp.einsum("bhgqk,bkhd->bqhgd", p.astype(v.dtype), v)


def to_blocks(q):
    B, S = q.shape[:2]
    return jnp.swapaxes(q.reshape((B, S // Q_BLOCK, Q_BLOCK) + q.shape[2:]), 0, 1)


def from_blocks(ob, like):
    return flat_heads(jnp.swapaxes(ob, 0, 1).reshape(like.shape))


def global_attention(q, k_all, v_all):
    ob = lax.map(lambda qq: gqa_softmax(qq, k_all, v_all), to_blocks(q))
    return from_blocks(ob, q)


def window_attention(q, k_lat, v_lat, k_ctx, v_ctx, sink):
    S = q.shape[1]
    nblk = S // Q_BLOCK
    span = Q_BLOCK + 2 * WINDOW
    pad = ((0, 0), (WINDOW, WINDOW), (0, 0), (0, 0))
    k_pad, v_pad = jnp.pad(k_lat, pad), jnp.pad(v_lat, pad)
    ctx_bias = jnp.zeros((Q_BLOCK, k_ctx.shape[1]), jnp.float32)

    def one_block(args):
        qq, blk = args
        start = blk * Q_BLOCK
        kk = lax.dynamic_slice_in_dim(k_pad, start, span, axis=1)
        vv = lax.dynamic_slice_in_dim(v_pad, start, span, axis=1)
        q_pos = start + jnp.arange(Q_BLOCK)
        k_pos = start - WINDOW + jnp.arange(span)
        ok = (jnp.abs(q_pos[:, None] - k_pos[None, :]) <= WINDOW) & (k_pos >= 0) & (k_pos < S)
        bias = jnp.concatenate([ctx_bias, jnp.where(ok, 0.0, NEG_INF)], -1)
        return gqa_softmax(qq, jnp.concatenate([k_ctx, kk], 1), jnp.concatenate([v_ctx, vv], 1), bias, sink)

    ob = lax.map(one_block, (to_blocks(q), jnp.arange(nblk)))
    return from_blocks(ob, q)


def neighbourhood_attention(q, k_lat, v_lat, k_ctx, v_ctx, rpb, na_tab):
    idx, off_r, off_c = na_tab
    S = q.shape[1]
    nblk = S // Q_BLOCK
    n_nb = idx.shape[1]
    n_ctx = k_ctx.shape[1]
    scale = HEAD_DIM ** -0.5
    rpb_g = rpb.reshape((N_KV_HEADS, Q_PER_KV) + rpb.shape[1:]).astype(jnp.float32)
    bias = rpb_g[:, :, off_r, off_c]
    bias_b = jnp.moveaxis(bias.reshape(N_KV_HEADS, Q_PER_KV, nblk, Q_BLOCK, n_nb), 2, 0)
    idx_b = idx.reshape(nblk, Q_BLOCK, n_nb)

    def one_block(args):
        qq, ib, bb = args
        kk, vv = k_lat[:, ib], v_lat[:, ib]
        s_ctx = jnp.einsum("bqhgd,bchd->bhgqc", qq, k_ctx).astype(jnp.float32) * scale
        s_nb = jnp.einsum("bqhgd,bqkhd->bhgqk", qq, kk).astype(jnp.float32) * scale + bb
        p = jax.nn.softmax(jnp.concatenate([s_ctx, s_nb], -1), axis=-1).astype(v_lat.dtype)
        return (jnp.einsum("bhgqc,bchd->bqhgd", p[..., :n_ctx], v_ctx)
                + jnp.einsum("bhgqk,bqkhd->bqhgd", p[..., n_ctx:], vv))

    ob = lax.map(one_block, (to_blocks(q), idx_b, bias_b))
    return from_blocks(ob, q)


def fourier_mix(u):
    g = u.reshape(u.shape[:-1] + (FOURIER_GROUPS, FOURIER_GROUP_W)).astype(jnp.float32)
    f = jnp.fft.fft2(g, axes=(-3, -1), norm="ortho").real
    return f.reshape(u.shape).astype(u.dtype)


def merge_branches(branches, gate_logits, w_branch, w_out):
    gates = jax.nn.sigmoid(gate_logits)
    merged = gates[..., :D_MODEL] * (branches[0] @ w_branch[0])
    for i in range(1, N_BRANCH):
        merged = merged + gates[..., i * D_MODEL:(i + 1) * D_MODEL] * (branches[i] @ w_branch[i])
    return merged @ w_out


def token_mixers(u_lat, u_ctx, w_in, qk_gain, sink_logit, na_rpb, w_branch, w_out, rope, na_tab, with_ctx):
    pl = split_columns(u_lat @ w_in)
    pc = split_columns(u_ctx @ w_in)
    sink = sink_logit.reshape(N_KV_HEADS, Q_PER_KV)
    qa = group_q(axial_rope(rms_norm(heads(pl[0], N_Q_HEADS), qk_gain[0]), rope))
    ka = axial_rope(rms_norm(heads(pl[1], N_KV_HEADS), qk_gain[1]), rope)
    va = heads(pl[2], N_KV_HEADS)
    ka_c = rms_norm(heads(pc[1], N_KV_HEADS), qk_gain[1])
    va_c = heads(pc[2], N_KV_HEADS)
    br_a = global_attention(qa, jnp.concatenate([ka_c, ka], 1), jnp.concatenate([va_c, va], 1))
    br_b = fourier_mix(pl[3])
    qc = group_q(axial_rope(heads(pl[4], N_Q_HEADS), rope))
    kc = axial_rope(heads(pl[5], N_KV_HEADS), rope)
    vc = heads(pl[6], N_KV_HEADS)
    kc_c, vc_c = heads(pc[5], N_KV_HEADS), heads(pc[6], N_KV_HEADS)
    br_c = window_attention(qc, kc, vc, kc_c, vc_c, sink)
    qd = group_q(heads(pl[7], N_Q_HEADS))
    kd, vd = heads(pl[8], N_KV_HEADS), heads(pl[9], N_KV_HEADS)
    kd_c, vd_c = heads(pc[8], N_KV_HEADS), heads(pc[9], N_KV_HEADS)
    br_d = neighbourhood_attention(qd, kd, vd, kd_c, vd_c, na_rpb, na_tab)
    y_lat = merge_branches([br_a, br_b, br_c, br_d], pl[10], w_branch, w_out)
    if not with_ctx:
        return y_lat, None
    qa_c = group_q(rms_norm(heads(pc[0], N_Q_HEADS), qk_gain[0]))
    qc_c = group_q(heads(pc[4], N_Q_HEADS))
    qd_c = group_q(heads(pc[7], N_Q_HEADS))
    br_ctx = [flat_heads(gqa_softmax(qa_c, ka_c, va_c)),
              fourier_mix(pc[3]),
              flat_heads(gqa_softmax(qc_c, kc_c, vc_c, sink=sink)),
              flat_heads(gqa_softmax(qd_c, kd_c, vd_c))]
    y_ctx = merge_branches(br_ctx, pc[10], w_branch, w_out)
    return y_lat, y_ctx


def expert_choice_ffn(h, w_router, w_gate, w_up, w_down):
    B, n, D = h.shape
    cap = max(1, EC_CAPACITY * n // N_EXPERTS)
    aff = jax.nn.softmax(jnp.einsum("bnd,de->bne", h, w_router).astype(jnp.float32), axis=-1)
    sel_w, sel_idx = lax.top_k(jnp.swapaxes(aff, 1, 2), cap)
    xs = jax.vmap(lambda hb, ib: hb[ib])(h, sel_idx)
    a = jnp.einsum("becd,edf->becf", xs, w_gate)
    u = jnp.einsum("becd,edf->becf", xs, w_up)
    y = jnp.einsum("becf,efd->becd", jax.nn.silu(a) * u, w_down) * sel_w[..., None].astype(h.dtype)
    scatter = lambda yb, ib: jnp.zeros((n, D), h.dtype).at[ib.reshape(-1)].add(yb.reshape(-1, D))
    return jax.vmap(scatter)(y, sel_idx)


def trunk_layer(h_lat, h_ctx, c, c_ctx, w_mod, b_mod, w_in, qk_gain, sink_logit, na_rpb, w_branch,
                w_out, ln1_g, ln1_b, w_router, w_gate, w_up, w_down, ln2_g, ln2_b, rope, na_tab, with_ctx):
    m_lat = (jax.nn.silu(c) @ w_mod + b_mod)[:, None, :]
    m_ctx = jax.nn.silu(c_ctx) @ w_mod + b_mod
    sh1, sc1, g1, sh2, sc2, g2 = jnp.split(m_lat, N_MOD, -1)
    csh1, csc1, cg1, csh2, csc2, cg2 = jnp.split(m_ctx, N_MOD, -1)
    u_lat = h_lat * (1.0 + sc1) + sh1
    u_ctx = h_ctx * (1.0 + csc1) + csh1
    y_lat, y_ctx = token_mixers(u_lat, u_ctx, w_in, qk_gain, sink_logit, na_rpb, w_branch, w_out,
                                rope, na_tab, with_ctx)
    h_lat = layer_norm(DN_ALPHA * h_lat + g1 * y_lat, ln1_g, ln1_b)
    u_lat = h_lat * (1.0 + sc2) + sh2
    h_lat = layer_norm(DN_ALPHA * h_lat + g2 * expert_choice_ffn(u_lat, w_router, w_gate, w_up, w_down),
                       ln2_g, ln2_b)
    if with_ctx:
        h_ctx = layer_norm(DN_ALPHA * h_ctx + cg1 * y_ctx, ln1_g, ln1_b)
        u_ctx = h_ctx * (1.0 + csc2) + csh2
        h_ctx = layer_norm(DN_ALPHA * h_ctx + cg2 * expert_choice_ffn(u_ctx, w_router, w_gate, w_up, w_down),
                           ln2_g, ln2_b)
    return h_lat, h_ctx


def setup_inputs(seed: int = 0) -> dict:
    key = jax.random.key(seed)
    ks = jax.random.split(key, 22)
    L, D, E, F = DEPTH, D_MODEL, N_EXPERTS, EXPERT_FF

    def nrm(k, shape, s):
        return jax.random.normal(k, shape, jnp.float32) * s

    return {
        "x": nrm(ks[0], (BATCH, SEQ, D), 1.0),
        "c": nrm(ks[1], (BATCH, D), 1.0),
        "ctx": nrm(ks[2], (BATCH, CTX_LEN, D), 1.0),
        "c_ctx": nrm(ks[3], (D,), 1.0),
        "w_mod": nrm(ks[4], (L, D, N_MOD * D), 0.5 * D ** -0.5),
        "b_mod": nrm(ks[5], (L, N_MOD * D), 0.02),
        "w_in": nrm(ks[6], (L, D, IN_W), D ** -0.5),
        "qk_gain": 1.0 + nrm(ks[7], (L, 2, HEAD_DIM), 0.1),
        "sink_logit": nrm(ks[8], (L, N_Q_HEADS), 0.5),
        "na_rpb": nrm(ks[9], (L, N_Q_HEADS, 2 * NA_KH_MAX - 1, 2 * NA_KW - 1), 0.5),
        "w_branch": nrm(ks[10], (L, N_BRANCH, BRANCH_W, D), DN_BETA * BRANCH_W ** -0.5),
        "w_out": nrm(ks[11], (L, D, D), DN_BETA * D ** -0.5),
        "ln1_g": 1.0 + nrm(ks[12], (L, D), 0.1),
        "ln1_b": nrm(ks[13], (L, D), 0.02),
        "w_router": nrm(ks[14], (L, D, E), D ** -0.5),
        "w_gate": nrm(ks[15], (L, E, D, F), D ** -0.5),
        "w_up": nrm(ks[16], (L, E, D, F), D ** -0.5),
        "w_down": nrm(ks[17], (L, E, F, D), DN_BETA * F ** -0.5),
        "ln2_g": 1.0 + nrm(ks[18], (L, D), 0.1),
        "ln2_b": nrm(ks[19], (L, D), 0.02),
    }


def reference(x, c, ctx, c_ctx, w_mod, b_mod, w_in, qk_gain, sink_logit, na_rpb, w_branch, w_out,
              ln1_g, ln1_b, w_router, w_gate, w_up, w_down, ln2_g, ln2_b):
    n_tokens = x.shape[1]
    rope = axial_rope_tables(n_tokens, x.dtype)
    na_tab = neighbourhood_tables(n_tokens)
    h_lat, h_ctx = x, ctx
    for l in range(DEPTH):
        h_lat, h_ctx = trunk_layer(
            h_lat, h_ctx, c, c_ctx, w_mod[l], b_mod[l], w_in[l], qk_gain[l], sink_logit[l], na_rpb[l],
            w_branch[l], w_out[l], ln1_g[l], ln1_b[l], w_router[l], w_gate[l], w_up[l], w_down[l],
            ln2_g[l], ln2_b[l], rope, na_tab, l < DEPTH - 1)
    return h_lat
```

```python
import numpy as np
from contextlib import ExitStack
import ml_dtypes
import concourse.bass as bass
import concourse.mybir as mybir
from concourse.bass_utils import run_bass_kernel_spmd

F32 = mybir.dt.float32
BF16 = mybir.dt.bfloat16
U32 = mybir.dt.uint32
I32 = mybir.dt.int32
ALU = mybir.AluOpType
AF = mybir.ActivationFunctionType

L = 2
D = 1024
KC = 8
NT = 2304
NCX = 256
NL = 2048
E = 16
ALPHA = (2 * L) ** 0.25
LN_EPS_S = 1e-6 / (ALPHA * ALPHA)
RMS_EPS = 1e-6
NA_COLS = 17 * 128
ENG = ("pe", "act", "dve", "pool", "sp")


class T:
    __slots__ = ("h", "w", "r", "name")

    def __init__(self, h, name=""):
        self.h = h
        self.w = {}
        self.r = {}
        self.name = name

    def __getitem__(self, idx):
        return V(self, self.h[idx])

    def full(self):
        return V(self, self.h[:])


class V:
    __slots__ = ("t", "ap")

    def __init__(self, t, ap):
        self.t = t
        self.ap = ap

    def re(self, pat, **kw):
        return V(self.t, self.ap.rearrange(pat, **kw))

    def __getitem__(self, idx):
        return V(self.t, self.ap[idx])


def _ap(x):
    return x.ap if isinstance(x, V) else x


def _ts(xs):
    return [x.t for x in xs if isinstance(x, V)]


class Sched:
    def __init__(self, nc, es, n_dma_sems=(("sp", 12), ("pool", 10), ("act", 4))):
        self.nc = nc
        self.es = es
        self.items = {e: [] for e in ENG}
        self.sems = {}
        self.cnt = {}
        self.seen = {e: {} for e in ENG}
        for e in ENG:
            self._mk("c_" + e)
        self.dma_pool = {}
        self.dma_rr = {}
        for e, n in n_dma_sems:
            self.dma_pool[e] = [self._mk("d_%s%d" % (e, i)) for i in range(n)]
            self.dma_rr[e] = 0
        self.nops = 0

    def _mk(self, key):
        self.sems[key] = self.es.enter_context(self.nc.semaphore(key))
        self.cnt[key] = 0
        return key

    def _need(self, e, key, val):
        if val <= 0 or self.seen[e].get(key, 0) >= val:
            return
        self.seen[e][key] = val
        self.items[e].append(("wait", key, val))

    def _deps(self, e, reads, writes, is_pe=False, is_dma=False):
        own = "c_" + e if not is_dma else "__none__"
        for t in reads:
            for k, v in t.w.items():
                if is_pe and k == own:
                    continue
                self._need(e, k, v)
        for t in writes:
            for k, v in t.r.items():
                if k != own:
                    self._need(e, k, v)
            for k, v in t.w.items():
                if k != own:
                    self._need(e, k, v)

    def _mark(self, key, val, reads, writes):
        for t in reads:
            t.r[key] = max(t.r.get(key, 0), val)
        for t in writes:
            if t.r:
                t.r = {}
                t.w = {}
            t.w[key] = max(t.w.get(key, 0), val)

    def op(self, e, fn, reads=(), writes=()):
        reads = _ts(reads)
        writes = _ts(writes)
        self._deps(e, reads, writes, is_pe=(e == "pe"))
        key = "c_" + e
        self.cnt[key] += 1
        self.items[e].append(("op", fn, key, 1))
        self._mark(key, self.cnt[key], reads, writes)
        self.nops += 1

    def dma(self, e, out, in_, extra_reads=(), fn=None, **kw):
        reads = _ts([in_] + list(extra_reads))
        writes = _ts([out])
        pool = self.dma_pool[e]
        key = pool[self.dma_rr[e] % len(pool)]
        self.dma_rr[e] += 1
        self._need(e, key, self.cnt[key])
        self._deps(e, reads, writes, is_dma=True)
        self.cnt[key] += 16
        if fn is None:
            o, i = _ap(out), _ap(in_)

            def fn(eng, o=o, i=i, kw=kw):
                return eng.dma_start(out=o, in_=i, **kw)
        self.items[e].append(("op", fn, key, 16))
        self._mark(key, self.cnt[key], reads, writes)
        self.nops += 1

    def barrier(self):
        for e in ENG:
            for k, v in self.cnt.items():
                if k != "c_" + e:
                    self._need(e, k, v)

    def emit(self):
        nc = self.nc
        with nc.Block() as block:
            def run(e):
                def body(eng):
                    for it in self.items[e]:
                        if it[0] == "wait":
                            eng.wait_ge(self.sems[it[1]], it[2])
                        else:
                            it[1](eng).then_inc(self.sems[it[2]], it[3])
                return body
            block.tensor(run("pe"))
            block.scalar(run("act"))
            block.vector(run("dve"))
            block.gpsimd(run("pool"))
            block.sync(run("sp"))

    def mm(self, out, lhsT, rhs, start=True, stop=True):
        o, a, b = _ap(out), _ap(lhsT), _ap(rhs)
        self.op("pe", lambda e: e.matmul(o, a, b, start=start, stop=stop), reads=[lhsT, rhs], writes=[out])

    def tr(self, out, in_, ident):
        o, a, b = _ap(out), _ap(in_), _ap(ident)
        self.op("pe", lambda e: e.transpose(o, a, b), reads=[in_, ident], writes=[out])

    def act(self, out, in_, func, scale=1.0, bias=0.0):
        o, i = _ap(out), _ap(in_)
        sc, bi = _ap(scale), _ap(bias)
        self.op("act", lambda e: e.activation(out=o, in_=i, func=func, bias=bi, scale=sc),
                reads=[in_, scale, bias], writes=[out])

    def tt(self, eng, out, a, b, op):
        o, x, y = _ap(out), _ap(a), _ap(b)
        self.op(eng, lambda e: e.tensor_tensor(o, x, y, op), reads=[a, b], writes=[out])

    def ts(self, eng, out, a, s1, op0, s2=None, op1=None):
        o, x, p1, p2 = _ap(out), _ap(a), _ap(s1), _ap(s2)
        if op1 is None:
            self.op(eng, lambda e: e.tensor_scalar(o, x, p1, None, op0), reads=[a, s1], writes=[out])
        else:
            self.op(eng, lambda e: e.tensor_scalar(o, x, p1, p2, op0, op1), reads=[a, s1, s2], writes=[out])

    def stt(self, out, a, s, b, op0, op1):
        o, x, p, y = _ap(out), _ap(a), _ap(s), _ap(b)
        self.op("dve", lambda e: e.scalar_tensor_tensor(o, x, p, y, op0, op1), reads=[a, s, b], writes=[out])

    def copy(self, eng, out, in_):
        o, i = _ap(out), _ap(in_)
        if eng == "act":
            self.op("act", lambda e: e.activation(out=o, in_=i, func=AF.Copy), reads=[in_], writes=[out])
        else:
            self.op(eng, lambda e: e.tensor_copy(o, i), reads=[in_], writes=[out])

    def recip(self, out, in_):
        o, i = _ap(out), _ap(in_)
        self.op("dve", lambda e: e.reciprocal(o, i), reads=[in_], writes=[out])

    def memset(self, eng, out, val):
        o = _ap(out)
        self.op(eng, lambda e: e.memset(o, val), writes=[out])


def _partner(d):
    return d + 16 if (d % 32) < 16 else d - 16


def _rope_tables():
    cos = np.ones((128, NT), np.float32)
    sin = np.zeros((128, NT), np.float32)
    t = np.arange(NL)
    inv = (np.float32(10000.0) ** (-np.arange(0, 32, 2, dtype=np.float32) / np.float32(32))).astype(np.float32)
    for p in range(128):
        d = p % 64
        pos = (t // 64) if d < 32 else (t % 64)
        ang = pos.astype(np.float32) * inv[d % 16]
        cos[p, NCX:] = np.cos(ang).astype(np.float32)
        sgn = -1.0 if (d % 32) < 16 else 1.0
        sin[p, NCX:] = sgn * np.sin(ang).astype(np.float32)
    return cos, sin


def _na_masks():
    rows, W, kh, kw = 32, 64, 8, 16
    t = np.arange(NL)
    r, c = t // W, t % W
    r0 = np.clip(r - kh // 2, 0, rows - kh)
    c0 = np.clip(c - kw // 2, 0, W - kw)
    k = np.arange(NL)
    kr, kcol = k // W, k % W
    valid = ((kr[None, :] >= r0[:, None]) & (kr[None, :] < r0[:, None] + kh) &
             (kcol[None, :] >= c0[:, None]) & (kcol[None, :] < c0[:, None] + kw))
    full = np.zeros((16, 7, 128, 128), np.float32)
    for b in range(16):
        for dl in range(-3, 4):
            kci = b + dl
            if 0 <= kci < 16:
                full[b, dl + 3] = valid[b * 128:(b + 1) * 128, kci * 128:(kci + 1) * 128].T
    types = [0, 1] + [2] * 12 + [3, 4]
    rep = {0: 0, 1: 1, 2: 5, 3: 14, 4: 15}
    for b in range(16):
        assert np.array_equal(full[b], full[rep[types[b]]]), b
    namc = np.zeros((3, 7, 128, 512), np.float32)
    for qt, b0 in enumerate((0, 4, 12)):
        for bi in range(4):
            namc[qt, :, :, bi * 128:(bi + 1) * 128] = full[b0 + bi]
    for b0 in (4, 8):
        for bi in range(4):
            assert np.array_equal(full[b0 + bi], full[5])
    qdl = [[dl for dl in range(-3, 4) if namc[qt, dl + 3].any()] for qt in range(3)]
    return namc, qdl, full


def _dft(n):
    t = np.arange(n, dtype=np.int64)
    m = (t[:, None] * t[None, :]) % n
    ang = 2.0 * np.pi * m.astype(np.float64) / n
    return np.cos(ang), np.sin(ang)


_NA_MASKC, _NA_QDELTAS, _NA_FULL = _na_masks()


def _consts():
    c = {}
    cos, sin = _rope_tables()
    c["rope"] = np.stack([cos, sin], 1).copy()
    ident = np.eye(128, dtype=np.float32)
    onesbd = np.zeros((128, 128), np.float32)
    onesbd[:64, :64] = 1.0
    onesbd[64:, 64:] = 1.0
    c64, s64 = _dft(64)
    cbd = np.zeros((128, 128), np.float64)
    sbd = np.zeros((128, 128), np.float64)
    for g in range(2):
        cbd[g * 64:(g + 1) * 64, g * 64:(g + 1) * 64] = c64 / 8.0
        sbd[g * 64:(g + 1) * 64, g * 64:(g + 1) * 64] = s64 / 8.0
    win = np.zeros((3, 128, 128), np.float32)
    i = np.arange(128)[:, None]
    j = np.arange(128)[None, :]
    win[0] = (j <= i)
    win[1] = 1.0
    win[2] = (i <= j)
    cb = np.concatenate([ident, onesbd, cbd.astype(np.float32), sbd.astype(np.float32)], axis=1)
    c["cbf"] = cb.astype(ml_dtypes.bfloat16)
    mC = np.zeros((3, 6, 128, 512), np.float32)
    mD = np.zeros((3, 8, 128, 512), np.float32)
    for qt, b0 in enumerate((0, 4, 12)):
        for bi in range(4):
            for r in range(6):
                kb = b0 - 1 + r
                dl = kb - (b0 + bi)
                if 0 <= kb < 16 and abs(dl) <= 1:
                    mC[qt, r, :, bi * 128:(bi + 1) * 128] = win[dl + 1]
            for r in range(8):
                kb = b0 - 2 + r
                dl = kb - (b0 + bi)
                if 0 <= kb < 16 and abs(dl) <= 3:
                    mD[qt, r, :, bi * 128:(bi + 1) * 128] = _NA_FULL[b0 + bi, dl + 3]
    for qt, b0 in enumerate((0, 4, 12)):
        for bi in range(4):
            for dl in range(-3, 4):
                if _NA_FULL[b0 + bi, dl + 3].any():
                    assert 0 <= (b0 + bi + dl) - (b0 - 2) < 8
    mk = np.concatenate([mC[qt, r] for qt in range(3) for r in range(6)] +
                        [mD[qt, r] for qt in range(3) for r in range(8)], axis=1)
    c["maskc"] = mk.astype(ml_dtypes.bfloat16)
    cf = np.concatenate([ident, np.full((128, 128), 1.0 / D, np.float32), np.ones((128, 128), np.float32)], axis=1)
    c["cf32"] = cf.astype(np.float32)
    cs, ss = _dft(NL)
    c["dftc"] = (cs / np.sqrt(NL)).astype(ml_dtypes.bfloat16)
    c["dfts"] = (-ss / np.sqrt(NL)).astype(ml_dtypes.bfloat16)
    cs, ss = _dft(NCX)
    c["dftc_c"] = (cs / np.sqrt(NCX)).astype(ml_dtypes.bfloat16)
    c["dfts_c"] = (-ss / np.sqrt(NCX)).astype(ml_dtypes.bfloat16)
    return c


def _wina_cols():
    offs = {"qA": 0, "kA": 256, "vA": 384, "f": 512, "qC": 768, "kC": 1024, "vC": 1152,
            "qD": 1280, "kD": 1536, "vD": 1664}

    def qch(base, pair, perm):
        cols = []
        for hq in pair:
            for d_ in range(64):
                dd = _partner(d_) if perm else d_
                cols.append(base + hq * 64 + dd)
        return cols
    cols = []
    for mname, roped in (("A", True), ("C", True), ("D", False)):
        qb, kb = offs["q" + mname], offs["k" + mname]
        cols += qch(qb, (0, 2), False) + qch(qb, (1, 3), False)
        if roped:
            cols += qch(qb, (0, 2), True) + qch(qb, (1, 3), True)
        cols += qch(kb, (0, 1), False)
        if roped:
            cols += qch(kb, (0, 1), True)
    cols += list(range(offs["f"], offs["f"] + 256))
    vcols = list(range(384, 512)) + list(range(1152, 1280)) + list(range(1664, 1792))
    return np.array(cols), np.array(vcols)


def build_program(debug=False, stop_after=None):
    nc = bass.Bass("TRN2", target_bir_lowering=False)

    def din(name, shape, dt=F32):
        return nc.dram_tensor(name, list(shape), dt, kind="ExternalInput").ap()

    def dscr(name, shape, dt):
        return nc.dram_tensor(name, list(shape), dt, kind=("ExternalOutput" if debug else "Internal")).ap()

    x2 = din("x2", [2, NT, D])
    cT = din("cT", [128, KC, 4])
    w_mod = din("w_mod", [L, D, 6 * D])
    b_modT = din("b_modT", [128, L, 48])
    w_inA = din("w_inA", [L, D, NA_COLS])
    w_inV = din("w_inV", [L, D, 384])
    w_inG = din("w_inG", [L, D, 4096])
    gainT = din("gainT", [128, L, 4])
    sinkB = din("sinkB", [128, L * 4])
    tbs = din("tbs", [L, 128, 4 * 15 * 64])
    rope = din("rope", [128, 2, NT])
    cbf = din("cbf", [128, 4 * 128], BF16)
    maskc = din("maskc", [128, 42 * 512], BF16)
    cf32 = din("cf32", [128, 384])
    dftc = din("dftc", [NL, NL], BF16)
    dfts = din("dfts", [NL, NL], BF16)
    dftc_c = din("dftc_c", [NCX, NCX], BF16)
    dfts_c = din("dfts_c", [NCX, NCX], BF16)
    w_brP = din("w_brP", [L, 1024, D])
    w_out = din("w_out", [L, D, D])
    lnT = din("lnT", [128, L * 4 * KC])
    w_router = din("w_router", [L, D, E])
    w_gate = din("w_gate", [L, E, D, D])
    w_up = din("w_up", [L, E, D, D])
    w_down = din("w_down", [L, E, D, D])
    out = nc.dram_tensor("out", [2, NL, D], F32, kind="ExternalOutput").ap()

    hT_d = dscr("hT_d", [2, 128, KC, NT], F32)
    uT_d = dscr("uT_d", [2, 128, KC, NT], BF16)
    qT_d = dscr("qT_d", [2, 128, 6, NT], BF16)
    kT_d = dscr("kT_d", [2, 128, 3, NT], BF16)
    v_d = dscr("v_d", [2, 128, 18, 390], BF16)
    fT_d = dscr("fT_d", [2, 128, 2, NT], BF16)
    brT_d = dscr("brT_d", [2, 128, 8, NT], BF16)
    u2tok_d = dscr("u2tok_d", [2 * NT, D], BF16)
    ffn_d = dscr("ffn_d", [2 * NT, D], F32)
    aff_d = dscr("aff_d", [2, 16, NT], F32)

    CH512 = [(0, 256)] + [(256 + i * 512, 512) for i in range(4)]
    CH256 = [(i * 256, 256) for i in range(9)]

    with ExitStack() as es:
        S = Sched(nc, es)

        uid = [0]

        def sb(shape, dt, name, stack=es):
            uid[0] += 1
            nm = "%s_%d" % (name, uid[0])
            return T(stack.enter_context(nc.sbuf_tensor(nm, list(shape), dt)), nm)

        PSF = [T(es.enter_context(nc.psum_tensor("psf%d" % i, [128, 512], F32)), "psf%d" % i) for i in range(6)]
        PSB = [T(es.enter_context(nc.psum_tensor("psb%d" % i, [128, 1024], BF16)), "psb%d" % i) for i in range(2)]

        CB = sb([128, 4 * 128], BF16, "CB")
        CF = sb([128, 384], F32, "CF")
        MOD = sb([128, L, 48, 4], F32, "MOD")
        LNT = sb([128, L * 4 * KC], F32, "LNT")
        GAIN = sb([128, L, 4], F32, "GAIN")
        SINKE = sb([128, L * 4], F32, "SINKE")
        S.dma("sp", CB.full(), cbf[:, :])
        S.dma("sp", CF.full(), cf32[:, :])
        S.dma("sp", LNT.full(), lnT[:, :])
        S.dma("sp", GAIN.full(), gainT[:, :, :])
        S.dma("sp", SINKE.full(), sinkB[:, :])
        S.act(SINKE.full(), SINKE.full(), AF.Exp)
        ident_b = CB[:, 0:128]
        onesbd = CB[:, 128:256]
        c64bd = CB[:, 256:384]
        s64bd = CB[:, 384:512]

        ident_f = CF[:, 0:128]
        ones_mean = CF[:, 128:256]
        ones_f = CF[:, 256:384]

        def lnv(l, which, kc):
            i_ = (l * 4 + which) * KC + kc
            return LNT[:, i_:i_ + 1]

        def mod(l, kind, kc, j):
            return MOD[:, l, kind * 8 + kc, j:j + 1]

        def cast_load(dst, src_ap_fn, ncols, eng="pool", step=1024):
            for c0 in range(0, ncols, step):
                c1 = min(ncols, c0 + step)
                S.dma(eng, dst[:, :, c0:c1], src_ap_fn(c0, c1))

        marks = []

        def phase_end(name="?"):
            S.barrier()
            marks.append((name, S.cnt["c_pe"]))

        with ExitStack() as ps:
            cTs = sb([128, KC, 4], F32, "cTs", ps)
            bm = sb([128, L, 48], F32, "bm", ps)
            wm = [sb([128, KC, 768], F32, "wm%d" % i, ps) for i in range(2)]
            S.dma("sp", cTs.full(), cT[:, :, :])
            S.dma("sp", bm.full(), b_modT[:, :, :])
            S.act(cTs.full(), cTs.full(), AF.Silu)
            n = 0
            for l in range(L):
                pm = PSF[l]
                for blk in range(8):
                    w = wm[n % 2]
                    n += 1
                    S.dma("sp", w.full(), w_mod[l, :, blk * 768:(blk + 1) * 768].rearrange("(kc p) n -> p kc n", p=128))
                    for cc in range(6):
                        ccg = blk * 6 + cc
                        for kc in range(KC):
                            S.mm(pm[:, ccg * 4:ccg * 4 + 4], w[:, kc, cc * 128:(cc + 1) * 128], cTs[:, kc, :],
                                 start=(kc == 0), stop=(kc == KC - 1))
                for ccg in range(48):
                    S.ts("dve", MOD[:, l, ccg, :], pm[:, ccg * 4:ccg * 4 + 4], bm[:, l, ccg:ccg + 1], ALU.add)
                for kind in (1, 4):
                    S.ts("dve", MOD[:, l, kind * 8:(kind + 1) * 8, :], MOD[:, l, kind * 8:(kind + 1) * 8, :], 1.0, ALU.add)
                for kind in (2, 5):
                    S.ts("dve", MOD[:, l, kind * 8:(kind + 1) * 8, :], MOD[:, l, kind * 8:(kind + 1) * 8, :],
                         1.0 / ALPHA, ALU.mult)
            phase_end("p0")

        with ExitStack() as ps:
            xin = [sb([128, 4, D], F32, "xin%d" % i, ps) for i in range(2)]
            hst = [sb([128, KC, 512], F32, "hst%d" % i, ps) for i in range(2)]
            n = 0
            for s in range(2):
                for (t0, Tn) in CH512:
                    nt = Tn // 128
                    xi, hs = xin[n % 2], hst[n % 2]
                    n += 1
                    S.dma("sp", xi[:, 0:nt, :], x2[s, t0:t0 + Tn, :].rearrange("(ti p) d -> p ti d", p=128))
                    for kc in range(KC):
                        pt = PSF[kc % 4]
                        for ti in range(nt):
                            S.tr(pt[:, ti * 128:(ti + 1) * 128], xi[:, ti, kc * 128:(kc + 1) * 128], ident_f)
                        S.copy("act" if kc % 2 == 0 else "dve", hs[:, kc, 0:Tn], pt[:, 0:Tn])
                    S.dma("pool", hT_d[s, :, :, t0:t0 + Tn], hs[:, :, 0:Tn])
            phase_end("pin")

        if stop_after == "pin":
            S.barrier()
            S.emit()
            return nc

        def layernorm(zT, Tn, l, which_g, tmp):
            zsq, mean_sb, var_sb, accs, accq = tmp
            pm, pq = PSF[4], PSF[5]
            for kc in range(KC):
                q = zsq[kc % 2]
                S.act(q[:, 0:Tn], zT[:, kc, 0:Tn], AF.Square)
                if kc == 1:
                    S.tt("pool", accs[:, 0:Tn], zT[:, 0, 0:Tn], zT[:, 1, 0:Tn], ALU.add)
                    S.tt("dve", accq[:, 0:Tn], zsq[0][:, 0:Tn], zsq[1][:, 0:Tn], ALU.add)
                elif kc > 1:
                    S.tt("pool", accs[:, 0:Tn], accs[:, 0:Tn], zT[:, kc, 0:Tn], ALU.add)
                    S.tt("dve", accq[:, 0:Tn], accq[:, 0:Tn], q[:, 0:Tn], ALU.add)
            S.mm(pm[:, 0:Tn], ones_mean, accs[:, 0:Tn])
            S.mm(pq[:, 0:Tn], ones_mean, accq[:, 0:Tn])
            S.copy("act", mean_sb[:, 0:Tn], pm[:, 0:Tn])
            S.tt("dve", var_sb[:, 0:Tn], mean_sb[:, 0:Tn], mean_sb[:, 0:Tn], ALU.mult)
            S.tt("dve", var_sb[:, 0:Tn], pq[:, 0:Tn], var_sb[:, 0:Tn], ALU.subtract)
            S.act(var_sb[:, 0:Tn], var_sb[:, 0:Tn], AF.Sqrt, bias=EPSV[:, 0:1])
            S.recip(var_sb[:, 0:Tn], var_sb[:, 0:Tn])
            for kc in range(KC):
                S.tt("pool", zT[:, kc, 0:Tn], zT[:, kc, 0:Tn], mean_sb[:, 0:Tn], ALU.subtract)
                S.tt("dve", zT[:, kc, 0:Tn], zT[:, kc, 0:Tn], var_sb[:, 0:Tn], ALU.mult)
                S.act(zT[:, kc, 0:Tn], zT[:, kc, 0:Tn], AF.Identity, scale=lnv(l, which_g, kc), bias=lnv(l, which_g + 1, kc))

        EPSV = sb([128, 2], F32, "EPSV")
        S.memset("dve", EPSV[:, 0:1], LN_EPS_S)
        S.memset("dve", EPSV[:, 1:2], RMS_EPS)

        def p1(l, samples):
            with ExitStack() as ps:
                wA = sb([128, KC, NA_COLS], BF16, "wA", ps)
                wV = sb([128, KC, 384], BF16, "wV", ps)
                ROPE = sb([128, 2, NT], F32, "ROPE", ps)
                hTc = [sb([128, KC, 512], F32, "hTc%d" % i, ps) for i in range(2)]
                uTc = [sb([128, KC, 512], BF16, "uTc%d" % i, ps) for i in range(2)]
                stg = [sb([128, 512], BF16, "stg%d" % i, ps) for i in range(4)]
                sq = sb([128, 512], BF16, "sq", ps)
                rs = sb([128, 512], F32, "rs", ps)
                t1 = sb([128, 512], F32, "t1", ps)
                t2 = sb([128, 512], F32, "t2", ps)
                vst = [sb([128, 4, 6, 65], BF16, "vst%d" % i, ps) for i in range(2)]
                cast_load(wA, lambda c0, c1: w_inA[l, :, c0:c1].rearrange("(kc p) n -> p kc n", p=128), NA_COLS)
                cast_load(wV, lambda c0, c1: w_inV[l, :, c0:c1].rearrange("(kc p) n -> p kc n", p=128), 384)
                S.dma("sp", ROPE.full(), rope[:, :, :])
                for i in range(2):
                    S.memset("pool", vst[i].full(), 1.0)
                nst = 0
                for s in samples:
                    for ci, (t0, Tn) in enumerate(CH512):
                        j = 2 if t0 == 0 else s
                        h, u = hTc[ci % 2], uTc[ci % 2]
                        S.dma("sp", h[:, :, 0:Tn], hT_d[s, :, :, t0:t0 + Tn])
                        for kc in range(KC):
                            S.act(u[:, kc, 0:Tn], h[:, kc, 0:Tn], AF.Identity, scale=mod(l, 1, kc, j), bias=mod(l, 0, kc, j))
                        S.dma("pool", uT_d[s, :, :, t0:t0 + Tn], u[:, :, 0:Tn])

                        def proj(ps_t, cc):
                            for kc in range(KC):
                                S.mm(ps_t[:, 0:Tn], wA[:, kc, cc * 128:(cc + 1) * 128], u[:, kc, 0:Tn],
                                     start=(kc == 0), stop=(kc == KC - 1))

                        def store(dst_ap, v):
                            S.dma("pool", dst_ap, v)

                        plain = [(12, qT_d[s, :, 4, t0:t0 + Tn]), (13, qT_d[s, :, 5, t0:t0 + Tn]),
                                 (14, kT_d[s, :, 2, t0:t0 + Tn]), (15, fT_d[s, :, 0, t0:t0 + Tn]),
                                 (16, fT_d[s, :, 1, t0:t0 + Tn])]
                        for n_, (cc, dst) in enumerate(plain):
                            pt = PSF[n_ % 2]
                            proj(pt, cc)
                            st = stg[nst % 4]
                            nst += 1
                            S.copy("act", st[:, 0:Tn], pt[:, 0:Tn])
                            store(dst, st[:, 0:Tn])
                        roped = [(0, 2, 0, 1, True, qT_d[s, :, 0, t0:t0 + Tn]), (1, 3, 0, 1, True, qT_d[s, :, 1, t0:t0 + Tn]),
                                 (4, 5, 2, 3, True, kT_d[s, :, 0, t0:t0 + Tn]),
                                 (6, 8, None, None, False, qT_d[s, :, 2, t0:t0 + Tn]),
                                 (7, 9, None, None, False, qT_d[s, :, 3, t0:t0 + Tn]),
                                 (10, 11, None, None, False, kT_d[s, :, 1, t0:t0 + Tn])]
                        for n_, (cb_, cp_, g0, g1, norm, dst) in enumerate(roped):
                            pa, pb = PSF[2 * (n_ % 2)], PSF[2 * (n_ % 2) + 1]
                            proj(pa, cb_)
                            proj(pb, cp_)
                            cosv = ROPE[:, 0, t0:t0 + Tn]
                            sinv = ROPE[:, 1, t0:t0 + Tn]
                            st = stg[nst % 4]
                            nst += 1
                            if norm:
                                S.act(sq[:, 0:Tn], pa[:, 0:Tn], AF.Square)
                                S.mm(PSF[4][:, 0:Tn], onesbd, sq[:, 0:Tn])
                                S.act(rs[:, 0:Tn], PSF[4][:, 0:Tn], AF.Sqrt, scale=1.0 / 64.0, bias=EPSV[:, 1:2])
                                S.recip(rs[:, 0:Tn], rs[:, 0:Tn])
                                S.stt(t1[:, 0:Tn], pa[:, 0:Tn], GAIN[:, l, g0:g0 + 1], cosv, ALU.mult, ALU.mult)
                                S.stt(t2[:, 0:Tn], pb[:, 0:Tn], GAIN[:, l, g1:g1 + 1], sinv, ALU.mult, ALU.mult)
                                S.tt("pool", t1[:, 0:Tn], t1[:, 0:Tn], t2[:, 0:Tn], ALU.add)
                                S.tt("dve", st[:, 0:Tn], t1[:, 0:Tn], rs[:, 0:Tn], ALU.mult)
                            else:
                                S.tt("dve", t1[:, 0:Tn], pa[:, 0:Tn], cosv, ALU.mult)
                                S.tt("dve", t2[:, 0:Tn], pb[:, 0:Tn], sinv, ALU.mult)
                                S.tt("pool", st[:, 0:Tn], t1[:, 0:Tn], t2[:, 0:Tn], ALU.add)
                            store(dst, st[:, 0:Tn])
                        vs = vst[ci % 2]
                        nt = Tn // 128
                        for ti in range(nt):
                            pv = PSF[5]
                            for kc in range(KC):
                                S.mm(pv[:, 0:384], u[:, kc, ti * 128:(ti + 1) * 128], wV[:, kc, :],
                                     start=(kc == 0), stop=(kc == KC - 1))
                            S.copy("act", vs[:, ti, :, 0:64], pv[:, 0:384].re("p (m d) -> p m d", d=64))
                        S.dma("pool", v_d[s, :, t0 // 128:t0 // 128 + nt, :].rearrange("p t (m d) -> p t m d", d=65),
                              vs[:, 0:nt, :, :])
            phase_end("p1")

        def p2(s, l):
            with ExitStack() as ps:
                kT = sb([128, 3, NT], BF16, "kT", ps)
                Vt = sb([128, 18, 390], BF16, "Vt", ps)
                TB = sb([128, 4 * 15 * 64], F32, "TB", ps)
                MK = sb([128, 42 * 512], BF16, "MK", ps)
                qcb = [sb([128, 6, 512], BF16, "qc%d" % i, ps) for i in range(2)]
                Pb = [sb([128, 512], BF16, "Pb%d" % i, ps) for i in range(3)]
                Pe = [sb([128, 512], BF16, "Pe%d" % i, ps) for i in range(3)]
                Pm = [sb([128, 512], BF16, "Pm%d" % i, ps) for i in range(2)]
                sbias = [sb([128, 512], F32, "sbias%d" % i, ps) for i in range(2)]
                rd = sb([128, 512], F32, "rd", ps)
                osb = [sb([64, 512], F32, "osb%d" % i, ps) for i in range(2)]
                brst = [sb([128, 512], BF16, "brst%d" % i, ps) for i in range(2)]
                S.dma("sp", kT.full(), kT_d[s, :, :, :])
                S.dma("sp", Vt.full(), v_d[s, :, :, :])
                S.dma("sp", TB.full(), tbs[l, :, :])
                S.dma("sp", MK.full(), maskc[:, :])
                TBv = TB.full().re("p (h m c) -> p h m c", h=4, m=15)

                def maskC(qt, r):
                    i_ = qt * 6 + r
                    return MK[:, i_ * 512:(i_ + 1) * 512]

                def maskD(qt, r):
                    i_ = 18 + qt * 8 + r
                    return MK[:, i_ * 512:(i_ + 1) * 512]
                qch = ([(0, 256, True)] if l == 0 else []) + [(256 + i * 512, 512, False) for i in range(4)]
                cnt = {}

                def nxt(k, lst):
                    c_ = cnt.get(k, 0)
                    cnt[k] = c_ + 1
                    return lst[c_ % len(lst)]
                for qi, (t0, Tn, is_ctx) in enumerate(qch):
                    qc = qcb[qi % 2]
                    S.dma("sp", qc[:, :, 0:Tn], qT_d[s, :, :, t0:t0 + Tn])
                    b0 = (t0 - NCX) // 128
                    qt = 0 if b0 == 0 else (2 if b0 == 12 else 1)
                    for m in range(3):
                        for qcnk in range(2):
                            bst = nxt("b", brst)
                            for ph in range(2):
                                hq = qcnk + 2 * ph
                                O = nxt("o", [PSF[3], PSF[4]])
                                p0, p1_ = ph * 64, (ph + 1) * 64
                                qv = qc[p0:p1_, m * 2 + qcnk, 0:Tn]
                                vo = (m * 2 + ph) * 65
                                if is_ctx:
                                    steps = [("g", 0), ("g", 1)]
                                elif m == 0:
                                    steps = [("g", kc) for kc in range(18)]
                                else:
                                    if m == 1:
                                        rs_ = [r for r in range(6) if 0 <= b0 - 1 + r < 16]
                                    else:
                                        rs_ = [r for r in range(8) if 0 <= b0 - 2 + r < 16]
                                    steps = [("g", 0), ("g", 1)] + [("b", r) for r in rs_]

                                def emit_S(st):
                                    Sp = nxt("s", [PSF[0], PSF[1], PSF[2]])
                                    if st[0] == "g":
                                        kc = st[1]
                                    else:
                                        kc = 2 + b0 + st[1] - (1 if m == 1 else 2)
                                    S.mm(Sp[:, 0:Tn], kT[p0:p1_, m, kc * 128:(kc + 1) * 128], qv)
                                    return Sp

                                def emit_rest(st, Sp, first, last):
                                    if st[0] == "g":
                                        kc = st[1]
                                        P = nxt("p", Pb)
                                        S.act(P[:, 0:Tn], Sp[:, 0:Tn], AF.Exp, scale=0.125)
                                        S.mm(O[0:65, 0:Tn], Vt[:, kc, vo:vo + 65], P[:, 0:Tn], start=first, stop=last)
                                        return
                                    r = st[1]
                                    kb = b0 + r - (1 if m == 1 else 2)
                                    kc = 2 + kb
                                    pe_ = nxt("e", Pe)
                                    pfin = nxt("m", Pm)
                                    if m == 2:
                                        sbv = nxt("e2", sbias)
                                        for bi in range(4):
                                            dl = min(3, max(-3, kb - (b0 + bi)))
                                            m0 = 7 - 2 * dl
                                            S.stt(sbv[:, bi * 128:(bi + 1) * 128].re("p (j c) -> p j c", c=64),
                                                  Sp[:, bi * 128:(bi + 1) * 128].re("p (j c) -> p j c", c=64), 0.125,
                                                  TBv[:, hq, m0:m0 + 2, :], ALU.mult, ALU.add)
                                        S.act(pe_.full(), sbv.full(), AF.Exp)
                                        S.tt("pool", pfin.full(), pe_.full(), maskD(qt, r), ALU.mult)
                                    else:
                                        S.act(pe_.full(), Sp.full(), AF.Exp, scale=0.125)
                                        S.tt("pool", pfin.full(), pe_.full(), maskC(qt, r), ALU.mult)
                                    S.mm(O[0:65, 0:Tn], Vt[:, kc, vo:vo + 65], pfin.full(), start=False, stop=last)
                                n_ = len(steps)
                                sps = [None] * n_
                                sps[0] = emit_S(steps[0])
                                for i_ in range(n_):
                                    if i_ + 1 < n_:
                                        sps[i_ + 1] = emit_S(steps[i_ + 1])
                                    emit_rest(steps[i_], sps[i_], i_ == 0, i_ == n_ - 1)
                                if m == 1:
                                    S.ts("dve", rd[64:65, 0:Tn], O[64:65, 0:Tn], SINKE[64:65, l * 4 + hq:l * 4 + hq + 1], ALU.add)
                                    S.recip(rd[64:65, 0:Tn], rd[64:65, 0:Tn])
                                else:
                                    S.recip(rd[64:65, 0:Tn], O[64:65, 0:Tn])
                                BC = PSF[5]
                                S.mm(BC[0:64, 0:Tn], ones_f[64:65, 0:64], rd[64:65, 0:Tn])
                                ob = osb[ph]
                                S.copy("act", ob[:, 0:Tn], O[0:64, 0:Tn])
                                S.tt("dve", bst[p0:p1_, 0:Tn], ob[:, 0:Tn], BC[0:64, 0:Tn], ALU.mult)
                            cidx = (0, 4, 6)[m] + qcnk
                            S.dma("pool", brT_d[s, :, cidx, t0:t0 + Tn], bst[:, 0:Tn])
            phase_end("p2")

        def p2b(s, l):
            with ExitStack() as ps:
                fT = sb([128, 2, NT], BF16, "fT", ps)
                gcs = sb([128, 18, 4, 128], BF16, "gcs", ps)
                Cc = [sb([128, 16, 512], BF16, "Cc%d" % i, ps) for i in range(2)]
                Sc = [sb([128, 16, 512], BF16, "Sc%d" % i, ps) for i in range(2)]
                ost = [sb([128, 512], BF16, "ost%d" % i, ps) for i in range(2)]
                S.dma("sp", fT.full(), fT_d[s, :, :, :])
                tcs = range(18) if l == 0 else range(2, 18)
                for tc in tcs:
                    pg = PSF[tc % 2]
                    for fc in range(2):
                        S.mm(pg[:, (fc * 2) * 128:(fc * 2 + 1) * 128], fT[:, fc, tc * 128:(tc + 1) * 128], c64bd)
                        S.mm(pg[:, (fc * 2 + 1) * 128:(fc * 2 + 2) * 128], fT[:, fc, tc * 128:(tc + 1) * 128], s64bd)
                    S.copy("act" if tc % 2 == 0 else "dve", gcs[:, tc, :, :], pg[:, 0:512].re("p (a c) -> p a c", c=128))
                n = 0
                jobs = [(256 + i * 512, 512, 2, 16, dftc, dfts) for i in range(4)]
                if l == 0:
                    jobs = [(0, 256, 0, 2, dftc_c, dfts_c)] + jobs
                for ji, (t0, Tn, tc0, ntc, mc, msn) in enumerate(jobs):
                    c_, s_ = Cc[ji % 2], Sc[ji % 2]
                    col0 = 0 if t0 == 0 else t0 - NCX
                    S.dma("sp", c_[:, 0:ntc, 0:Tn], mc[:, col0:col0 + Tn].rearrange("(tc p) n -> p tc n", p=128))
                    S.dma("sp", s_[:, 0:ntc, 0:Tn], msn[:, col0:col0 + Tn].rearrange("(tc p) n -> p tc n", p=128))
                    for fc in range(2):
                        po = PSF[2 + (n % 2)]
                        for k_ in range(ntc):
                            S.mm(po[:, 0:Tn], gcs[:, tc0 + k_, fc * 2, :], c_[:, k_, 0:Tn], start=(k_ == 0), stop=False)
                            S.mm(po[:, 0:Tn], gcs[:, tc0 + k_, fc * 2 + 1, :], s_[:, k_, 0:Tn], start=False, stop=(k_ == ntc - 1))
                        o_ = ost[n % 2]
                        n += 1
                        S.copy("act", o_[:, 0:Tn], po[:, 0:Tn])
                        S.dma("pool", brT_d[s, :, 2 + fc, t0:t0 + Tn], o_[:, 0:Tn])
            phase_end("p2b")

        def p3_weights(l, ws):
            wG = sb([128, KC, 4096], BF16, "wG", ws)
            wB = sb([128, KC, D], BF16, "wB", ws)
            wO = sb([128, KC, D], BF16, "wO", ws)
            wR = sb([128, KC, E], F32, "wR", ws)
            cast_load(wG, lambda c0, c1: w_inG[l, :, c0:c1].rearrange("(kc p) n -> p kc n", p=128), 4096)
            cast_load(wB, lambda c0, c1: w_brP[l, :, c0:c1].rearrange("(kc p) n -> p kc n", p=128), D)
            cast_load(wO, lambda c0, c1: w_out[l, :, c0:c1].rearrange("(kc p) n -> p kc n", p=128), D)
            S.dma("sp", wR.full(), w_router[l, :, :].rearrange("(kc p) n -> p kc n", p=128))
            return wG, wB, wO, wR

        def p3(s, l, wts):
            wG, wB, wO, wR = wts
            with ExitStack() as ps:
                TT = 512
                uTc = sb([128, KC, TT], BF16, "uTc3", ps)
                brc = sb([128, 8, TT], BF16, "brc", ps)
                hTc = sb([128, KC, TT], F32, "hTc3", ps)
                zT = sb([128, KC, TT], F32, "zT", ps)
                mrg = sb([128, KC, TT], BF16, "mrg", ps)
                u2f = hTc
                u2b = brc
                gate = [sb([128, TT], F32, "gate%d" % i, ps) for i in range(2)]
                tmpm = [sb([128, TT], F32, "tmpm%d" % i, ps) for i in range(2)]
                acc = sb([128, TT], F32, "acc", ps)
                zsq = [sb([128, TT], F32, "zsq%d" % i, ps) for i in range(2)]
                mean_sb = sb([128, TT], F32, "mean_sb", ps)
                var_sb = sb([128, TT], F32, "var_sb", ps)
                accs = sb([128, TT], F32, "accs", ps)
                accq = sb([128, TT], F32, "accq", ps)
                ex = sb([16, TT], F32, "ex", ps)
                rsum = sb([16, TT], F32, "rsum", ps)
                utok = [sb([128, D], BF16, "utok%d" % i, ps) for i in range(2)]
                chunks = CH512 if l == 0 else CH512[1:]
                ng = 0
                for (t0, Tn) in chunks:
                    j = 2 if t0 == 0 else s
                    S.dma("sp", uTc[:, :, 0:Tn], uT_d[s, :, :, t0:t0 + Tn])
                    S.dma("sp", brc[:, :, 0:Tn], brT_d[s, :, :, t0:t0 + Tn])
                    S.dma("sp", hTc[:, :, 0:Tn], hT_d[s, :, :, t0:t0 + Tn])
                    for dc in range(KC):
                        for i in range(4):
                            pg = PSF[ng % 2]
                            pb = PSF[2 + ng % 2]
                            g_ = gate[ng % 2]
                            tm = tmpm[ng % 2]
                            ng += 1
                            for kc in range(KC):
                                S.mm(pg[:, 0:Tn], wG[:, kc, i * D + dc * 128:i * D + (dc + 1) * 128], uTc[:, kc, 0:Tn],
                                     start=(kc == 0), stop=(kc == KC - 1))
                            S.act(g_[:, 0:Tn], pg[:, 0:Tn], AF.Sigmoid)
                            for hf in range(2):
                                S.mm(pb[:, 0:Tn], wB[:, i * 2 + hf, dc * 128:(dc + 1) * 128], brc[:, i * 2 + hf, 0:Tn],
                                     start=(hf == 0), stop=(hf == 1))
                            if i == 0:
                                S.tt("dve", acc[:, 0:Tn], g_[:, 0:Tn], pb[:, 0:Tn], ALU.mult)
                            else:
                                S.tt("dve", tm[:, 0:Tn], g_[:, 0:Tn], pb[:, 0:Tn], ALU.mult)
                                if i < 3:
                                    S.tt("pool", acc[:, 0:Tn], acc[:, 0:Tn], tm[:, 0:Tn], ALU.add)
                                else:
                                    S.tt("pool", mrg[:, dc, 0:Tn], acc[:, 0:Tn], tm[:, 0:Tn], ALU.add)
                    for dc in range(KC):
                        py = PSF[dc % 2]
                        for kc in range(KC):
                            S.mm(py[:, 0:Tn], wO[:, kc, dc * 128:(dc + 1) * 128], mrg[:, kc, 0:Tn],
                                 start=(kc == 0), stop=(kc == KC - 1))
                        S.stt(zT[:, dc, 0:Tn], py[:, 0:Tn], mod(l, 2, dc, j), hTc[:, dc, 0:Tn], ALU.mult, ALU.add)
                    layernorm(zT, Tn, l, 0, (zsq, mean_sb, var_sb, accs, accq))
                    S.dma("pool", hT_d[s, :, :, t0:t0 + Tn], zT[:, :, 0:Tn])
                    for kc in range(KC):
                        S.act(u2f[:, kc, 0:Tn], zT[:, kc, 0:Tn], AF.Identity, scale=mod(l, 4, kc, j), bias=mod(l, 3, kc, j))
                        S.copy("pool", u2b[:, kc, 0:Tn], u2f[:, kc, 0:Tn])
                    pl = PSF[2]
                    for kc in range(KC):
                        S.mm(pl[0:16, 0:Tn], wR[:, kc, :], u2f[:, kc, 0:Tn], start=(kc == 0), stop=(kc == KC - 1))
                    S.act(ex[:, 0:Tn], pl[0:16, 0:Tn], AF.Exp)
                    S.mm(PSF[3][0:16, 0:Tn], ones_f[0:16, 0:16], ex[:, 0:Tn])
                    S.recip(rsum[:, 0:Tn], PSF[3][0:16, 0:Tn])
                    S.tt("dve", ex[:, 0:Tn], ex[:, 0:Tn], rsum[:, 0:Tn], ALU.mult)
                    S.dma("pool", aff_d[s, :, t0:t0 + Tn], ex[:, 0:Tn])
                    for ti in range(Tn // 128):
                        pt = PSB[ti % 2]
                        ut = utok[ti % 2]
                        for kc in range(KC):
                            S.tr(pt[:, kc * 128:(kc + 1) * 128], u2b[:, kc, ti * 128:(ti + 1) * 128], ident_b)
                        S.copy("act", ut.full(), pt.full())
                        r0 = s * NT + t0 + ti * 128
                        S.dma("pool", u2tok_d[r0:r0 + 128, :], ut.full())
            phase_end("p3")

        IDXT = sb([128, 5, E], I32, "IDXT")
        SELW = sb([128, 5, E], F32, "SELW")

        def p4(l):
            with ExitStack() as ps:
                work = sb([48, NL], F32, "work", ps)
                workc = sb([48, NCX], F32, "workc", ps)
                vals = sb([48, 288], F32, "vals", ps)
                idxu = sb([48, 288], U32, "idxu", ps)
                idxf = sb([48, 288], F32, "idxf", ps)
                lo = sb([16, 2, 288], F32, "lo", ps)
                tmpi = sb([128, E], F32, "tmpi", ps)
                S.memset("dve", work.full(), 0.0)
                S.memset("dve", workc.full(), 0.0)
                S.memset("dve", vals.full(), 0.0)
                S.memset("dve", idxu.full(), 0)
                for s in range(2):
                    S.dma("sp", work[32 * s:32 * s + 16, :], aff_d[s, :, NCX:NT])
                    if l == 0:
                        S.dma("sp", workc[32 * s:32 * s + 16, :], aff_d[s, :, 0:NCX])
                jobs = [(work, 0, 32)] + ([(workc, 256, 4)] if l == 0 else [])
                for (wt, slot0, rounds) in jobs:
                    for r_ in range(rounds):
                        sl = slice(slot0 + r_ * 8, slot0 + r_ * 8 + 8)
                        mx, ix, wk = _ap(vals[:, sl]), _ap(idxu[:, sl]), _ap(wt.full())
                        S.op("dve", lambda e, mx=mx, wk=wk: e.max(out=mx, in_=wk), reads=[wt.full()], writes=[vals.full()])
                        S.op("dve", lambda e, mx=mx, ix=ix, wk=wk: e.max_index(out=ix, in_max=mx, in_values=wk),
                             reads=[wt.full(), vals.full()], writes=[idxu.full()])
                        S.op("dve", lambda e, mx=mx, wk=wk: e.match_replace(out=wk, in_to_replace=mx, in_values=wk, imm_value=-1.0),
                             reads=[wt.full(), vals.full()], writes=[wt.full()])
                S.copy("dve", idxf.full(), idxu.full())
                S.ts("dve", idxf[0:16, 0:256], idxf[0:16, 0:256], float(NCX), ALU.add)
                S.ts("dve", idxf[32:48, 0:256], idxf[32:48, 0:256], float(NT + NCX), ALU.add)
                S.ts("dve", idxf[32:48, 256:288], idxf[32:48, 256:288], float(NT), ALU.add)
                S.copy("dve", lo[:, 0, :], idxf[32:48, :])
                S.copy("dve", lo[:, 1, :], vals[32:48, :])
                srcs = [(idxf[0:16, :], vals[0:16, :]), (lo[:, 0, :], lo[:, 1, :])]
                n = 0
                for s in range(2):
                    si, sv = srcs[s]
                    for half in range(2):
                        st = s * 2 + half
                        pt = PSF[n % 2]
                        n += 1
                        S.tr(pt[:, 0:16], si[:, half * 128:(half + 1) * 128], ident_f[0:16, 0:16])
                        S.copy("dve", tmpi.full(), pt[:, 0:16])
                        S.copy("dve", IDXT[:, st, :], tmpi.full())
                        S.tr(pt[:, 16:32], sv[:, half * 128:(half + 1) * 128], ident_f[0:16, 0:16])
                        S.copy("dve", SELW[:, st, :], pt[:, 16:32])
                    if l == 0:
                        pt = PSF[n % 2]
                        n += 1
                        S.tr(pt[0:32, 0:16], si[:, 256:288], ident_f[0:16, 0:16])
                        S.copy("dve", tmpi[0:32, :], pt[0:32, 0:16])
                        S.copy("dve", IDXT[32 * s:32 * s + 32, 4, :], tmpi[0:32, :])
                        S.tr(pt[0:32, 16:32], sv[:, 256:288], ident_f[0:16, 0:16])
                        S.copy("dve", SELW[32 * s:32 * s + 32, 4, :], pt[0:32, 16:32])
            phase_end("p4")

        def p5_weights(l, ws):
            wg = [sb([128, KC, D], BF16, "wg%d" % i, ws) for i in range(2)]
            wu = [sb([128, KC, D], BF16, "wu%d" % i, ws) for i in range(2)]
            wd = [sb([128, KC, D], BF16, "wd%d" % i, ws) for i in range(2)]
            for wt, src in ((wg[0], w_gate), (wu[0], w_up), (wd[0], w_down)):
                S.dma("pool", wt.full(), src[l, 0, :, :].rearrange("(kc p) n -> p kc n", p=128))
            return wg, wu, wd

        def p5(l, wts):
            wg, wu, wd = wts
            with ExitStack() as ps:
                xs = [sb([128, 5, D], BF16, "xs%d" % i, ps) for i in range(2)]
                xsT = sb([128, KC, 640], BF16, "xsT", ps)
                actT = sb([128, KC, 640], BF16, "actT", ps)
                sa = [sb([128, 512], F32, "sa%d" % i, ps) for i in range(2)]
                ysb = [sb([128, D], F32, "ysb%d" % i, ps) for i in range(2)]
                zer = sb([128, 2, D], F32, "zer", ps)
                FFN = T(ffn_d, "ffn_d")
                U2 = T(u2tok_d, "u2tok_d")
                S.memset("dve", zer.full(), 0.0)
                for i in range(2 * NT // 256):
                    S.dma("sp", V(FFN, ffn_d[i * 256:(i + 1) * 256, :].rearrange("(a p) d -> p a d", p=128)), zer.full())
                sts = [(0, 128), (1, 128), (2, 128), (3, 128)] + ([(4, 64)] if l == 0 else [])
                nsl = 576 if l == 0 else 512
                cgs = [(0, 512)] + ([(512, 576)] if l == 0 else [])

                def loadw(e_):
                    b_ = e_ % 2
                    for wt, src in ((wg[b_], w_gate), (wu[b_], w_up), (wd[b_], w_down)):
                        S.dma("pool", wt.full(), src[l, e_, :, :].rearrange("(kc p) n -> p kc n", p=128))

                def gathers(e_):
                    x_ = xs[e_ % 2]
                    for (st, np_) in sts:
                        o_ = _ap(x_[0:np_, st, :])
                        ix = _ap(IDXT[0:np_, st, e_:e_ + 1])

                        def fn(eng, o_=o_, ix=ix):
                            return eng.indirect_dma_start(out=o_, out_offset=None, in_=u2tok_d[:, :],
                                                          in_offset=bass.IndirectOffsetOnAxis(ap=ix, axis=0))
                        S.dma("pool", x_[0:np_, st, :], V(U2, None), extra_reads=[IDXT.full()], fn=fn)
                gathers(0)
                ny = 0
                for e_ in range(E):
                    if e_ + 1 < E:
                        loadw(e_ + 1)
                        gathers(e_ + 1)
                    b_ = e_ % 2
                    x_ = xs[b_]
                    for (st, np_) in sts:
                        pt = PSB[st % 2]
                        for kc in range(KC):
                            S.tr(pt[:, kc * 128:kc * 128 + np_], x_[0:np_, st, kc * 128:(kc + 1) * 128], ident_b[0:np_, 0:np_])
                        S.copy("act" if st % 2 == 0 else "dve", xsT[:, :, st * 128:st * 128 + np_],
                               pt.full().re("p (k t) -> p k t", t=128)[:, :, 0:np_])
                    for fc in range(KC):
                        for (c0, c1) in cgs:
                            pa, pu = PSF[(fc % 2) * 2], PSF[(fc % 2) * 2 + 1]
                            w_ = c1 - c0
                            for kc in range(KC):
                                S.mm(pa[:, 0:w_], wg[b_][:, kc, fc * 128:(fc + 1) * 128], xsT[:, kc, c0:c1],
                                     start=(kc == 0), stop=(kc == KC - 1))
                            for kc in range(KC):
                                S.mm(pu[:, 0:w_], wu[b_][:, kc, fc * 128:(fc + 1) * 128], xsT[:, kc, c0:c1],
                                     start=(kc == 0), stop=(kc == KC - 1))
                            s_ = sa[fc % 2]
                            S.act(s_[:, 0:w_], pa[:, 0:w_], AF.Silu)
                            S.tt("dve", actT[:, fc, c0:c1], s_[:, 0:w_], pu[:, 0:w_], ALU.mult)
                    for (st, np_) in sts:
                        y_ = ysb[ny % 2]
                        ny += 1
                        for dh in range(2):
                            py = PSF[4 + dh]
                            for fc in range(KC):
                                S.mm(py[0:np_, :], actT[:, fc, st * 128:st * 128 + np_], wd[b_][:, fc, dh * 512:(dh + 1) * 512],
                                     start=(fc == 0), stop=(fc == KC - 1))
                            S.act(y_[0:np_, dh * 512:(dh + 1) * 512], py[0:np_, :], AF.Copy, scale=SELW[0:np_, st, e_:e_ + 1])
                        yi = _ap(y_[0:np_, :])
                        ix = _ap(IDXT[0:np_, st, e_:e_ + 1])

                        def fn(eng, yi=yi, ix=ix):
                            return eng.indirect_dma_start(out=ffn_d[:, :], out_offset=bass.IndirectOffsetOnAxis(ap=ix, axis=0),
                                                          in_=yi, in_offset=None, compute_op=ALU.add)
                        S.dma("pool", V(FFN, None), y_[0:np_, :], extra_reads=[IDXT.full(), V(FFN, None)], fn=fn)
            phase_end("p5")

        def p6(s, l):
            with ExitStack() as ps:
                ftok = sb([128, 4, D], F32, "ftok", ps)
                hTc = sb([128, KC, 512], F32, "hTc6", ps)
                zT = sb([128, KC, 512], F32, "zT6", ps)
                zsq = [sb([128, 512], F32, "zsq6%d" % i, ps) for i in range(2)]
                mean_sb = sb([128, 512], F32, "mean6", ps)
                var_sb = sb([128, 512], F32, "var6", ps)
                accs = sb([128, 512], F32, "accs6", ps)
                accq = sb([128, 512], F32, "accq6", ps)
                otok = [sb([128, D], F32, "otok%d" % i, ps) for i in range(2)]
                chunks = CH512 if l == 0 else CH512[1:]
                for (t0, Tn) in chunks:
                    j = 2 if t0 == 0 else s
                    nt = Tn // 128
                    r0 = s * NT + t0
                    S.dma("sp", ftok[:, 0:nt, :], ffn_d[r0:r0 + Tn, :].rearrange("(ti p) d -> p ti d", p=128))
                    S.dma("sp", hTc[:, :, 0:Tn], hT_d[s, :, :, t0:t0 + Tn])
                    for kc in range(KC):
                        pk = PSF[kc % 4]
                        for ti in range(nt):
                            S.tr(pk[:, ti * 128:(ti + 1) * 128], ftok[:, ti, kc * 128:(kc + 1) * 128], ident_f)
                        S.stt(zT[:, kc, 0:Tn], pk[:, 0:Tn], mod(l, 5, kc, j), hTc[:, kc, 0:Tn], ALU.mult, ALU.add)
                    layernorm(zT, Tn, l, 2, (zsq, mean_sb, var_sb, accs, accq))
                    if l < L - 1:
                        S.dma("pool", hT_d[s, :, :, t0:t0 + Tn], zT[:, :, 0:Tn])
                    else:
                        for ti in range(nt):
                            ot = otok[ti % 2]
                            for hf in range(2):
                                po = PSF[hf]
                                for k4 in range(4):
                                    kc = hf * 4 + k4
                                    S.tr(po[:, k4 * 128:(k4 + 1) * 128], zT[:, kc, ti * 128:(ti + 1) * 128], ident_f)
                                S.copy("act" if hf == 0 else "dve", ot[:, hf * 512:(hf + 1) * 512], po.full())
                            S.dma("pool", out[s, t0 - NCX + ti * 128:t0 - NCX + (ti + 1) * 128, :], ot.full())
            phase_end("p6")

        done = False
        for l in range(L):
            p1(l, (0,) if stop_after == "p1" else (0, 1))
            if stop_after == "p1":
                break
            for s in range(2):
                p2(s, l)
                with ExitStack() as ws:
                    wts = p3_weights(l, ws) if stop_after != "p2" else None
                    p2b(s, l)
                    if stop_after == "p2":
                        done = True
                        break
                    p3(s, l, wts)
                if stop_after == "p3":
                    done = True
                    break
            if done:
                break
            with ExitStack() as ws:
                wts5 = p5_weights(l, ws)
                p4(l)
                p5(l, wts5)
            if stop_after == "p5":
                break
            for s in range(2):
                p6(s, l)
            if stop_after == "l0":
                break
        S.barrier()
        S.emit()
        print("bass ops:", S.nops, {e: len(S.items[e]) for e in ENG})
        _CACHE["marks"] = marks
    return nc


_CACHE = {}


def _host_shared(inp):
    f = np.float32
    sh = dict(_consts())
    cols, vcols = _wina_cols()
    w_in = np.asarray(inp["w_in"], f)
    sh["w_inA"] = np.ascontiguousarray(w_in[:, :, cols])
    sh["w_inV"] = np.ascontiguousarray(w_in[:, :, vcols])
    sh["w_inG"] = np.ascontiguousarray(w_in[:, :, 1792:])
    sh["w_mod"] = np.asarray(inp["w_mod"], f)
    sh["b_modT"] = np.ascontiguousarray(np.asarray(inp["b_mod"], f).reshape(L, 48, 128).transpose(2, 0, 1))
    g = np.asarray(inp["qk_gain"], f)
    d_ = np.arange(128) % 64
    pd = np.array([_partner(x) for x in d_])
    gt = np.stack([g[:, 0, d_], g[:, 0, pd], g[:, 1, d_], g[:, 1, pd]], -1)
    sh["gainT"] = np.ascontiguousarray(gt.transpose(1, 0, 2))
    sk = np.asarray(inp["sink_logit"], f).reshape(1, L * 4)
    sh["sinkB"] = np.ascontiguousarray(np.broadcast_to(sk, (128, L * 4)))
    rpb = np.asarray(inp["na_rpb"], f)
    p = np.arange(128)
    kcol = (p % 64)[:, None, None]
    i_ = (p // 64)[:, None, None]
    m_ = np.arange(15)[None, :, None]
    c_ = np.arange(64)[None, None, :]
    a_ = np.clip(14 - m_ + i_, 0, 14) + 0 * c_
    oc = np.clip(kcol - c_ + 15, 0, 30) + 0 * m_
    tb = rpb[:, :, a_, oc]
    sh["tbs"] = np.ascontiguousarray(tb.transpose(0, 2, 1, 3, 4).reshape(L, 128, 4 * 15 * 64))
    wb = np.asarray(inp["w_branch"], f)
    perm = np.concatenate([np.arange(0, 64), np.arange(128, 192), np.arange(64, 128), np.arange(192, 256)])
    wbp = wb.copy()
    for i in (0, 2, 3):
        wbp[:, i] = wb[:, i][:, perm]
    sh["w_brP"] = np.ascontiguousarray(wbp.reshape(L, 1024, D))
    sh["w_out"] = np.asarray(inp["w_out"], f)
    ln = np.stack([inp["ln1_g"], inp["ln1_b"], inp["ln2_g"], inp["ln2_b"]], 1).astype(f)
    sh["lnT"] = np.ascontiguousarray(ln.reshape(L, 4, KC, 128).transpose(3, 0, 1, 2).reshape(128, L * 4 * KC))
    sh["w_router"] = np.asarray(inp["w_router"], f)
    sh["w_gate"] = np.asarray(inp["w_gate"], f)
    sh["w_up"] = np.asarray(inp["w_up"], f)
    sh["w_down"] = np.asarray(inp["w_down"], f)
    return sh


def _host_core(inp, core):
    f = np.float32
    b0 = core * 2
    x = np.asarray(inp["x"], f)
    ctx = np.asarray(inp["ctx"], f)
    c = np.asarray(inp["c"], f)
    cc = np.asarray(inp["c_ctx"], f)
    d = {}
    d["x2"] = np.ascontiguousarray(np.concatenate([ctx[b0:b0 + 2], x[b0:b0 + 2]], axis=1))
    cv = np.stack([c[b0], c[b0 + 1], cc, np.zeros_like(cc)], -1)
    d["cT"] = np.ascontiguousarray(cv.reshape(KC, 128, 4).transpose(1, 0, 2))
    return d


def kernel(**inputs):
    n = 8
    if "nc" not in _CACHE:
        _CACHE["nc"] = build_program(debug=False)
    nc = _CACHE["nc"]
    sh = _host_shared(inputs)
    in_maps = []
    for core in range(n):
        m = dict(sh)
        m.update(_host_core(inputs, core))
        in_maps.append(m)
    res = run_bass_kernel_spmd(nc, in_maps, core_ids=list(range(n)))
    _CACHE["last"] = res
    outs = [np.asarray(r["out"], np.float32) for r in res.results]
    return np.concatenate(outs, axis=0)
```

```python
import numpy as np
from contextlib import ExitStack
import ml_dtypes
import concourse.bass as bass
import concourse.mybir as mybir
from concourse.bass_utils import run_bass_kernel_spmd

F32 = mybir.dt.float32
BF16 = mybir.dt.bfloat16
U32 = mybir.dt.uint32
I32 = mybir.dt.int32
ALU = mybir.AluOpType
AF = mybir.ActivationFunctionType

L = 2
D = 1024
KC = 8
NT = 2304
NCX = 256
NL = 2048
E = 16
ALPHA = (2 * L) ** 0.25
LN_EPS_S = 1e-6 / (ALPHA * ALPHA)
RMS_EPS = 1e-6
NA_COLS = 17 * 128
ENG = ("pe", "act", "dve", "pool", "sp")


class T:
    __slots__ = ("h", "w", "r", "name")

    def __init__(self, h, name=""):
        self.h = h
        self.w = {}
        self.r = {}
        self.name = name

    def __getitem__(self, idx):
        return V(self, self.h[idx])

    def full(self):
        return V(self, self.h[:])


class V:
    __slots__ = ("t", "ap")

    def __init__(self, t, ap):
        self.t = t
        self.ap = ap

    def re(self, pat, **kw):
        return V(self.t, self.ap.rearrange(pat, **kw))

    def __getitem__(self, idx):
        return V(self.t, self.ap[idx])


def _ap(x):
    return x.ap if isinstance(x, V) else x


def _ts(xs):
    return [x.t for x in xs if isinstance(x, V)]


class Sched:
    def __init__(self, nc, es, n_dma_sems=(("sp", 12), ("pool", 10), ("act", 4))):
        self.nc = nc
        self.es = es
        self.items = {e: [] for e in ENG}
        self.sems = {}
        self.cnt = {}
        self.seen = {e: {} for e in ENG}
        for e in ENG:
            self._mk("c_" + e)
        self.dma_pool = {}
        self.dma_rr = {}
        for e, n in n_dma_sems:
            self.dma_pool[e] = [self._mk("d_%s%d" % (e, i)) for i in range(n)]
            self.dma_rr[e] = 0
        self.nops = 0

    def _mk(self, key):
        self.sems[key] = self.es.enter_context(self.nc.semaphore(key))
        self.cnt[key] = 0
        return key

    def _need(self, e, key, val):
        if val <= 0 or self.seen[e].get(key, 0) >= val:
            return
        self.seen[e][key] = val
        self.items[e].append(("wait", key, val))

    def _deps(self, e, reads, writes, is_pe=False, is_dma=False):
        own = "c_" + e if not is_dma else "__none__"
        for t in reads:
            for k, v in t.w.items():
                if is_pe and k == own:
                    continue
                self._need(e, k, v)
        for t in writes:
            for k, v in t.r.items():
                if k != own:
                    self._need(e, k, v)
            for k, v in t.w.items():
                if k != own:
                    self._need(e, k, v)

    def _mark(self, key, val, reads, writes):
        for t in reads:
            t.r[key] = max(t.r.get(key, 0), val)
        for t in writes:
            if t.r:
                t.r = {}
                t.w = {}
            t.w[key] = max(t.w.get(key, 0), val)

    def op(self, e, fn, reads=(), writes=()):
        reads = _ts(reads)
        writes = _ts(writes)
        self._deps(e, reads, writes, is_pe=(e == "pe"))
        key = "c_" + e
        self.cnt[key] += 1
        self.items[e].append(("op", fn, key, 1))
        self._mark(key, self.cnt[key], reads, writes)
        self.nops += 1

    def dma(self, e, out, in_, extra_reads=(), fn=None, **kw):
        reads = _ts([in_] + list(extra_reads))
        writes = _ts([out])
        pool = self.dma_pool[e]
        key = pool[self.dma_rr[e] % len(pool)]
        self.dma_rr[e] += 1
        self._need(e, key, self.cnt[key])
        self._deps(e, reads, writes, is_dma=True)
        self.cnt[key] += 16
        if fn is None:
            o, i = _ap(out), _ap(in_)

            def fn(eng, o=o, i=i, kw=kw):
                return eng.dma_start(out=o, in_=i, **kw)
        self.items[e].append(("op", fn, key, 16))
        self._mark(key, self.cnt[key], reads, writes)
        self.nops += 1

    def barrier(self):
        for e in ENG:
            for k, v in self.cnt.items():
                if k != "c_" + e:
                    self._need(e, k, v)

    def emit(self):
        nc = self.nc
        with nc.Block() as block:
            def run(e):
                def body(eng):
                    for it in self.items[e]:
                        if it[0] == "wait":
                            eng.wait_ge(self.sems[it[1]], it[2])
                        else:
                            it[1](eng).then_inc(self.sems[it[2]], it[3])
                return body
            block.tensor(run("pe"))
            block.scalar(run("act"))
            block.vector(run("dve"))
            block.gpsimd(run("pool"))
            block.sync(run("sp"))

    def mm(self, out, lhsT, rhs, start=True, stop=True):
        o, a, b = _ap(out), _ap(lhsT), _ap(rhs)
        self.op("pe", lambda e: e.matmul(o, a, b, start=start, stop=stop), reads=[lhsT, rhs], writes=[out])

    def tr(self, out, in_, ident):
        o, a, b = _ap(out), _ap(in_), _ap(ident)
        self.op("pe", lambda e: e.transpose(o, a, b), reads=[in_, ident], writes=[out])

    def act(self, out, in_, func, scale=1.0, bias=0.0):
        o, i = _ap(out), _ap(in_)
        sc, bi = _ap(scale), _ap(bias)
        self.op("act", lambda e: e.activation(out=o, in_=i, func=func, bias=bi, scale=sc),
                reads=[in_, scale, bias], writes=[out])

    def tt(self, eng, out, a, b, op):
        o, x, y = _ap(out), _ap(a), _ap(b)
        self.op(eng, lambda e: e.tensor_tensor(o, x, y, op), reads=[a, b], writes=[out])

    def ts(self, eng, out, a, s1, op0, s2=None, op1=None):
        o, x, p1, p2 = _ap(out), _ap(a), _ap(s1), _ap(s2)
        if op1 is None:
            self.op(eng, lambda e: e.tensor_scalar(o, x, p1, None, op0), reads=[a, s1], writes=[out])
        else:
            self.op(eng, lambda e: e.tensor_scalar(o, x, p1, p2, op0, op1), reads=[a, s1, s2], writes=[out])

    def stt(self, out, a, s, b, op0, op1):
        o, x, p, y = _ap(out), _ap(a), _ap(s), _ap(b)
        self.op("dve", lambda e: e.scalar_tensor_tensor(o, x, p, y, op0, op1), reads=[a, s, b], writes=[out])

    def copy(self, eng, out, in_):
        o, i = _ap(out), _ap(in_)
        if eng == "act":
            self.op("act", lambda e: e.activation(out=o, in_=i, func=AF.Copy), reads=[in_], writes=[out])
        else:
            self.op(eng, lambda e: e.tensor_copy(o, i), reads=[in_], writes=[out])

    def recip(self, out, in_):
        o, i = _ap(out), _ap(in_)
        self.op("dve", lambda e: e.reciprocal(o, i), reads=[in_], writes=[out])

    def memset(self, eng, out, val):
        o = _ap(out)
        self.op(eng, lambda e: e.memset(o, val), writes=[out])


def _partner(d):
    return d + 16 if (d % 32) < 16 else d - 16


def _rope_tables():
    cos = np.ones((128, NT), np.float32)
    sin = np.zeros((128, NT), np.float32)
    t = np.arange(NL)
    inv = (np.float32(10000.0) ** (-np.arange(0, 32, 2, dtype=np.float32) / np.float32(32))).astype(np.float32)
    for p in range(128):
        d = p % 64
        pos = (t // 64) if d < 32 else (t % 64)
        ang = pos.astype(np.float32) * inv[d % 16]
        cos[p, NCX:] = np.cos(ang).astype(np.float32)
        sgn = -1.0 if (d % 32) < 16 else 1.0
        sin[p, NCX:] = sgn * np.sin(ang).astype(np.float32)
    return cos, sin


def _na_masks():
    rows, W, kh, kw = 32, 64, 8, 16
    t = np.arange(NL)
    r, c = t // W, t % W
    r0 = np.clip(r - kh // 2, 0, rows - kh)
    c0 = np.clip(c - kw // 2, 0, W - kw)
    k = np.arange(NL)
    kr, kcol = k // W, k % W
    valid = ((kr[None, :] >= r0[:, None]) & (kr[None, :] < r0[:, None] + kh) &
             (kcol[None, :] >= c0[:, None]) & (kcol[None, :] < c0[:, None] + kw))
    full = np.zeros((16, 7, 128, 128), np.float32)
    for b in range(16):
        for dl in range(-3, 4):
            kci = b + dl
            if 0 <= kci < 16:
                full[b, dl + 3] = valid[b * 128:(b + 1) * 128, kci * 128:(kci + 1) * 128].T
    types = [0, 1] + [2] * 12 + [3, 4]
    rep = {0: 0, 1: 1, 2: 5, 3: 14, 4: 15}
    for b in range(16):
        assert np.array_equal(full[b], full[rep[types[b]]]), b
    namc = np.zeros((3, 7, 128, 512), np.float32)
    for qt, b0 in enumerate((0, 4, 12)):
        for bi in range(4):
            namc[qt, :, :, bi * 128:(bi + 1) * 128] = full[b0 + bi]
    for b0 in (4, 8):
        for bi in range(4):
            assert np.array_equal(full[b0 + bi], full[5])
    qdl = [[dl for dl in range(-3, 4) if namc[qt, dl + 3].any()] for qt in range(3)]
    return namc, qdl


def _dft(n):
    t = np.arange(n, dtype=np.int64)
    m = (t[:, None] * t[None, :]) % n
    ang = 2.0 * np.pi * m.astype(np.float64) / n
    return np.cos(ang), np.sin(ang)


_NA_MASKC, _NA_QDELTAS = _na_masks()


def _consts():
    c = {}
    cos, sin = _rope_tables()
    c["rope"] = np.stack([cos, sin], 1).copy()
    ident = np.eye(128, dtype=np.float32)
    onesbd = np.zeros((128, 128), np.float32)
    onesbd[:64, :64] = 1.0
    onesbd[64:, 64:] = 1.0
    c64, s64 = _dft(64)
    cbd = np.zeros((128, 128), np.float64)
    sbd = np.zeros((128, 128), np.float64)
    for g in range(2):
        cbd[g * 64:(g + 1) * 64, g * 64:(g + 1) * 64] = c64 / 8.0
        sbd[g * 64:(g + 1) * 64, g * 64:(g + 1) * 64] = s64 / 8.0
    win = np.zeros((3, 128, 128), np.float32)
    i = np.arange(128)[:, None]
    j = np.arange(128)[None, :]
    win[0] = (j <= i)
    win[1] = 1.0
    win[2] = (i <= j)
    permm = np.zeros((128, 128), np.float32)
    for m_ in range(128):
        permm[(m_ // 64) * 64 + _partner(m_ % 64), m_] = 1.0
    cb = np.concatenate([ident, onesbd, cbd.astype(np.float32), sbd.astype(np.float32), permm], axis=1)
    c["cbf"] = cb.astype(ml_dtypes.bfloat16)
    winc = np.zeros((3, 3, 128, 512), np.float32)
    for qt, b0 in enumerate((0, 4, 12)):
        for bi in range(4):
            for dl in (-1, 0, 1):
                if 0 <= b0 + bi + dl < 16:
                    winc[qt, dl + 1, :, bi * 128:(bi + 1) * 128] = win[dl + 1]
    mk = np.concatenate([winc[qt, d_] for qt in range(3) for d_ in range(3)] +
                        [_NA_MASKC[qt, d_] for qt in range(3) for d_ in range(7)], axis=1)
    c["maskc"] = mk.astype(ml_dtypes.bfloat16)
    cf = np.concatenate([ident, np.full((128, 128), 1.0 / D, np.float32), np.ones((128, 128), np.float32)], axis=1)
    c["cf32"] = cf.astype(np.float32)
    cs, ss = _dft(NL)
    c["dftc"] = (cs / np.sqrt(NL)).astype(ml_dtypes.bfloat16)
    c["dfts"] = (-ss / np.sqrt(NL)).astype(ml_dtypes.bfloat16)
    cs, ss = _dft(NCX)
    c["dftc_c"] = (cs / np.sqrt(NCX)).astype(ml_dtypes.bfloat16)
    c["dfts_c"] = (-ss / np.sqrt(NCX)).astype(ml_dtypes.bfloat16)
    return c


def _wina_cols():
    offs = {"qA": 0, "kA": 256, "vA": 384, "f": 512, "qC": 768, "kC": 1024, "vC": 1152,
            "qD": 1280, "kD": 1536, "vD": 1664}

    def qch(base, pair, perm):
        cols = []
        for hq in pair:
            for d_ in range(64):
                dd = _partner(d_) if perm else d_
                cols.append(base + hq * 64 + dd)
        return cols
    cols = []
    for mname, roped in (("A", True), ("C", True), ("D", False)):
        qb, kb = offs["q" + mname], offs["k" + mname]
        cols += qch(qb, (0, 2), False) + qch(qb, (1, 3), False)
        if roped:
            cols += qch(qb, (0, 2), True) + qch(qb, (1, 3), True)
        cols += qch(kb, (0, 1), False)
        if roped:
            cols += qch(kb, (0, 1), True)
    cols += list(range(offs["f"], offs["f"] + 256))
    vcols = list(range(384, 512)) + list(range(1152, 1280)) + list(range(1664, 1792))
    return np.array(cols), np.array(vcols)


def build_program(debug=False, stop_after=None):
    nc = bass.Bass("TRN2", target_bir_lowering=False)

    def din(name, shape, dt=F32):
        return nc.dram_tensor(name, list(shape), dt, kind="ExternalInput").ap()

    def dscr(name, shape, dt):
        return nc.dram_tensor(name, list(shape), dt, kind=("ExternalOutput" if debug else "Internal")).ap()

    x2 = din("x2", [2, NT, D])
    cT = din("cT", [128, KC, 4])
    w_mod = din("w_mod", [L, D, 6 * D])
    b_modT = din("b_modT", [128, L, 48])
    w_inA = din("w_inA", [L, D, NA_COLS])
    w_inV = din("w_inV", [L, D, 384])
    w_inG = din("w_inG", [L, D, 4096])
    gainT = din("gainT", [128, L, 4])
    sinkB = din("sinkB", [128, L * 4])
    tbs = din("tbs", [L, 128, 4 * 15 * 64])
    rope = din("rope", [128, 2, NT])
    cbf = din("cbf", [128, 5 * 128], BF16)
    maskc = din("maskc", [128, 30 * 512], BF16)
    cf32 = din("cf32", [128, 384])
    dftc = din("dftc", [NL, NL], BF16)
    dfts = din("dfts", [NL, NL], BF16)
    dftc_c = din("dftc_c", [NCX, NCX], BF16)
    dfts_c = din("dfts_c", [NCX, NCX], BF16)
    w_brP = din("w_brP", [L, 1024, D])
    w_out = din("w_out", [L, D, D])
    lnT = din("lnT", [128, L * 4 * KC])
    w_router = din("w_router", [L, D, E])
    w_gate = din("w_gate", [L, E, D, D])
    w_up = din("w_up", [L, E, D, D])
    w_down = din("w_down", [L, E, D, D])
    out = nc.dram_tensor("out", [2, NL, D], F32, kind="ExternalOutput").ap()

    hT_d = dscr("hT_d", [2, 128, KC, NT], F32)
    uT_d = dscr("uT_d", [2, 128, KC, NT], BF16)
    qT_d = dscr("qT_d", [2, 128, 6, NT], BF16)
    kT_d = dscr("kT_d", [2, 128, 3, NT], BF16)
    v_d = dscr("v_d", [2, 128, 18, 768], BF16)
    fT_d = dscr("fT_d", [2, 128, 2, NT], BF16)
    brT_d = dscr("brT_d", [2, 128, 8, NT], BF16)
    u2tok_d = dscr("u2tok_d", [2 * NT, D], BF16)
    ffn_d = dscr("ffn_d", [2 * NT, D], F32)
    aff_d = dscr("aff_d", [2, 16, NT], F32)

    CH512 = [(0, 256)] + [(256 + i * 512, 512) for i in range(4)]
    CH256 = [(i * 256, 256) for i in range(9)]

    with ExitStack() as es:
        S = Sched(nc, es)

        uid = [0]

        def sb(shape, dt, name, stack=es):
            uid[0] += 1
            nm = "%s_%d" % (name, uid[0])
            return T(stack.enter_context(nc.sbuf_tensor(nm, list(shape), dt)), nm)

        PSF = [T(es.enter_context(nc.psum_tensor("psf%d" % i, [128, 512], F32)), "psf%d" % i) for i in range(6)]
        PSB = [T(es.enter_context(nc.psum_tensor("psb%d" % i, [128, 1024], BF16)), "psb%d" % i) for i in range(2)]

        CB = sb([128, 5 * 128], BF16, "CB")
        CF = sb([128, 384], F32, "CF")
        MOD = sb([128, L, 48, 4], F32, "MOD")
        LNT = sb([128, L * 4 * KC], F32, "LNT")
        GAIN = sb([128, L, 4], F32, "GAIN")
        SINKE = sb([128, L * 4], F32, "SINKE")
        S.dma("sp", CB.full(), cbf[:, :])
        S.dma("sp", CF.full(), cf32[:, :])
        S.dma("sp", LNT.full(), lnT[:, :])
        S.dma("sp", GAIN.full(), gainT[:, :, :])
        S.dma("sp", SINKE.full(), sinkB[:, :])
        S.act(SINKE.full(), SINKE.full(), AF.Exp)
        ident_b = CB[:, 0:128]
        onesbd = CB[:, 128:256]
        c64bd = CB[:, 256:384]
        s64bd = CB[:, 384:512]
        permm = CB[:, 512:640]

        ident_f = CF[:, 0:128]
        ones_mean = CF[:, 128:256]
        ones_f = CF[:, 256:384]

        def lnv(l, which, kc):
            i_ = (l * 4 + which) * KC + kc
            return LNT[:, i_:i_ + 1]

        def mod(l, kind, kc, j):
            return MOD[:, l, kind * 8 + kc, j:j + 1]

        def cast_load(dst, src_ap_fn, ncols, eng="pool", step=1024):
            for c0 in range(0, ncols, step):
                c1 = min(ncols, c0 + step)
                S.dma(eng, dst[:, :, c0:c1], src_ap_fn(c0, c1))

        marks = []

        def phase_end(name="?"):
            S.barrier()
            marks.append((name, S.cnt["c_pe"]))

        with ExitStack() as ps:
            cTs = sb([128, KC, 4], F32, "cTs", ps)
            bm = sb([128, L, 48], F32, "bm", ps)
            wm = [sb([128, KC, 768], F32, "wm%d" % i, ps) for i in range(2)]
            S.dma("sp", cTs.full(), cT[:, :, :])
            S.dma("sp", bm.full(), b_modT[:, :, :])
            S.act(cTs.full(), cTs.full(), AF.Silu)
            n = 0
            for l in range(L):
                pm = PSF[l]
                for blk in range(8):
                    w = wm[n % 2]
                    n += 1
                    S.dma("sp", w.full(), w_mod[l, :, blk * 768:(blk + 1) * 768].rearrange("(kc p) n -> p kc n", p=128))
                    for cc in range(6):
                        ccg = blk * 6 + cc
                        for kc in range(KC):
                            S.mm(pm[:, ccg * 4:ccg * 4 + 4], w[:, kc, cc * 128:(cc + 1) * 128], cTs[:, kc, :],
                                 start=(kc == 0), stop=(kc == KC - 1))
                for ccg in range(48):
                    S.ts("dve", MOD[:, l, ccg, :], pm[:, ccg * 4:ccg * 4 + 4], bm[:, l, ccg:ccg + 1], ALU.add)
                for kind in (1, 4):
                    S.ts("dve", MOD[:, l, kind * 8:(kind + 1) * 8, :], MOD[:, l, kind * 8:(kind + 1) * 8, :], 1.0, ALU.add)
                for kind in (2, 5):
                    S.ts("dve", MOD[:, l, kind * 8:(kind + 1) * 8, :], MOD[:, l, kind * 8:(kind + 1) * 8, :],
                         1.0 / ALPHA, ALU.mult)
            phase_end("p0")

        with ExitStack() as ps:
            xin = [sb([128, 4, D], F32, "xin%d" % i, ps) for i in range(2)]
            hst = [sb([128, KC, 512], F32, "hst%d" % i, ps) for i in range(2)]
            n = 0
            for s in range(2):
                for (t0, Tn) in CH512:
                    nt = Tn // 128
                    xi, hs = xin[n % 2], hst[n % 2]
                    n += 1
                    S.dma("sp", xi[:, 0:nt, :], x2[s, t0:t0 + Tn, :].rearrange("(ti p) d -> p ti d", p=128))
                    for kc in range(KC):
                        pt = PSF[kc % 4]
                        for ti in range(nt):
                            S.tr(pt[:, ti * 128:(ti + 1) * 128], xi[:, ti, kc * 128:(kc + 1) * 128], ident_f)
                        S.copy("act" if kc % 2 == 0 else "dve", hs[:, kc, 0:Tn], pt[:, 0:Tn])
                    S.dma("pool", hT_d[s, :, :, t0:t0 + Tn], hs[:, :, 0:Tn])
            phase_end("pin")

        if stop_after == "pin":
            S.barrier()
            S.emit()
            return nc

        def layernorm(zT, Tn, l, which_g, tmp):
            zsq, mean_sb, var_sb, accs, accq = tmp
            pm, pq = PSF[4], PSF[5]
            for kc in range(KC):
                q = zsq[kc % 2]
                S.act(q[:, 0:Tn], zT[:, kc, 0:Tn], AF.Square)
                if kc == 1:
                    S.tt("pool", accs[:, 0:Tn], zT[:, 0, 0:Tn], zT[:, 1, 0:Tn], ALU.add)
                    S.tt("dve", accq[:, 0:Tn], zsq[0][:, 0:Tn], zsq[1][:, 0:Tn], ALU.add)
                elif kc > 1:
                    S.tt("pool", accs[:, 0:Tn], accs[:, 0:Tn], zT[:, kc, 0:Tn], ALU.add)
                    S.tt("dve", accq[:, 0:Tn], accq[:, 0:Tn], q[:, 0:Tn], ALU.add)
            S.mm(pm[:, 0:Tn], ones_mean, accs[:, 0:Tn])
            S.mm(pq[:, 0:Tn], ones_mean, accq[:, 0:Tn])
            S.copy("act", mean_sb[:, 0:Tn], pm[:, 0:Tn])
            S.tt("dve", var_sb[:, 0:Tn], mean_sb[:, 0:Tn], mean_sb[:, 0:Tn], ALU.mult)
            S.tt("dve", var_sb[:, 0:Tn], pq[:, 0:Tn], var_sb[:, 0:Tn], ALU.subtract)
            S.act(var_sb[:, 0:Tn], var_sb[:, 0:Tn], AF.Sqrt, bias=EPSV[:, 0:1])
            S.recip(var_sb[:, 0:Tn], var_sb[:, 0:Tn])
            for kc in range(KC):
                S.tt("pool", zT[:, kc, 0:Tn], zT[:, kc, 0:Tn], mean_sb[:, 0:Tn], ALU.subtract)
                S.tt("dve", zT[:, kc, 0:Tn], zT[:, kc, 0:Tn], var_sb[:, 0:Tn], ALU.mult)
                S.act(zT[:, kc, 0:Tn], zT[:, kc, 0:Tn], AF.Identity, scale=lnv(l, which_g, kc), bias=lnv(l, which_g + 1, kc))

        EPSV = sb([128, 2], F32, "EPSV")
        S.memset("dve", EPSV[:, 0:1], LN_EPS_S)
        S.memset("dve", EPSV[:, 1:2], RMS_EPS)

        def p1(l, samples):
            with ExitStack() as ps:
                wA = sb([128, KC, NA_COLS], BF16, "wA", ps)
                wV = sb([128, KC, 384], BF16, "wV", ps)
                ROPE = sb([128, 2, NT], F32, "ROPE", ps)
                hTc = [sb([128, KC, 512], F32, "hTc%d" % i, ps) for i in range(2)]
                uTc = [sb([128, KC, 512], BF16, "uTc%d" % i, ps) for i in range(2)]
                stg = [sb([128, 512], BF16, "stg%d" % i, ps) for i in range(4)]
                sq = sb([128, 512], BF16, "sq", ps)
                qb_ = sb([128, 512], BF16, "qb_", ps)
                rs = sb([128, 512], F32, "rs", ps)
                t1 = sb([128, 512], F32, "t1", ps)
                t2 = sb([128, 512], F32, "t2", ps)
                vst = [sb([128, 4, 6, 128], BF16, "vst%d" % i, ps) for i in range(2)]
                cast_load(wA, lambda c0, c1: w_inA[l, :, c0:c1].rearrange("(kc p) n -> p kc n", p=128), NA_COLS)
                cast_load(wV, lambda c0, c1: w_inV[l, :, c0:c1].rearrange("(kc p) n -> p kc n", p=128), 384)
                S.dma("sp", ROPE.full(), rope[:, :, :])
                for i in range(2):
                    S.memset("pool", vst[i].full(), 1.0)
                nst = 0
                for s in samples:
                    for ci, (t0, Tn) in enumerate(CH512):
                        j = 2 if t0 == 0 else s
                        h, u = hTc[ci % 2], uTc[ci % 2]
                        S.dma("sp", h[:, :, 0:Tn], hT_d[s, :, :, t0:t0 + Tn])
                        for kc in range(KC):
                            S.act(u[:, kc, 0:Tn], h[:, kc, 0:Tn], AF.Identity, scale=mod(l, 1, kc, j), bias=mod(l, 0, kc, j))
                        S.dma("pool", uT_d[s, :, :, t0:t0 + Tn], u[:, :, 0:Tn])

                        def proj(ps_t, cc):
                            for kc in range(KC):
                                S.mm(ps_t[:, 0:Tn], wA[:, kc, cc * 128:(cc + 1) * 128], u[:, kc, 0:Tn],
                                     start=(kc == 0), stop=(kc == KC - 1))

                        def store(dst_ap, v):
                            S.dma("pool", dst_ap, v)

                        plain = [(12, qT_d[s, :, 4, t0:t0 + Tn]), (13, qT_d[s, :, 5, t0:t0 + Tn]),
                                 (14, kT_d[s, :, 2, t0:t0 + Tn]), (15, fT_d[s, :, 0, t0:t0 + Tn]),
                                 (16, fT_d[s, :, 1, t0:t0 + Tn])]
                        for n_, (cc, dst) in enumerate(plain):
                            pt = PSF[n_ % 2]
                            proj(pt, cc)
                            st = stg[nst % 4]
                            nst += 1
                            S.copy("act", st[:, 0:Tn], pt[:, 0:Tn])
                            store(dst, st[:, 0:Tn])
                        roped = [(0, 2, 0, 1, True, qT_d[s, :, 0, t0:t0 + Tn]), (1, 3, 0, 1, True, qT_d[s, :, 1, t0:t0 + Tn]),
                                 (4, 5, 2, 3, True, kT_d[s, :, 0, t0:t0 + Tn]),
                                 (6, 8, None, None, False, qT_d[s, :, 2, t0:t0 + Tn]),
                                 (7, 9, None, None, False, qT_d[s, :, 3, t0:t0 + Tn]),
                                 (10, 11, None, None, False, kT_d[s, :, 1, t0:t0 + Tn])]
                        for n_, (cb_, cp_, g0, g1, norm, dst) in enumerate(roped):
                            pa, pb = PSF[2 * (n_ % 2)], PSF[2 * (n_ % 2) + 1]
                            proj(pa, cb_)
                            proj(pb, cp_)
                            cosv = ROPE[:, 0, t0:t0 + Tn]
                            sinv = ROPE[:, 1, t0:t0 + Tn]
                            st = stg[nst % 4]
                            nst += 1
                            if norm:
                                S.act(sq[:, 0:Tn], pa[:, 0:Tn], AF.Square)
                                S.mm(PSF[4][:, 0:Tn], onesbd, sq[:, 0:Tn])
                                S.act(rs[:, 0:Tn], PSF[4][:, 0:Tn], AF.Sqrt, scale=1.0 / 64.0, bias=EPSV[:, 1:2])
                                S.recip(rs[:, 0:Tn], rs[:, 0:Tn])
                                S.stt(t1[:, 0:Tn], pa[:, 0:Tn], GAIN[:, l, g0:g0 + 1], cosv, ALU.mult, ALU.mult)
                                S.stt(t2[:, 0:Tn], pb[:, 0:Tn], GAIN[:, l, g1:g1 + 1], sinv, ALU.mult, ALU.mult)
                                S.tt("pool", t1[:, 0:Tn], t1[:, 0:Tn], t2[:, 0:Tn], ALU.add)
                                S.tt("dve", st[:, 0:Tn], t1[:, 0:Tn], rs[:, 0:Tn], ALU.mult)
                            else:
                                S.tt("dve", t1[:, 0:Tn], pa[:, 0:Tn], cosv, ALU.mult)
                                S.tt("dve", t2[:, 0:Tn], pb[:, 0:Tn], sinv, ALU.mult)
                                S.tt("pool", st[:, 0:Tn], t1[:, 0:Tn], t2[:, 0:Tn], ALU.add)
                            store(dst, st[:, 0:Tn])
                        vs = vst[ci % 2]
                        nt = Tn // 128
                        for ti in range(nt):
                            pv = PSF[5]
                            for kc in range(KC):
                                S.mm(pv[:, 0:384], u[:, kc, ti * 128:(ti + 1) * 128], wV[:, kc, :],
                                     start=(kc == 0), stop=(kc == KC - 1))
                            S.copy("act", vs[:, ti, :, 0:64], pv[:, 0:384].re("p (m d) -> p m d", d=64))
                        S.dma("pool", v_d[s, :, t0 // 128:t0 // 128 + nt, :].rearrange("p t (m d) -> p t m d", d=128),
                              vs[:, 0:nt, :, :])
            phase_end("p1")

        def p2(s, l):
            with ExitStack() as ps:
                kT = sb([128, 3, NT], BF16, "kT", ps)
                Vt = sb([128, 18, 768], BF16, "Vt", ps)
                TB = sb([128, 4 * 15 * 64], F32, "TB", ps)
                MK = sb([128, 30 * 512], BF16, "MK", ps)
                qcb = [sb([128, 6, 512], BF16, "qc%d" % i, ps) for i in range(2)]
                Pb = [sb([128, 512], BF16, "Pb%d" % i, ps) for i in range(3)]
                Pe = [sb([128, 512], BF16, "Pe%d" % i, ps) for i in range(3)]
                Pm = [sb([128, 512], BF16, "Pm%d" % i, ps) for i in range(2)]
                sbias = [sb([128, 512], F32, "sbias%d" % i, ps) for i in range(2)]
                rd = sb([128, 512], F32, "rd", ps)
                dsb = sb([128, 512], F32, "dsb", ps)
                brst = [sb([128, 512], BF16, "brst%d" % i, ps) for i in range(2)]
                S.dma("sp", kT.full(), kT_d[s, :, :, :])
                S.dma("sp", Vt.full(), v_d[s, :, :, :])
                S.dma("sp", TB.full(), tbs[l, :, :])
                S.dma("sp", MK.full(), maskc[:, :])
                TBv = TB.full().re("p (h m c) -> p h m c", h=4, m=15)

                def winc(qt, dl):
                    i_ = qt * 3 + dl + 1
                    return MK[:, i_ * 512:(i_ + 1) * 512]

                def namc(qt, dl):
                    i_ = 9 + qt * 7 + dl + 3
                    return MK[:, i_ * 512:(i_ + 1) * 512]
                qch = ([(0, 256, True)] if l == 0 else []) + [(256 + i * 512, 512, False) for i in range(4)]
                cnt = {}

                def nxt(k, lst):
                    c_ = cnt.get(k, 0)
                    cnt[k] = c_ + 1
                    return lst[c_ % len(lst)]
                for qi, (t0, Tn, is_ctx) in enumerate(qch):
                    qc = qcb[qi % 2]
                    S.dma("sp", qc[:, :, 0:Tn], qT_d[s, :, :, t0:t0 + Tn])
                    b0 = (t0 - NCX) // 128
                    qt = 0 if b0 == 0 else (2 if b0 == 12 else 1)
                    for m in range(3):
                        for qcnk in range(2):
                            bst = nxt("b", brst)
                            for ph in range(2):
                                hq = qcnk + 2 * ph
                                O = nxt("o", [PSF[3], PSF[4]])
                                p0, p1_ = ph * 64, (ph + 1) * 64
                                qv = qc[p0:p1_, m * 2 + qcnk, 0:Tn]
                                vo = (m * 2 + ph) * 128
                                if is_ctx:
                                    steps = [("g", 0), ("g", 1)]
                                elif m == 0:
                                    steps = [("g", kc) for kc in range(18)]
                                else:
                                    dls = [-1, 0, 1] if m == 1 else _NA_QDELTAS[qt]
                                    steps = [("g", 0), ("g", 1)] + [("b", dl) for dl in dls]

                                def emit_S(st):
                                    Sp = nxt("s", [PSF[0], PSF[1], PSF[2]])
                                    if st[0] == "g":
                                        kc = st[1]
                                        S.mm(Sp[:, 0:Tn], kT[p0:p1_, m, kc * 128:(kc + 1) * 128], qv)
                                    else:
                                        for bi in range(4):
                                            kc = 2 + min(15, max(0, b0 + bi + st[1]))
                                            S.mm(Sp[:, bi * 128:(bi + 1) * 128], kT[p0:p1_, m, kc * 128:(kc + 1) * 128],
                                                 qc[p0:p1_, m * 2 + qcnk, bi * 128:(bi + 1) * 128])
                                    return Sp

                                def emit_rest(st, Sp, first, last):
                                    if st[0] == "g":
                                        kc = st[1]
                                        P = nxt("p", Pb)
                                        S.act(P[:, 0:Tn], Sp[:, 0:Tn], AF.Exp, scale=0.125)
                                        S.mm(O[:, 0:Tn], Vt[:, kc, vo:vo + 128], P[:, 0:Tn], start=first, stop=last)
                                        return
                                    dl = st[1]
                                    pe_ = nxt("e", Pe)
                                    if m == 2:
                                        sbv = nxt("e2", sbias)
                                        m0 = 7 - 2 * dl
                                        for bi in range(4):
                                            S.stt(sbv[:, bi * 128:(bi + 1) * 128].re("p (j c) -> p j c", c=64),
                                                  Sp[:, bi * 128:(bi + 1) * 128].re("p (j c) -> p j c", c=64), 0.125,
                                                  TBv[:, hq, m0:m0 + 2, :], ALU.mult, ALU.add)
                                        S.act(pe_.full(), sbv.full(), AF.Exp)
                                        pfin = nxt("m", Pm)
                                        S.tt("pool", pfin.full(), pe_.full(), namc(qt, dl), ALU.mult)
                                    else:
                                        S.act(pe_.full(), Sp.full(), AF.Exp, scale=0.125)
                                        if dl == 0:
                                            pfin = pe_
                                        else:
                                            pfin = nxt("m", Pm)
                                            S.tt("pool", pfin.full(), pe_.full(), winc(qt, dl), ALU.mult)
                                    valid = [bi for bi in range(4) if 0 <= b0 + bi + dl < 16]
                                    for bi in valid:
                                        kc = 2 + b0 + bi + dl
                                        S.mm(O[:, bi * 128:(bi + 1) * 128], Vt[:, kc, vo:vo + 128],
                                             pfin[:, bi * 128:(bi + 1) * 128], start=False, stop=(last and bi == valid[-1]))
                                n_ = len(steps)
                                sps = [None] * n_
                                sps[0] = emit_S(steps[0])
                                for i_ in range(n_):
                                    if i_ + 1 < n_:
                                        sps[i_ + 1] = emit_S(steps[i_ + 1])
                                    emit_rest(steps[i_], sps[i_], i_ == 0, i_ == n_ - 1)
                                if m == 1:
                                    S.ts("dve", dsb[64:128, 0:Tn], O[64:128, 0:Tn],
                                         SINKE[64:128, l * 4 + hq:l * 4 + hq + 1], ALU.add)
                                    S.recip(rd[0:64, 0:Tn], dsb[64:128, 0:Tn])
                                else:
                                    S.recip(rd[0:64, 0:Tn], O[64:128, 0:Tn])
                                S.tt("dve", bst[p0:p1_, 0:Tn], O[0:64, 0:Tn], rd[0:64, 0:Tn], ALU.mult)
                            cidx = (0, 4, 6)[m] + qcnk
                            S.dma("pool", brT_d[s, :, cidx, t0:t0 + Tn], bst[:, 0:Tn])
            phase_end("p2")

        def p2b(s, l):
            with ExitStack() as ps:
                fT = sb([128, 2, NT], BF16, "fT", ps)
                gcs = sb([128, 18, 4, 128], BF16, "gcs", ps)
                Cc = [sb([128, 16, 512], BF16, "Cc%d" % i, ps) for i in range(2)]
                Sc = [sb([128, 16, 512], BF16, "Sc%d" % i, ps) for i in range(2)]
                ost = [sb([128, 512], BF16, "ost%d" % i, ps) for i in range(2)]
                S.dma("sp", fT.full(), fT_d[s, :, :, :])
                tcs = range(18) if l == 0 else range(2, 18)
                for tc in tcs:
                    pg = PSF[tc % 2]
                    for fc in range(2):
                        S.mm(pg[:, (fc * 2) * 128:(fc * 2 + 1) * 128], fT[:, fc, tc * 128:(tc + 1) * 128], c64bd)
                        S.mm(pg[:, (fc * 2 + 1) * 128:(fc * 2 + 2) * 128], fT[:, fc, tc * 128:(tc + 1) * 128], s64bd)
                    S.copy("act" if tc % 2 == 0 else "dve", gcs[:, tc, :, :], pg[:, 0:512].re("p (a c) -> p a c", c=128))
                n = 0
                jobs = [(256 + i * 512, 512, 2, 16, dftc, dfts) for i in range(4)]
                if l == 0:
                    jobs = [(0, 256, 0, 2, dftc_c, dfts_c)] + jobs
                for ji, (t0, Tn, tc0, ntc, mc, msn) in enumerate(jobs):
                    c_, s_ = Cc[ji % 2], Sc[ji % 2]
                    col0 = 0 if t0 == 0 else t0 - NCX
                    S.dma("sp", c_[:, 0:ntc, 0:Tn], mc[:, col0:col0 + Tn].rearrange("(tc p) n -> p tc n", p=128))
                    S.dma("sp", s_[:, 0:ntc, 0:Tn], msn[:, col0:col0 + Tn].rearrange("(tc p) n -> p tc n", p=128))
                    for fc in range(2):
                        po = PSF[2 + (n % 2)]
                        for k_ in range(ntc):
                            S.mm(po[:, 0:Tn], gcs[:, tc0 + k_, fc * 2, :], c_[:, k_, 0:Tn], start=(k_ == 0), stop=False)
                            S.mm(po[:, 0:Tn], gcs[:, tc0 + k_, fc * 2 + 1, :], s_[:, k_, 0:Tn], start=False, stop=(k_ == ntc - 1))
                        o_ = ost[n % 2]
                        n += 1
                        S.copy("act", o_[:, 0:Tn], po[:, 0:Tn])
                        S.dma("pool", brT_d[s, :, 2 + fc, t0:t0 + Tn], o_[:, 0:Tn])
            phase_end("p2b")

        def p3_weights(l, ws):
            wG = sb([128, KC, 4096], BF16, "wG", ws)
            wB = sb([128, KC, D], BF16, "wB", ws)
            wO = sb([128, KC, D], BF16, "wO", ws)
            wR = sb([128, KC, E], F32, "wR", ws)
            cast_load(wG, lambda c0, c1: w_inG[l, :, c0:c1].rearrange("(kc p) n -> p kc n", p=128), 4096)
            cast_load(wB, lambda c0, c1: w_brP[l, :, c0:c1].rearrange("(kc p) n -> p kc n", p=128), D)
            cast_load(wO, lambda c0, c1: w_out[l, :, c0:c1].rearrange("(kc p) n -> p kc n", p=128), D)
            S.dma("sp", wR.full(), w_router[l, :, :].rearrange("(kc p) n -> p kc n", p=128))
            return wG, wB, wO, wR

        def p3(s, l, wts):
            wG, wB, wO, wR = wts
            with ExitStack() as ps:
                TT = 512
                uTc = sb([128, KC, TT], BF16, "uTc3", ps)
                brc = sb([128, 8, TT], BF16, "brc", ps)
                hTc = sb([128, KC, TT], F32, "hTc3", ps)
                zT = sb([128, KC, TT], F32, "zT", ps)
                mrg = sb([128, KC, TT], BF16, "mrg", ps)
                u2f = hTc
                u2b = brc
                gate = [sb([128, TT], F32, "gate%d" % i, ps) for i in range(2)]
                tmpm = [sb([128, TT], F32, "tmpm%d" % i, ps) for i in range(2)]
                acc = sb([128, TT], F32, "acc", ps)
                zsq = [sb([128, TT], F32, "zsq%d" % i, ps) for i in range(2)]
                mean_sb = sb([128, TT], F32, "mean_sb", ps)
                var_sb = sb([128, TT], F32, "var_sb", ps)
                accs = sb([128, TT], F32, "accs", ps)
                accq = sb([128, TT], F32, "accq", ps)
                ex = sb([16, TT], F32, "ex", ps)
                rsum = sb([16, TT], F32, "rsum", ps)
                utok = [sb([128, D], BF16, "utok%d" % i, ps) for i in range(2)]
                chunks = CH512 if l == 0 else CH512[1:]
                ng = 0
                for (t0, Tn) in chunks:
                    j = 2 if t0 == 0 else s
                    S.dma("sp", uTc[:, :, 0:Tn], uT_d[s, :, :, t0:t0 + Tn])
                    S.dma("sp", brc[:, :, 0:Tn], brT_d[s, :, :, t0:t0 + Tn])
                    S.dma("sp", hTc[:, :, 0:Tn], hT_d[s, :, :, t0:t0 + Tn])
                    for dc in range(KC):
                        for i in range(4):
                            pg = PSF[ng % 2]
                            pb = PSF[2 + ng % 2]
                            g_ = gate[ng % 2]
                            tm = tmpm[ng % 2]
                            ng += 1
                            for kc in range(KC):
                                S.mm(pg[:, 0:Tn], wG[:, kc, i * D + dc * 128:i * D + (dc + 1) * 128], uTc[:, kc, 0:Tn],
                                     start=(kc == 0), stop=(kc == KC - 1))
                            S.act(g_[:, 0:Tn], pg[:, 0:Tn], AF.Sigmoid)
                            for hf in range(2):
                                S.mm(pb[:, 0:Tn], wB[:, i * 2 + hf, dc * 128:(dc + 1) * 128], brc[:, i * 2 + hf, 0:Tn],
                                     start=(hf == 0), stop=(hf == 1))
                            if i == 0:
                                S.tt("dve", acc[:, 0:Tn], g_[:, 0:Tn], pb[:, 0:Tn], ALU.mult)
                            else:
                                S.tt("dve", tm[:, 0:Tn], g_[:, 0:Tn], pb[:, 0:Tn], ALU.mult)
                                if i < 3:
                                    S.tt("pool", acc[:, 0:Tn], acc[:, 0:Tn], tm[:, 0:Tn], ALU.add)
                                else:
                                    S.tt("pool", mrg[:, dc, 0:Tn], acc[:, 0:Tn], tm[:, 0:Tn], ALU.add)
                    for dc in range(KC):
                        py = PSF[dc % 2]
                        for kc in range(KC):
                            S.mm(py[:, 0:Tn], wO[:, kc, dc * 128:(dc + 1) * 128], mrg[:, kc, 0:Tn],
                                 start=(kc == 0), stop=(kc == KC - 1))
                        S.stt(zT[:, dc, 0:Tn], py[:, 0:Tn], mod(l, 2, dc, j), hTc[:, dc, 0:Tn], ALU.mult, ALU.add)
                    layernorm(zT, Tn, l, 0, (zsq, mean_sb, var_sb, accs, accq))
                    S.dma("pool", hT_d[s, :, :, t0:t0 + Tn], zT[:, :, 0:Tn])
                    for kc in range(KC):
                        S.act(u2f[:, kc, 0:Tn], zT[:, kc, 0:Tn], AF.Identity, scale=mod(l, 4, kc, j), bias=mod(l, 3, kc, j))
                        S.copy("pool", u2b[:, kc, 0:Tn], u2f[:, kc, 0:Tn])
                    pl = PSF[2]
                    for kc in range(KC):
                        S.mm(pl[0:16, 0:Tn], wR[:, kc, :], u2f[:, kc, 0:Tn], start=(kc == 0), stop=(kc == KC - 1))
                    S.act(ex[:, 0:Tn], pl[0:16, 0:Tn], AF.Exp)
                    S.mm(PSF[3][0:16, 0:Tn], ones_f[0:16, 0:16], ex[:, 0:Tn])
                    S.recip(rsum[:, 0:Tn], PSF[3][0:16, 0:Tn])
                    S.tt("dve", ex[:, 0:Tn], ex[:, 0:Tn], rsum[:, 0:Tn], ALU.mult)
                    S.dma("pool", aff_d[s, :, t0:t0 + Tn], ex[:, 0:Tn])
                    for ti in range(Tn // 128):
                        pt = PSB[ti % 2]
                        ut = utok[ti % 2]
                        for kc in range(KC):
                            S.tr(pt[:, kc * 128:(kc + 1) * 128], u2b[:, kc, ti * 128:(ti + 1) * 128], ident_b)
                        S.copy("act", ut.full(), pt.full())
                        r0 = s * NT + t0 + ti * 128
                        S.dma("pool", u2tok_d[r0:r0 + 128, :], ut.full())
            phase_end("p3")

        IDXT = sb([128, 5, E], I32, "IDXT")
        SELW = sb([128, 5, E], F32, "SELW")

        def p4(l):
            with ExitStack() as ps:
                work = sb([48, NL], F32, "work", ps)
                workc = sb([48, NCX], F32, "workc", ps)
                vals = sb([48, 288], F32, "vals", ps)
                idxu = sb([48, 288], U32, "idxu", ps)
                idxf = sb([48, 288], F32, "idxf", ps)
                lo = sb([16, 2, 288], F32, "lo", ps)
                tmpi = sb([128, E], F32, "tmpi", ps)
                S.memset("dve", work.full(), 0.0)
                S.memset("dve", workc.full(), 0.0)
                S.memset("dve", vals.full(), 0.0)
                S.memset("dve", idxu.full(), 0)
                for s in range(2):
                    S.dma("sp", work[32 * s:32 * s + 16, :], aff_d[s, :, NCX:NT])
                    if l == 0:
                        S.dma("sp", workc[32 * s:32 * s + 16, :], aff_d[s, :, 0:NCX])
                jobs = [(work, 0, 32)] + ([(workc, 256, 4)] if l == 0 else [])
                for (wt, slot0, rounds) in jobs:
                    for r_ in range(rounds):
                        sl = slice(slot0 + r_ * 8, slot0 + r_ * 8 + 8)
                        mx, ix, wk = _ap(vals[:, sl]), _ap(idxu[:, sl]), _ap(wt.full())
                        S.op("dve", lambda e, mx=mx, wk=wk: e.max(out=mx, in_=wk), reads=[wt.full()], writes=[vals.full()])
                        S.op("dve", lambda e, mx=mx, ix=ix, wk=wk: e.max_index(out=ix, in_max=mx, in_values=wk),
                             reads=[wt.full(), vals.full()], writes=[idxu.full()])
                        S.op("dve", lambda e, mx=mx, wk=wk: e.match_replace(out=wk, in_to_replace=mx, in_values=wk, imm_value=-1.0),
                             reads=[wt.full(), vals.full()], writes=[wt.full()])
                S.copy("dve", idxf.full(), idxu.full())
                S.ts("dve", idxf[0:16, 0:256], idxf[0:16, 0:256], float(NCX), ALU.add)
                S.ts("dve", idxf[32:48, 0:256], idxf[32:48, 0:256], float(NT + NCX), ALU.add)
                S.ts("dve", idxf[32:48, 256:288], idxf[32:48, 256:288], float(NT), ALU.add)
                S.copy("dve", lo[:, 0, :], idxf[32:48, :])
                S.copy("dve", lo[:, 1, :], vals[32:48, :])
                srcs = [(idxf[0:16, :], vals[0:16, :]), (lo[:, 0, :], lo[:, 1, :])]
                n = 0
                for s in range(2):
                    si, sv = srcs[s]
                    for half in range(2):
                        st = s * 2 + half
                        pt = PSF[n % 2]
                        n += 1
                        S.tr(pt[:, 0:16], si[:, half * 128:(half + 1) * 128], ident_f[0:16, 0:16])
                        S.copy("dve", tmpi.full(), pt[:, 0:16])
                        S.copy("dve", IDXT[:, st, :], tmpi.full())
                        S.tr(pt[:, 16:32], sv[:, half * 128:(half + 1) * 128], ident_f[0:16, 0:16])
                        S.copy("dve", SELW[:, st, :], pt[:, 16:32])
                    if l == 0:
                        pt = PSF[n % 2]
                        n += 1
                        S.tr(pt[0:32, 0:16], si[:, 256:288], ident_f[0:16, 0:16])
                        S.copy("dve", tmpi[0:32, :], pt[0:32, 0:16])
                        S.copy("dve", IDXT[32 * s:32 * s + 32, 4, :], tmpi[0:32, :])
                        S.tr(pt[0:32, 16:32], sv[:, 256:288], ident_f[0:16, 0:16])
                        S.copy("dve", SELW[32 * s:32 * s + 32, 4, :], pt[0:32, 16:32])
            phase_end("p4")

        def p5_weights(l, ws):
            wg = [sb([128, KC, D], BF16, "wg%d" % i, ws) for i in range(2)]
            wu = [sb([128, KC, D], BF16, "wu%d" % i, ws) for i in range(2)]
            wd = [sb([128, KC, D], BF16, "wd%d" % i, ws) for i in range(2)]
            for wt, src in ((wg[0], w_gate), (wu[0], w_up), (wd[0], w_down)):
                S.dma("pool", wt.full(), src[l, 0, :, :].rearrange("(kc p) n -> p kc n", p=128))
            return wg, wu, wd

        def p5(l, wts):
            wg, wu, wd = wts
            with ExitStack() as ps:
                xs = [sb([128, 5, D], BF16, "xs%d" % i, ps) for i in range(2)]
                xsT = sb([128, KC, 640], BF16, "xsT", ps)
                actT = sb([128, KC, 640], BF16, "actT", ps)
                sa = [sb([128, 512], F32, "sa%d" % i, ps) for i in range(2)]
                ysb = [sb([128, D], F32, "ysb%d" % i, ps) for i in range(2)]
                zer = sb([128, 2, D], F32, "zer", ps)
                FFN = T(ffn_d, "ffn_d")
                U2 = T(u2tok_d, "u2tok_d")
                S.memset("dve", zer.full(), 0.0)
                for i in range(2 * NT // 256):
                    S.dma("sp", V(FFN, ffn_d[i * 256:(i + 1) * 256, :].rearrange("(a p) d -> p a d", p=128)), zer.full())
                sts = [(0, 128), (1, 128), (2, 128), (3, 128)] + ([(4, 64)] if l == 0 else [])
                nsl = 576 if l == 0 else 512
                cgs = [(0, 512)] + ([(512, 576)] if l == 0 else [])

                def loadw(e_):
                    b_ = e_ % 2
                    for wt, src in ((wg[b_], w_gate), (wu[b_], w_up), (wd[b_], w_down)):
                        S.dma("pool", wt.full(), src[l, e_, :, :].rearrange("(kc p) n -> p kc n", p=128))

                def gathers(e_):
                    x_ = xs[e_ % 2]
                    for (st, np_) in sts:
                        o_ = _ap(x_[0:np_, st, :])
                        ix = _ap(IDXT[0:np_, st, e_:e_ + 1])

                        def fn(eng, o_=o_, ix=ix):
                            return eng.indirect_dma_start(out=o_, out_offset=None, in_=u2tok_d[:, :],
                                                          in_offset=bass.IndirectOffsetOnAxis(ap=ix, axis=0))
                        S.dma("pool", x_[0:np_, st, :], V(U2, None), extra_reads=[IDXT.full()], fn=fn)
                gathers(0)
                ny = 0
                for e_ in range(E):
                    if e_ + 1 < E:
                        loadw(e_ + 1)
                        gathers(e_ + 1)
                    b_ = e_ % 2
                    x_ = xs[b_]
                    for (st, np_) in sts:
                        pt = PSB[st % 2]
                        for kc in range(KC):
                            S.tr(pt[:, kc * 128:kc * 128 + np_], x_[0:np_, st, kc * 128:(kc + 1) * 128], ident_b[0:np_, 0:np_])
                        S.copy("act" if st % 2 == 0 else "dve", xsT[:, :, st * 128:st * 128 + np_],
                               pt.full().re("p (k t) -> p k t", t=128)[:, :, 0:np_])
                    for fc in range(KC):
                        for (c0, c1) in cgs:
                            pa, pu = PSF[(fc % 2) * 2], PSF[(fc % 2) * 2 + 1]
                            w_ = c1 - c0
                            for kc in range(KC):
                                S.mm(pa[:, 0:w_], wg[b_][:, kc, fc * 128:(fc + 1) * 128], xsT[:, kc, c0:c1],
                                     start=(kc == 0), stop=(kc == KC - 1))
                            for kc in range(KC):
                                S.mm(pu[:, 0:w_], wu[b_][:, kc, fc * 128:(fc + 1) * 128], xsT[:, kc, c0:c1],
                                     start=(kc == 0), stop=(kc == KC - 1))
                            s_ = sa[fc % 2]
                            S.act(s_[:, 0:w_], pa[:, 0:w_], AF.Silu)
                            S.tt("dve", actT[:, fc, c0:c1], s_[:, 0:w_], pu[:, 0:w_], ALU.mult)
                    for (st, np_) in sts:
                        y_ = ysb[ny % 2]
                        ny += 1
                        for dh in range(2):
                            py = PSF[4 + dh]
                            for fc in range(KC):
                                S.mm(py[0:np_, :], actT[:, fc, st * 128:st * 128 + np_], wd[b_][:, fc, dh * 512:(dh + 1) * 512],
                                     start=(fc == 0), stop=(fc == KC - 1))
                            S.act(y_[0:np_, dh * 512:(dh + 1) * 512], py[0:np_, :], AF.Copy, scale=SELW[0:np_, st, e_:e_ + 1])
                        yi = _ap(y_[0:np_, :])
                        ix = _ap(IDXT[0:np_, st, e_:e_ + 1])

                        def fn(eng, yi=yi, ix=ix):
                            return eng.indirect_dma_start(out=ffn_d[:, :], out_offset=bass.IndirectOffsetOnAxis(ap=ix, axis=0),
                                                          in_=yi, in_offset=None, compute_op=ALU.add)
                        S.dma("pool", V(FFN, None), y_[0:np_, :], extra_reads=[IDXT.full(), V(FFN, None)], fn=fn)
            phase_end("p5")

        def p6(s, l):
            with ExitStack() as ps:
                ftok = sb([128, 4, D], F32, "ftok", ps)
                hTc = sb([128, KC, 512], F32, "hTc6", ps)
                zT = sb([128, KC, 512], F32, "zT6", ps)
                zsq = [sb([128, 512], F32, "zsq6%d" % i, ps) for i in range(2)]
                mean_sb = sb([128, 512], F32, "mean6", ps)
                var_sb = sb([128, 512], F32, "var6", ps)
                accs = sb([128, 512], F32, "accs6", ps)
                accq = sb([128, 512], F32, "accq6", ps)
                otok = [sb([128, D], F32, "otok%d" % i, ps) for i in range(2)]
                chunks = CH512 if l == 0 else CH512[1:]
                for (t0, Tn) in chunks:
                    j = 2 if t0 == 0 else s
                    nt = Tn // 128
                    r0 = s * NT + t0
                    S.dma("sp", ftok[:, 0:nt, :], ffn_d[r0:r0 + Tn, :].rearrange("(ti p) d -> p ti d", p=128))
                    S.dma("sp", hTc[:, :, 0:Tn], hT_d[s, :, :, t0:t0 + Tn])
                    for kc in range(KC):
                        pk = PSF[kc % 4]
                        for ti in range(nt):
                            S.tr(pk[:, ti * 128:(ti + 1) * 128], ftok[:, ti, kc * 128:(kc + 1) * 128], ident_f)
                        S.stt(zT[:, kc, 0:Tn], pk[:, 0:Tn], mod(l, 5, kc, j), hTc[:, kc, 0:Tn], ALU.mult, ALU.add)
                    layernorm(zT, Tn, l, 2, (zsq, mean_sb, var_sb, accs, accq))
                    if l < L - 1:
                        S.dma("pool", hT_d[s, :, :, t0:t0 + Tn], zT[:, :, 0:Tn])
                    else:
                        for ti in range(nt):
                            ot = otok[ti % 2]
                            for hf in range(2):
                                po = PSF[hf]
                                for k4 in range(4):
                                    kc = hf * 4 + k4
                                    S.tr(po[:, k4 * 128:(k4 + 1) * 128], zT[:, kc, ti * 128:(ti + 1) * 128], ident_f)
                                S.copy("act" if hf == 0 else "dve", ot[:, hf * 512:(hf + 1) * 512], po.full())
                            S.dma("pool", out[s, t0 - NCX + ti * 128:t0 - NCX + (ti + 1) * 128, :], ot.full())
            phase_end("p6")

        done = False
        for l in range(L):
            p1(l, (0,) if stop_after == "p1" else (0, 1))
            if stop_after == "p1":
                break
            for s in range(2):
                p2(s, l)
                with ExitStack() as ws:
                    wts = p3_weights(l, ws) if stop_after != "p2" else None
                    p2b(s, l)
                    if stop_after == "p2":
                        done = True
                        break
                    p3(s, l, wts)
                if stop_after == "p3":
                    done = True
                    break
            if done:
                break
            with ExitStack() as ws:
                wts5 = p5_weights(l, ws)
                p4(l)
                p5(l, wts5)
            if stop_after == "p5":
                break
            for s in range(2):
                p6(s, l)
            if stop_after == "l0":
                break
        S.barrier()
        S.emit()
        print("bass ops:", S.nops, {e: len(S.items[e]) for e in ENG})
        _CACHE["marks"] = marks
    return nc


_CACHE = {}


def _host_shared(inp):
    f = np.float32
    sh = dict(_consts())
    cols, vcols = _wina_cols()
    w_in = np.asarray(inp["w_in"], f)
    sh["w_inA"] = np.ascontiguousarray(w_in[:, :, cols])
    sh["w_inV"] = np.ascontiguousarray(w_in[:, :, vcols])
    sh["w_inG"] = np.ascontiguousarray(w_in[:, :, 1792:])
    sh["w_mod"] = np.asarray(inp["w_mod"], f)
    sh["b_modT"] = np.ascontiguousarray(np.asarray(inp["b_mod"], f).reshape(L, 48, 128).transpose(2, 0, 1))
    g = np.asarray(inp["qk_gain"], f)
    d_ = np.arange(128) % 64
    pd = np.array([_partner(x) for x in d_])
    gt = np.stack([g[:, 0, d_], g[:, 0, pd], g[:, 1, d_], g[:, 1, pd]], -1)
    sh["gainT"] = np.ascontiguousarray(gt.transpose(1, 0, 2))
    sk = np.asarray(inp["sink_logit"], f).reshape(1, L * 4)
    sh["sinkB"] = np.ascontiguousarray(np.broadcast_to(sk, (128, L * 4)))
    rpb = np.asarray(inp["na_rpb"], f)
    p = np.arange(128)
    kcol = (p % 64)[:, None, None]
    i_ = (p // 64)[:, None, None]
    m_ = np.arange(15)[None, :, None]
    c_ = np.arange(64)[None, None, :]
    a_ = np.clip(14 - m_ + i_, 0, 14) + 0 * c_
    oc = np.clip(kcol - c_ + 15, 0, 30) + 0 * m_
    tb = rpb[:, :, a_, oc]
    sh["tbs"] = np.ascontiguousarray(tb.transpose(0, 2, 1, 3, 4).reshape(L, 128, 4 * 15 * 64))
    wb = np.asarray(inp["w_branch"], f)
    perm = np.concatenate([np.arange(0, 64), np.arange(128, 192), np.arange(64, 128), np.arange(192, 256)])
    wbp = wb.copy()
    for i in (0, 2, 3):
        wbp[:, i] = wb[:, i][:, perm]
    sh["w_brP"] = np.ascontiguousarray(wbp.reshape(L, 1024, D))
    sh["w_out"] = np.asarray(inp["w_out"], f)
    ln = np.stack([inp["ln1_g"], inp["ln1_b"], inp["ln2_g"], inp["ln2_b"]], 1).astype(f)
    sh["lnT"] = np.ascontiguousarray(ln.reshape(L, 4, KC, 128).transpose(3, 0, 1, 2).reshape(128, L * 4 * KC))
    sh["w_router"] = np.asarray(inp["w_router"], f)
    sh["w_gate"] = np.asarray(inp["w_gate"], f)
    sh["w_up"] = np.asarray(inp["w_up"], f)
    sh["w_down"] = np.asarray(inp["w_down"], f)
    return sh


def _host_core(inp, core):
    f = np.float32
    b0 = core * 2
    x = np.asarray(inp["x"], f)
    ctx = np.asarray(inp["ctx"], f)
    c = np.asarray(inp["c"], f)
    cc = np.asarray(inp["c_ctx"], f)
    d = {}
    d["x2"] = np.ascontiguousarray(np.concatenate([ctx[b0:b0 + 2], x[b0:b0 + 2]], axis=1))
    cv = np.stack([c[b0], c[b0 + 1], cc, np.zeros_like(cc)], -1)
    d["cT"] = np.ascontiguousarray(cv.reshape(KC, 128, 4).transpose(1, 0, 2))
    return d


def kernel(**inputs):
    n = 8
    if "nc" not in _CACHE:
        _CACHE["nc"] = build_program(debug=False)
    nc = _CACHE["nc"]
    sh = _host_shared(inputs)
    in_maps = []
    for core in range(n):
        m = dict(sh)
        m.update(_host_core(inputs, core))
        in_maps.append(m)
    res = run_bass_kernel_spmd(nc, in_maps, core_ids=list(range(n)))
    _CACHE["last"] = res
    outs = [np.asarray(r["out"], np.float32) for r in res.results]
    return np.concatenate(outs, axis=0)
```

```python
import numpy as np
from contextlib import ExitStack
import ml_dtypes
import concourse.bass as bass
import concourse.mybir as mybir
from concourse.bass_utils import run_bass_kernel_spmd

F32 = mybir.dt.float32
BF16 = mybir.dt.bfloat16
U32 = mybir.dt.uint32
I32 = mybir.dt.int32
ALU = mybir.AluOpType
AF = mybir.ActivationFunctionType

L = 2
D = 1024
KC = 8
NT = 2304
NCX = 256
NL = 2048
E = 16
ALPHA = (2 * L) ** 0.25
LN_EPS_S = 1e-6 / (ALPHA * ALPHA)
RMS_EPS = 1e-6
NA_COLS = 11 * 128
ENG = ("pe", "act", "dve", "pool", "sp")


class T:
    __slots__ = ("h", "w", "r", "name")

    def __init__(self, h, name=""):
        self.h = h
        self.w = {}
        self.r = {}
        self.name = name

    def __getitem__(self, idx):
        return V(self, self.h[idx])

    def full(self):
        return V(self, self.h[:])


class V:
    __slots__ = ("t", "ap")

    def __init__(self, t, ap):
        self.t = t
        self.ap = ap

    def re(self, pat, **kw):
        return V(self.t, self.ap.rearrange(pat, **kw))

    def __getitem__(self, idx):
        return V(self.t, self.ap[idx])


def _ap(x):
    return x.ap if isinstance(x, V) else x


def _ts(xs):
    return [x.t for x in xs if isinstance(x, V)]


class Sched:
    def __init__(self, nc, es, n_dma_sems=(("sp", 12), ("pool", 10), ("act", 4))):
        self.nc = nc
        self.es = es
        self.items = {e: [] for e in ENG}
        self.sems = {}
        self.cnt = {}
        self.seen = {e: {} for e in ENG}
        for e in ENG:
            self._mk("c_" + e)
        self.dma_pool = {}
        self.dma_rr = {}
        for e, n in n_dma_sems:
            self.dma_pool[e] = [self._mk("d_%s%d" % (e, i)) for i in range(n)]
            self.dma_rr[e] = 0
        self.nops = 0

    def _mk(self, key):
        self.sems[key] = self.es.enter_context(self.nc.semaphore(key))
        self.cnt[key] = 0
        return key

    def _need(self, e, key, val):
        if val <= 0 or self.seen[e].get(key, 0) >= val:
            return
        self.seen[e][key] = val
        self.items[e].append(("wait", key, val))

    def _deps(self, e, reads, writes, is_pe=False, is_dma=False):
        own = "c_" + e if not is_dma else "__none__"
        for t in reads:
            for k, v in t.w.items():
                if is_pe and k == own:
                    continue
                self._need(e, k, v)
        for t in writes:
            for k, v in t.r.items():
                if k != own:
                    self._need(e, k, v)
            for k, v in t.w.items():
                if k != own:
                    self._need(e, k, v)

    def _mark(self, key, val, reads, writes):
        for t in reads:
            t.r[key] = max(t.r.get(key, 0), val)
        for t in writes:
            if t.r:
                t.r = {}
                t.w = {}
            t.w[key] = max(t.w.get(key, 0), val)

    def op(self, e, fn, reads=(), writes=()):
        reads = _ts(reads)
        writes = _ts(writes)
        self._deps(e, reads, writes, is_pe=(e == "pe"))
        key = "c_" + e
        self.cnt[key] += 1
        self.items[e].append(("op", fn, key, 1))
        self._mark(key, self.cnt[key], reads, writes)
        self.nops += 1

    def dma(self, e, out, in_, extra_reads=(), fn=None, **kw):
        reads = _ts([in_] + list(extra_reads))
        writes = _ts([out])
        pool = self.dma_pool[e]
        key = pool[self.dma_rr[e] % len(pool)]
        self.dma_rr[e] += 1
        self._need(e, key, self.cnt[key])
        self._deps(e, reads, writes, is_dma=True)
        self.cnt[key] += 16
        if fn is None:
            o, i = _ap(out), _ap(in_)

            def fn(eng, o=o, i=i, kw=kw):
                return eng.dma_start(out=o, in_=i, **kw)
        self.items[e].append(("op", fn, key, 16))
        self._mark(key, self.cnt[key], reads, writes)
        self.nops += 1

    def barrier(self):
        for e in ENG:
            for k, v in self.cnt.items():
                if k != "c_" + e:
                    self._need(e, k, v)

    def emit(self):
        nc = self.nc
        with nc.Block() as block:
            def run(e):
                def body(eng):
                    for it in self.items[e]:
                        if it[0] == "wait":
                            eng.wait_ge(self.sems[it[1]], it[2])
                        else:
                            it[1](eng).then_inc(self.sems[it[2]], it[3])
                return body
            block.tensor(run("pe"))
            block.scalar(run("act"))
            block.vector(run("dve"))
            block.gpsimd(run("pool"))
            block.sync(run("sp"))

    def mm(self, out, lhsT, rhs, start=True, stop=True):
        o, a, b = _ap(out), _ap(lhsT), _ap(rhs)
        self.op("pe", lambda e: e.matmul(o, a, b, start=start, stop=stop), reads=[lhsT, rhs], writes=[out])

    def tr(self, out, in_, ident):
        o, a, b = _ap(out), _ap(in_), _ap(ident)
        self.op("pe", lambda e: e.transpose(o, a, b), reads=[in_, ident], writes=[out])

    def act(self, out, in_, func, scale=1.0, bias=0.0):
        o, i = _ap(out), _ap(in_)
        sc, bi = _ap(scale), _ap(bias)
        self.op("act", lambda e: e.activation(out=o, in_=i, func=func, bias=bi, scale=sc),
                reads=[in_, scale, bias], writes=[out])

    def tt(self, eng, out, a, b, op):
        o, x, y = _ap(out), _ap(a), _ap(b)
        self.op(eng, lambda e: e.tensor_tensor(o, x, y, op), reads=[a, b], writes=[out])

    def ts(self, eng, out, a, s1, op0, s2=None, op1=None):
        o, x, p1, p2 = _ap(out), _ap(a), _ap(s1), _ap(s2)
        if op1 is None:
            self.op(eng, lambda e: e.tensor_scalar(o, x, p1, None, op0), reads=[a, s1], writes=[out])
        else:
            self.op(eng, lambda e: e.tensor_scalar(o, x, p1, p2, op0, op1), reads=[a, s1, s2], writes=[out])

    def stt(self, out, a, s, b, op0, op1):
        o, x, p, y = _ap(out), _ap(a), _ap(s), _ap(b)
        self.op("dve", lambda e: e.scalar_tensor_tensor(o, x, p, y, op0, op1), reads=[a, s, b], writes=[out])

    def copy(self, eng, out, in_):
        o, i = _ap(out), _ap(in_)
        if eng == "act":
            self.op("act", lambda e: e.activation(out=o, in_=i, func=AF.Copy), reads=[in_], writes=[out])
        else:
            self.op(eng, lambda e: e.tensor_copy(o, i), reads=[in_], writes=[out])

    def recip(self, out, in_):
        o, i = _ap(out), _ap(in_)
        self.op("dve", lambda e: e.reciprocal(o, i), reads=[in_], writes=[out])

    def memset(self, eng, out, val):
        o = _ap(out)
        self.op(eng, lambda e: e.memset(o, val), writes=[out])


def _partner(d):
    return d + 16 if (d % 32) < 16 else d - 16


_ORD = list(range(0, 16)) + list(range(32, 48)) + list(range(16, 32)) + list(range(48, 64))


def _rope_tables():
    cos = np.ones((128, NT), np.float32)
    sin = np.zeros((128, NT), np.float32)
    t = np.arange(NL)
    inv = (np.float32(10000.0) ** (-np.arange(0, 32, 2, dtype=np.float32) / np.float32(32))).astype(np.float32)
    for p in range(128):
        d = _ORD[p % 64]
        pos = (t // 64) if d < 32 else (t % 64)
        ang = pos.astype(np.float32) * inv[d % 16]
        cos[p, NCX:] = np.cos(ang).astype(np.float32)
        sgn = -1.0 if (d % 32) < 16 else 1.0
        sin[p ^ 32, NCX:] = sgn * np.sin(ang).astype(np.float32)
    return cos, sin


def _na_masks():
    rows, W, kh, kw = 32, 64, 8, 16
    t = np.arange(NL)
    r, c = t // W, t % W
    r0 = np.clip(r - kh // 2, 0, rows - kh)
    c0 = np.clip(c - kw // 2, 0, W - kw)
    k = np.arange(NL)
    kr, kcol = k // W, k % W
    valid = ((kr[None, :] >= r0[:, None]) & (kr[None, :] < r0[:, None] + kh) &
             (kcol[None, :] >= c0[:, None]) & (kcol[None, :] < c0[:, None] + kw))
    full = np.zeros((16, 7, 128, 128), np.float32)
    for b in range(16):
        for dl in range(-3, 4):
            kci = b + dl
            if 0 <= kci < 16:
                full[b, dl + 3] = valid[b * 128:(b + 1) * 128, kci * 128:(kci + 1) * 128].T
    types = [0, 1] + [2] * 12 + [3, 4]
    rep = {0: 0, 1: 1, 2: 5, 3: 14, 4: 15}
    for b in range(16):
        assert np.array_equal(full[b], full[rep[types[b]]]), b
    namc = np.zeros((3, 7, 128, 512), np.float32)
    for qt, b0 in enumerate((0, 4, 12)):
        for bi in range(4):
            namc[qt, :, :, bi * 128:(bi + 1) * 128] = full[b0 + bi]
    for b0 in (4, 8):
        for bi in range(4):
            assert np.array_equal(full[b0 + bi], full[5])
    qdl = [[dl for dl in range(-3, 4) if namc[qt, dl + 3].any()] for qt in range(3)]
    return namc, qdl


def _dft(n):
    t = np.arange(n, dtype=np.int64)
    m = (t[:, None] * t[None, :]) % n
    ang = 2.0 * np.pi * m.astype(np.float64) / n
    return np.cos(ang), np.sin(ang)


_NA_MASKC, _NA_QDELTAS = _na_masks()


def _consts():
    c = {}
    cos, sin = _rope_tables()
    c["rope"] = np.stack([cos, sin], 1).copy()
    ident = np.eye(128, dtype=np.float32)
    onesbd = np.zeros((128, 128), np.float32)
    onesbd[:64, :64] = 1.0
    onesbd[64:, 64:] = 1.0
    c64, s64 = _dft(64)
    cbd = np.zeros((128, 128), np.float64)
    sbd = np.zeros((128, 128), np.float64)
    for g in range(2):
        cbd[g * 64:(g + 1) * 64, g * 64:(g + 1) * 64] = c64 / 8.0
        sbd[g * 64:(g + 1) * 64, g * 64:(g + 1) * 64] = s64 / 8.0
    win = np.zeros((3, 128, 128), np.float32)
    i = np.arange(128)[:, None]
    j = np.arange(128)[None, :]
    win[0] = (j <= i)
    win[1] = 1.0
    win[2] = (i <= j)
    permm = np.zeros((128, 128), np.float32)
    for m_ in range(128):
        permm[(m_ // 64) * 64 + _partner(m_ % 64), m_] = 1.0
    cb = np.concatenate([ident, onesbd, cbd.astype(np.float32), sbd.astype(np.float32), permm], axis=1)
    c["cbf"] = cb.astype(ml_dtypes.bfloat16)
    winc = np.zeros((3, 3, 128, 512), np.float32)
    for qt, b0 in enumerate((0, 4, 12)):
        for bi in range(4):
            for dl in (-1, 0, 1):
                if 0 <= b0 + bi + dl < 16:
                    winc[qt, dl + 1, :, bi * 128:(bi + 1) * 128] = win[dl + 1]
    mk = np.concatenate([winc[qt, d_] for qt in range(3) for d_ in range(3)] +
                        [_NA_MASKC[qt, d_] for qt in range(3) for d_ in range(7)], axis=1)
    c["maskc"] = mk.astype(ml_dtypes.bfloat16)
    cf = np.concatenate([ident, np.full((128, 128), 1.0 / D, np.float32), np.ones((128, 128), np.float32)], axis=1)
    c["cf32"] = cf.astype(np.float32)
    cs, ss = _dft(NL)
    c["dftc"] = (cs / np.sqrt(NL)).astype(ml_dtypes.bfloat16)
    c["dfts"] = (-ss / np.sqrt(NL)).astype(ml_dtypes.bfloat16)
    cs, ss = _dft(NCX)
    c["dftc_c"] = (cs / np.sqrt(NCX)).astype(ml_dtypes.bfloat16)
    c["dfts_c"] = (-ss / np.sqrt(NCX)).astype(ml_dtypes.bfloat16)
    return c


def _wina_cols():
    offs = {"qA": 0, "kA": 256, "vA": 384, "f": 512, "qC": 768, "kC": 1024, "vC": 1152,
            "qD": 1280, "kD": 1536, "vD": 1664}

    def qch(base, pair):
        return [base + hq * 64 + _ORD[p] for hq in pair for p in range(64)]
    cols = []
    for mname in ("A", "C", "D"):
        qb, kb = offs["q" + mname], offs["k" + mname]
        cols += qch(qb, (0, 2)) + qch(qb, (1, 3)) + qch(kb, (0, 1))
    cols += list(range(offs["f"], offs["f"] + 256))
    vcols = list(range(384, 512)) + list(range(1152, 1280)) + list(range(1664, 1792))
    return np.array(cols), np.array(vcols)


def build_program(debug=False, stop_after=None):
    nc = bass.Bass("TRN2", target_bir_lowering=False)

    def din(name, shape, dt=F32):
        return nc.dram_tensor(name, list(shape), dt, kind="ExternalInput").ap()

    def dscr(name, shape, dt):
        return nc.dram_tensor(name, list(shape), dt, kind=("ExternalOutput" if debug else "Internal")).ap()

    x2 = din("x2", [2, NT, D])
    cT = din("cT", [128, KC, 4])
    w_mod = din("w_mod", [L, D, 6 * D])
    b_modT = din("b_modT", [128, L, 48])
    w_inA = din("w_inA", [L, D, NA_COLS])
    w_inV = din("w_inV", [L, D, 384])
    w_inG = din("w_inG", [L, D, 4096])
    gainT = din("gainT", [128, L, 4])
    sinkB = din("sinkB", [128, L * 4])
    tbs = din("tbs", [L, 128, 4 * 15 * 64])
    rope = din("rope", [128, 2, NT])
    cbf = din("cbf", [128, 5 * 128], BF16)
    maskc = din("maskc", [128, 30 * 512], BF16)
    cf32 = din("cf32", [128, 384])
    dftc = din("dftc", [NL, NL], BF16)
    dfts = din("dfts", [NL, NL], BF16)
    dftc_c = din("dftc_c", [NCX, NCX], BF16)
    dfts_c = din("dfts_c", [NCX, NCX], BF16)
    w_brP = din("w_brP", [L, 1024, D])
    w_out = din("w_out", [L, D, D])
    lnT = din("lnT", [128, L * 4 * KC])
    w_router = din("w_router", [L, D, E])
    w_gate = din("w_gate", [L, E, D, D])
    w_up = din("w_up", [L, E, D, D])
    w_down = din("w_down", [L, E, D, D])
    out = nc.dram_tensor("out", [2, NL, D], F32, kind="ExternalOutput").ap()

    hT_d = dscr("hT_d", [2, 128, KC, NT], F32)
    uT_d = dscr("uT_d", [2, 128, KC, NT], BF16)
    qT_d = dscr("qT_d", [2, 128, 6, NT], BF16)
    kT_d = dscr("kT_d", [2, 128, 3, NT], BF16)
    v_d = dscr("v_d", [2, 128, 18, 768], BF16)
    fT_d = dscr("fT_d", [2, 128, 2, NT], BF16)
    brT_d = dscr("brT_d", [2, 128, 8, NT], BF16)
    u2tok_d = dscr("u2tok_d", [2 * NT, D], BF16)
    ffn_d = dscr("ffn_d", [2 * NT, D], F32)
    aff_d = dscr("aff_d", [2, 16, NT], F32)

    CH512 = [(0, 256)] + [(256 + i * 512, 512) for i in range(4)]
    CH256 = [(i * 256, 256) for i in range(9)]

    with ExitStack() as es:
        S = Sched(nc, es)

        uid = [0]

        def sb(shape, dt, name, stack=es):
            uid[0] += 1
            nm = "%s_%d" % (name, uid[0])
            return T(stack.enter_context(nc.sbuf_tensor(nm, list(shape), dt)), nm)

        PSF = [T(es.enter_context(nc.psum_tensor("psf%d" % i, [128, 512], F32)), "psf%d" % i) for i in range(6)]
        PSB = [T(es.enter_context(nc.psum_tensor("psb%d" % i, [128, 1024], BF16)), "psb%d" % i) for i in range(2)]

        CB = sb([128, 5 * 128], BF16, "CB")
        CF = sb([128, 384], F32, "CF")
        MOD = sb([128, L, 48, 4], F32, "MOD")
        LNT = sb([128, L * 4 * KC], F32, "LNT")
        GAIN = sb([128, L, 4], F32, "GAIN")
        SINKE = sb([128, L * 4], F32, "SINKE")
        S.dma("sp", CB.full(), cbf[:, :])
        S.dma("sp", CF.full(), cf32[:, :])
        S.dma("sp", LNT.full(), lnT[:, :])
        S.dma("sp", GAIN.full(), gainT[:, :, :])
        S.dma("sp", SINKE.full(), sinkB[:, :])
        S.act(SINKE.full(), SINKE.full(), AF.Exp)
        ident_b = CB[:, 0:128]
        onesbd = CB[:, 128:256]
        c64bd = CB[:, 256:384]
        s64bd = CB[:, 384:512]
        permm = CB[:, 512:640]

        ident_f = CF[:, 0:128]
        ones_mean = CF[:, 128:256]
        ones_f = CF[:, 256:384]

        def lnv(l, which, kc):
            i_ = (l * 4 + which) * KC + kc
            return LNT[:, i_:i_ + 1]

        def mod(l, kind, kc, j):
            return MOD[:, l, kind * 8 + kc, j:j + 1]

        def cast_load(dst, src_ap_fn, ncols, eng="pool", step=1024):
            for c0 in range(0, ncols, step):
                c1 = min(ncols, c0 + step)
                S.dma(eng, dst[:, :, c0:c1], src_ap_fn(c0, c1))

        marks = []

        def phase_end(name="?"):
            S.barrier()
            marks.append((name, S.cnt["c_pe"]))

        with ExitStack() as ps:
            cTs = sb([128, KC, 4], F32, "cTs", ps)
            bm = sb([128, L, 48], F32, "bm", ps)
            wm = [sb([128, KC, 768], F32, "wm%d" % i, ps) for i in range(2)]
            S.dma("sp", cTs.full(), cT[:, :, :])
            S.dma("sp", bm.full(), b_modT[:, :, :])
            S.act(cTs.full(), cTs.full(), AF.Silu)
            n = 0
            for l in range(L):
                pm = PSF[l]
                for blk in range(8):
                    w = wm[n % 2]
                    n += 1
                    S.dma("sp", w.full(), w_mod[l, :, blk * 768:(blk + 1) * 768].rearrange("(kc p) n -> p kc n", p=128))
                    for cc in range(6):
                        ccg = blk * 6 + cc
                        for kc in range(KC):
                            S.mm(pm[:, ccg * 4:ccg * 4 + 4], w[:, kc, cc * 128:(cc + 1) * 128], cTs[:, kc, :],
                                 start=(kc == 0), stop=(kc == KC - 1))
                for ccg in range(48):
                    S.ts("dve", MOD[:, l, ccg, :], pm[:, ccg * 4:ccg * 4 + 4], bm[:, l, ccg:ccg + 1], ALU.add)
                for kind in (1, 4):
                    S.ts("dve", MOD[:, l, kind * 8:(kind + 1) * 8, :], MOD[:, l, kind * 8:(kind + 1) * 8, :], 1.0, ALU.add)
                for kind in (2, 5):
                    S.ts("dve", MOD[:, l, kind * 8:(kind + 1) * 8, :], MOD[:, l, kind * 8:(kind + 1) * 8, :],
                         1.0 / ALPHA, ALU.mult)
            phase_end("p0")

        with ExitStack() as ps:
            xin = [sb([128, 4, D], F32, "xin%d" % i, ps) for i in range(2)]
            hst = [sb([128, KC, 512], F32, "hst%d" % i, ps) for i in range(2)]
            n = 0
            for s in range(2):
                for (t0, Tn) in CH512:
                    nt = Tn // 128
                    xi, hs = xin[n % 2], hst[n % 2]
                    n += 1
                    S.dma("sp", xi[:, 0:nt, :], x2[s, t0:t0 + Tn, :].rearrange("(ti p) d -> p ti d", p=128))
                    for kc in range(KC):
                        pt = PSF[kc % 4]
                        for ti in range(nt):
                            S.tr(pt[:, ti * 128:(ti + 1) * 128], xi[:, ti, kc * 128:(kc + 1) * 128], ident_f)
                        S.copy("act" if kc % 2 == 0 else "dve", hs[:, kc, 0:Tn], pt[:, 0:Tn])
                    S.dma("pool", hT_d[s, :, :, t0:t0 + Tn], hs[:, :, 0:Tn])
            phase_end("pin")

        if stop_after == "pin":
            S.barrier()
            S.emit()
            return nc

        def layernorm(zT, Tn, l, which_g, tmp):
            zsq, mean_sb, var_sb, accs, accq = tmp
            pm, pq = PSF[4], PSF[5]
            for kc in range(KC):
                q = zsq[kc % 2]
                S.act(q[:, 0:Tn], zT[:, kc, 0:Tn], AF.Square)
                if kc == 1:
                    S.tt("pool", accs[:, 0:Tn], zT[:, 0, 0:Tn], zT[:, 1, 0:Tn], ALU.add)
                    S.tt("dve", accq[:, 0:Tn], zsq[0][:, 0:Tn], zsq[1][:, 0:Tn], ALU.add)
                elif kc > 1:
                    S.tt("pool", accs[:, 0:Tn], accs[:, 0:Tn], zT[:, kc, 0:Tn], ALU.add)
                    S.tt("dve", accq[:, 0:Tn], accq[:, 0:Tn], q[:, 0:Tn], ALU.add)
            S.mm(pm[:, 0:Tn], ones_mean, accs[:, 0:Tn])
            S.mm(pq[:, 0:Tn], ones_mean, accq[:, 0:Tn])
            S.copy("act", mean_sb[:, 0:Tn], pm[:, 0:Tn])
            S.tt("dve", var_sb[:, 0:Tn], mean_sb[:, 0:Tn], mean_sb[:, 0:Tn], ALU.mult)
            S.tt("dve", var_sb[:, 0:Tn], pq[:, 0:Tn], var_sb[:, 0:Tn], ALU.subtract)
            S.act(var_sb[:, 0:Tn], var_sb[:, 0:Tn], AF.Sqrt, bias=EPSV[:, 0:1])
            S.recip(var_sb[:, 0:Tn], var_sb[:, 0:Tn])
            for kc in range(KC):
                S.tt("pool", zT[:, kc, 0:Tn], zT[:, kc, 0:Tn], mean_sb[:, 0:Tn], ALU.subtract)
                S.tt("dve", zT[:, kc, 0:Tn], zT[:, kc, 0:Tn], var_sb[:, 0:Tn], ALU.mult)
                S.act(zT[:, kc, 0:Tn], zT[:, kc, 0:Tn], AF.Identity, scale=lnv(l, which_g, kc), bias=lnv(l, which_g + 1, kc))

        EPSV = sb([128, 2], F32, "EPSV")
        S.memset("dve", EPSV[:, 0:1], LN_EPS_S)
        S.memset("dve", EPSV[:, 1:2], RMS_EPS)

        def p1(l, samples):
            with ExitStack() as ps:
                wA = sb([128, KC, NA_COLS], BF16, "wA", ps)
                wV = sb([128, KC, 384], BF16, "wV", ps)
                ROPE = sb([128, 2, NT], F32, "ROPE", ps)
                hTc = [sb([128, KC, 512], F32, "hTc%d" % i, ps) for i in range(2)]
                uTc = [sb([128, KC, 512], BF16, "uTc%d" % i, ps) for i in range(2)]
                stg = [sb([128, 512], BF16, "stg%d" % i, ps) for i in range(4)]
                sq = sb([128, 512], BF16, "sq", ps)
                qb_ = sb([128, 512], BF16, "qb_", ps)
                rs = sb([128, 512], F32, "rs", ps)
                t1 = sb([128, 512], F32, "t1", ps)
                t2 = sb([128, 512], F32, "t2", ps)
                vst = [sb([128, 4, 6, 128], BF16, "vst%d" % i, ps) for i in range(2)]
                cast_load(wA, lambda c0, c1: w_inA[l, :, c0:c1].rearrange("(kc p) n -> p kc n", p=128), NA_COLS)
                cast_load(wV, lambda c0, c1: w_inV[l, :, c0:c1].rearrange("(kc p) n -> p kc n", p=128), 384)
                S.dma("sp", ROPE.full(), rope[:, :, :])
                for i in range(2):
                    S.memset("pool", vst[i].full(), 1.0)
                nst = 0
                for s in samples:
                    for ci, (t0, Tn) in enumerate(CH512):
                        j = 2 if t0 == 0 else s
                        h, u = hTc[ci % 2], uTc[ci % 2]
                        S.dma("sp", h[:, :, 0:Tn], hT_d[s, :, :, t0:t0 + Tn])
                        for kc in range(KC):
                            S.act(u[:, kc, 0:Tn], h[:, kc, 0:Tn], AF.Identity, scale=mod(l, 1, kc, j), bias=mod(l, 0, kc, j))
                        S.dma("pool", uT_d[s, :, :, t0:t0 + Tn], u[:, :, 0:Tn])

                        def proj(ps_t, cc):
                            for kc in range(KC):
                                S.mm(ps_t[:, 0:Tn], wA[:, kc, cc * 128:(cc + 1) * 128], u[:, kc, 0:Tn],
                                     start=(kc == 0), stop=(kc == KC - 1))

                        def store(dst_ap, v):
                            S.dma("pool", dst_ap, v)

                        plain = [(6, qT_d[s, :, 4, t0:t0 + Tn]), (7, qT_d[s, :, 5, t0:t0 + Tn]),
                                 (8, kT_d[s, :, 2, t0:t0 + Tn]), (9, fT_d[s, :, 0, t0:t0 + Tn]),
                                 (10, fT_d[s, :, 1, t0:t0 + Tn])]
                        for n_, (cc, dst) in enumerate(plain):
                            pt = PSF[n_ % 2]
                            proj(pt, cc)
                            st = stg[nst % 4]
                            nst += 1
                            S.copy("act", st[:, 0:Tn], pt[:, 0:Tn])
                            store(dst, st[:, 0:Tn])
                        roped = [(0, 0, True, qT_d[s, :, 0, t0:t0 + Tn]), (1, 0, True, qT_d[s, :, 1, t0:t0 + Tn]),
                                 (2, 2, True, kT_d[s, :, 0, t0:t0 + Tn]),
                                 (3, None, False, qT_d[s, :, 2, t0:t0 + Tn]),
                                 (4, None, False, qT_d[s, :, 3, t0:t0 + Tn]),
                                 (5, None, False, kT_d[s, :, 1, t0:t0 + Tn])]
                        for n_, (cb_, g0, norm, dst) in enumerate(roped):
                            pa = PSF[n_ % 4]
                            proj(pa, cb_)
                            cosv = ROPE[:, 0, t0:t0 + Tn]
                            st = stg[nst % 4]
                            nst += 1
                            if norm:
                                S.act(sq[:, 0:Tn], pa[:, 0:Tn], AF.Square)
                                S.mm(PSF[4][:, 0:Tn], onesbd, sq[:, 0:Tn])
                                S.act(rs[:, 0:Tn], PSF[4][:, 0:Tn], AF.Sqrt, scale=1.0 / 64.0, bias=EPSV[:, 1:2])
                                S.recip(rs[:, 0:Tn], rs[:, 0:Tn])
                                S.stt(t1[:, 0:Tn], pa[:, 0:Tn], GAIN[:, l, g0:g0 + 1], cosv, ALU.mult, ALU.mult)
                                for B_ in (0, 32, 64, 96):
                                    Bp = B_ ^ 32
                                    S.stt(t2[B_:B_ + 32, 0:Tn], pa[Bp:Bp + 32, 0:Tn], GAIN[Bp:Bp + 32, l, g0:g0 + 1],
                                          ROPE[Bp:Bp + 32, 1, t0:t0 + Tn], ALU.mult, ALU.mult)
                                S.tt("pool", t1[:, 0:Tn], t1[:, 0:Tn], t2[:, 0:Tn], ALU.add)
                                S.tt("dve", st[:, 0:Tn], t1[:, 0:Tn], rs[:, 0:Tn], ALU.mult)
                            else:
                                S.tt("dve", t1[:, 0:Tn], pa[:, 0:Tn], cosv, ALU.mult)
                                for B_ in (0, 32, 64, 96):
                                    Bp = B_ ^ 32
                                    S.tt("dve", t2[B_:B_ + 32, 0:Tn], pa[Bp:Bp + 32, 0:Tn],
                                         ROPE[Bp:Bp + 32, 1, t0:t0 + Tn], ALU.mult)
                                S.tt("pool", st[:, 0:Tn], t1[:, 0:Tn], t2[:, 0:Tn], ALU.add)
                            store(dst, st[:, 0:Tn])
                        vs = vst[ci % 2]
                        nt = Tn // 128
                        for ti in range(nt):
                            pv = PSF[5]
                            for kc in range(KC):
                                S.mm(pv[:, 0:384], u[:, kc, ti * 128:(ti + 1) * 128], wV[:, kc, :],
                                     start=(kc == 0), stop=(kc == KC - 1))
                            S.copy("act", vs[:, ti, :, 0:64], pv[:, 0:384].re("p (m d) -> p m d", d=64))
                        S.dma("pool", v_d[s, :, t0 // 128:t0 // 128 + nt, :].rearrange("p t (m d) -> p t m d", d=128),
                              vs[:, 0:nt, :, :])
            phase_end("p1")

        def p2(s, l):
            with ExitStack() as ps:
                kT = sb([128, 3, NT], BF16, "kT", ps)
                Vt = sb([128, 18, 768], BF16, "Vt", ps)
                TB = sb([128, 4 * 15 * 64], F32, "TB", ps)
                MK = sb([128, 30 * 512], BF16, "MK", ps)
                qcb = [sb([128, 6, 512], BF16, "qc%d" % i, ps) for i in range(2)]
                Pb = [sb([128, 512], BF16, "Pb%d" % i, ps) for i in range(3)]
                Pe = [sb([128, 512], BF16, "Pe%d" % i, ps) for i in range(3)]
                Pm = [sb([128, 512], BF16, "Pm%d" % i, ps) for i in range(2)]
                sbias = [sb([128, 512], F32, "sbias%d" % i, ps) for i in range(2)]
                rd = sb([128, 512], F32, "rd", ps)
                dsb = sb([128, 512], F32, "dsb", ps)
                brst = [sb([128, 512], BF16, "brst%d" % i, ps) for i in range(2)]
                S.dma("sp", kT.full(), kT_d[s, :, :, :])
                S.dma("sp", Vt.full(), v_d[s, :, :, :])
                S.dma("sp", TB.full(), tbs[l, :, :])
                S.dma("sp", MK.full(), maskc[:, :])
                TBv = TB.full().re("p (h m c) -> p h m c", h=4, m=15)

                def winc(qt, dl):
                    i_ = qt * 3 + dl + 1
                    return MK[:, i_ * 512:(i_ + 1) * 512]

                def namc(qt, dl):
                    i_ = 9 + qt * 7 + dl + 3
                    return MK[:, i_ * 512:(i_ + 1) * 512]
                qch = ([(0, 256, True)] if l == 0 else []) + [(256 + i * 512, 512, False) for i in range(4)]
                cnt = {}

                def nxt(k, lst):
                    c_ = cnt.get(k, 0)
                    cnt[k] = c_ + 1
                    return lst[c_ % len(lst)]
                for qi, (t0, Tn, is_ctx) in enumerate(qch):
                    qc = qcb[qi % 2]
                    S.dma("sp", qc[:, :, 0:Tn], qT_d[s, :, :, t0:t0 + Tn])
                    b0 = (t0 - NCX) // 128
                    qt = 0 if b0 == 0 else (2 if b0 == 12 else 1)
                    for m in range(3):
                        for qcnk in range(2):
                            bst = nxt("b", brst)
                            for ph in range(2):
                                hq = qcnk + 2 * ph
                                O = nxt("o", [PSF[3], PSF[4]])
                                p0, p1_ = ph * 64, (ph + 1) * 64
                                qv = qc[p0:p1_, m * 2 + qcnk, 0:Tn]
                                vo = (m * 2 + ph) * 128
                                if is_ctx:
                                    steps = [("g", 0), ("g", 1)]
                                elif m == 0:
                                    steps = [("g", kc) for kc in range(18)]
                                else:
                                    dls = [-1, 0, 1] if m == 1 else _NA_QDELTAS[qt]
                                    steps = [("g", 0), ("g", 1)] + [("b", dl) for dl in dls]

                                def emit_S(st):
                                    Sp = nxt("s", [PSF[0], PSF[1], PSF[2]])
                                    if st[0] == "g":
                                        kc = st[1]
                                        S.mm(Sp[:, 0:Tn], kT[p0:p1_, m, kc * 128:(kc + 1) * 128], qv)
                                    else:
                                        for bi in range(4):
                                            kc = 2 + min(15, max(0, b0 + bi + st[1]))
                                            S.mm(Sp[:, bi * 128:(bi + 1) * 128], kT[p0:p1_, m, kc * 128:(kc + 1) * 128],
                                                 qc[p0:p1_, m * 2 + qcnk, bi * 128:(bi + 1) * 128])
                                    return Sp

                                def emit_rest(st, Sp, first, last):
                                    if st[0] == "g":
                                        kc = st[1]
                                        P = nxt("p", Pb)
                                        S.act(P[:, 0:Tn], Sp[:, 0:Tn], AF.Exp, scale=0.125)
                                        S.mm(O[:, 0:Tn], Vt[:, kc, vo:vo + 128], P[:, 0:Tn], start=first, stop=last)
                                        return
                                    dl = st[1]
                                    pe_ = nxt("e", Pe)
                                    if m == 2:
                                        sbv = nxt("e2", sbias)
                                        m0 = 7 - 2 * dl
                                        for bi in range(4):
                                            S.stt(sbv[:, bi * 128:(bi + 1) * 128].re("p (j c) -> p j c", c=64),
                                                  Sp[:, bi * 128:(bi + 1) * 128].re("p (j c) -> p j c", c=64), 0.125,
                                                  TBv[:, hq, m0:m0 + 2, :], ALU.mult, ALU.add)
                                        S.act(pe_.full(), sbv.full(), AF.Exp)
                                        pfin = nxt("m", Pm)
                                        S.tt("pool", pfin.full(), pe_.full(), namc(qt, dl), ALU.mult)
                                    else:
                                        S.act(pe_.full(), Sp.full(), AF.Exp, scale=0.125)
                                        if dl == 0:
                                            pfin = pe_
                                        else:
                                            pfin = nxt("m", Pm)
                                            S.tt("pool", pfin.full(), pe_.full(), winc(qt, dl), ALU.mult)
                                    valid = [bi for bi in range(4) if 0 <= b0 + bi + dl < 16]
                                    for bi in valid:
                                        kc = 2 + b0 + bi + dl
                                        S.mm(O[:, bi * 128:(bi + 1) * 128], Vt[:, kc, vo:vo + 128],
                                             pfin[:, bi * 128:(bi + 1) * 128], start=False, stop=(last and bi == valid[-1]))
                                n_ = len(steps)
                                sps = [None] * n_
                                sps[0] = emit_S(steps[0])
                                for i_ in range(n_):
                                    if i_ + 1 < n_:
                                        sps[i_ + 1] = emit_S(steps[i_ + 1])
                                    emit_rest(steps[i_], sps[i_], i_ == 0, i_ == n_ - 1)
                                if m == 1:
                                    S.ts("dve", dsb[64:128, 0:Tn], O[64:128, 0:Tn],
                                         SINKE[64:128, l * 4 + hq:l * 4 + hq + 1], ALU.add)
                                    S.recip(rd[0:64, 0:Tn], dsb[64:128, 0:Tn])
                                else:
                                    S.recip(rd[0:64, 0:Tn], O[64:128, 0:Tn])
                                S.tt("dve", bst[p0:p1_, 0:Tn], O[0:64, 0:Tn], rd[0:64, 0:Tn], ALU.mult)
                            cidx = (0, 4, 6)[m] + qcnk
                            S.dma("pool", brT_d[s, :, cidx, t0:t0 + Tn], bst[:, 0:Tn])
            phase_end("p2")

        def p2b(s, l):
            with ExitStack() as ps:
                fT = sb([128, 2, NT], BF16, "fT", ps)
                gcs = sb([128, 18, 4, 128], BF16, "gcs", ps)
                Cc = [sb([128, 16, 512], BF16, "Cc%d" % i, ps) for i in range(2)]
                Sc = [sb([128, 16, 512], BF16, "Sc%d" % i, ps) for i in range(2)]
                ost = [sb([128, 512], BF16, "ost%d" % i, ps) for i in range(2)]
                S.dma("sp", fT.full(), fT_d[s, :, :, :])
                tcs = range(18) if l == 0 else range(2, 18)
                for tc in tcs:
                    pg = PSF[tc % 2]
                    for fc in range(2):
                        S.mm(pg[:, (fc * 2) * 128:(fc * 2 + 1) * 128], fT[:, fc, tc * 128:(tc + 1) * 128], c64bd)
                        S.mm(pg[:, (fc * 2 + 1) * 128:(fc * 2 + 2) * 128], fT[:, fc, tc * 128:(tc + 1) * 128], s64bd)
                    S.copy("act" if tc % 2 == 0 else "dve", gcs[:, tc, :, :], pg[:, 0:512].re("p (a c) -> p a c", c=128))
                n = 0
                jobs = [(256 + i * 512, 512, 2, 16, dftc, dfts) for i in range(4)]
                if l == 0:
                    jobs = [(0, 256, 0, 2, dftc_c, dfts_c)] + jobs
                for ji, (t0, Tn, tc0, ntc, mc, msn) in enumerate(jobs):
                    c_, s_ = Cc[ji % 2], Sc[ji % 2]
                    col0 = 0 if t0 == 0 else t0 - NCX
                    S.dma("sp", c_[:, 0:ntc, 0:Tn], mc[:, col0:col0 + Tn].rearrange("(tc p) n -> p tc n", p=128))
                    S.dma("sp", s_[:, 0:ntc, 0:Tn], msn[:, col0:col0 + Tn].rearrange("(tc p) n -> p tc n", p=128))
                    for fc in range(2):
                        po = PSF[2 + (n % 2)]
                        for k_ in range(ntc):
                            S.mm(po[:, 0:Tn], gcs[:, tc0 + k_, fc * 2, :], c_[:, k_, 0:Tn], start=(k_ == 0), stop=False)
                            S.mm(po[:, 0:Tn], gcs[:, tc0 + k_, fc * 2 + 1, :], s_[:, k_, 0:Tn], start=False, stop=(k_ == ntc - 1))
                        o_ = ost[n % 2]
                        n += 1
                        S.copy("act", o_[:, 0:Tn], po[:, 0:Tn])
                        S.dma("pool", brT_d[s, :, 2 + fc, t0:t0 + Tn], o_[:, 0:Tn])
            phase_end("p2b")

        def p3_weights(l, ws):
            wG = sb([128, KC, 4096], BF16, "wG", ws)
            wB = sb([128, KC, D], BF16, "wB", ws)
            wO = sb([128, KC, D], BF16, "wO", ws)
            wR = sb([128, KC, E], F32, "wR", ws)
            cast_load(wG, lambda c0, c1: w_inG[l, :, c0:c1].rearrange("(kc p) n -> p kc n", p=128), 4096)
            cast_load(wB, lambda c0, c1: w_brP[l, :, c0:c1].rearrange("(kc p) n -> p kc n", p=128), D)
            cast_load(wO, lambda c0, c1: w_out[l, :, c0:c1].rearrange("(kc p) n -> p kc n", p=128), D)
            S.dma("sp", wR.full(), w_router[l, :, :].rearrange("(kc p) n -> p kc n", p=128))
            return wG, wB, wO, wR

        def p3(s, l, wts):
            wG, wB, wO, wR = wts
            with ExitStack() as ps:
                TT = 512
                uTc = sb([128, KC, TT], BF16, "uTc3", ps)
                brc = sb([128, 8, TT], BF16, "brc", ps)
                hTc = sb([128, KC, TT], F32, "hTc3", ps)
                zT = sb([128, KC, TT], F32, "zT", ps)
                mrg = sb([128, KC, TT], BF16, "mrg", ps)
                u2f = hTc
                u2b = brc
                gate = [sb([128, TT], F32, "gate%d" % i, ps) for i in range(2)]
                tmpm = [sb([128, TT], F32, "tmpm%d" % i, ps) for i in range(2)]
                acc = sb([128, TT], F32, "acc", ps)
                zsq = [sb([128, TT], F32, "zsq%d" % i, ps) for i in range(2)]
                mean_sb = sb([128, TT], F32, "mean_sb", ps)
                var_sb = sb([128, TT], F32, "var_sb", ps)
                accs = sb([128, TT], F32, "accs", ps)
                accq = sb([128, TT], F32, "accq", ps)
                ex = sb([16, TT], F32, "ex", ps)
                rsum = sb([16, TT], F32, "rsum", ps)
                utok = [sb([128, D], BF16, "utok%d" % i, ps) for i in range(2)]
                chunks = CH512 if l == 0 else CH512[1:]
                ng = 0
                for (t0, Tn) in chunks:
                    j = 2 if t0 == 0 else s
                    S.dma("sp", uTc[:, :, 0:Tn], uT_d[s, :, :, t0:t0 + Tn])
                    S.dma("sp", brc[:, :, 0:Tn], brT_d[s, :, :, t0:t0 + Tn])
                    S.dma("sp", hTc[:, :, 0:Tn], hT_d[s, :, :, t0:t0 + Tn])
                    for dc in range(KC):
                        for i in range(4):
                            pg = PSF[ng % 2]
                            pb = PSF[2 + ng % 2]
                            g_ = gate[ng % 2]
                            tm = tmpm[ng % 2]
                            ng += 1
                            for kc in range(KC):
                                S.mm(pg[:, 0:Tn], wG[:, kc, i * D + dc * 128:i * D + (dc + 1) * 128], uTc[:, kc, 0:Tn],
                                     start=(kc == 0), stop=(kc == KC - 1))
                            S.act(g_[:, 0:Tn], pg[:, 0:Tn], AF.Sigmoid)
                            for hf in range(2):
                                S.mm(pb[:, 0:Tn], wB[:, i * 2 + hf, dc * 128:(dc + 1) * 128], brc[:, i * 2 + hf, 0:Tn],
                                     start=(hf == 0), stop=(hf == 1))
                            if i == 0:
                                S.tt("dve", acc[:, 0:Tn], g_[:, 0:Tn], pb[:, 0:Tn], ALU.mult)
                            else:
                                S.tt("dve", tm[:, 0:Tn], g_[:, 0:Tn], pb[:, 0:Tn], ALU.mult)
                                if i < 3:
                                    S.tt("pool", acc[:, 0:Tn], acc[:, 0:Tn], tm[:, 0:Tn], ALU.add)
                                else:
                                    S.tt("pool", mrg[:, dc, 0:Tn], acc[:, 0:Tn], tm[:, 0:Tn], ALU.add)
                    for dc in range(KC):
                        py = PSF[dc % 2]
                        for kc in range(KC):
                            S.mm(py[:, 0:Tn], wO[:, kc, dc * 128:(dc + 1) * 128], mrg[:, kc, 0:Tn],
                                 start=(kc == 0), stop=(kc == KC - 1))
                        S.stt(zT[:, dc, 0:Tn], py[:, 0:Tn], mod(l, 2, dc, j), hTc[:, dc, 0:Tn], ALU.mult, ALU.add)
                    layernorm(zT, Tn, l, 0, (zsq, mean_sb, var_sb, accs, accq))
                    S.dma("pool", hT_d[s, :, :, t0:t0 + Tn], zT[:, :, 0:Tn])
                    for kc in range(KC):
                        S.act(u2f[:, kc, 0:Tn], zT[:, kc, 0:Tn], AF.Identity, scale=mod(l, 4, kc, j), bias=mod(l, 3, kc, j))
                        S.copy("pool", u2b[:, kc, 0:Tn], u2f[:, kc, 0:Tn])
                    pl = PSF[2]
                    for kc in range(KC):
                        S.mm(pl[0:16, 0:Tn], wR[:, kc, :], u2f[:, kc, 0:Tn], start=(kc == 0), stop=(kc == KC - 1))
                    S.act(ex[:, 0:Tn], pl[0:16, 0:Tn], AF.Exp)
                    S.mm(PSF[3][0:16, 0:Tn], ones_f[0:16, 0:16], ex[:, 0:Tn])
                    S.recip(rsum[:, 0:Tn], PSF[3][0:16, 0:Tn])
                    S.tt("dve", ex[:, 0:Tn], ex[:, 0:Tn], rsum[:, 0:Tn], ALU.mult)
                    S.dma("pool", aff_d[s, :, t0:t0 + Tn], ex[:, 0:Tn])
                    for ti in range(Tn // 128):
                        pt = PSB[ti % 2]
                        ut = utok[ti % 2]
                        for kc in range(KC):
                            S.tr(pt[:, kc * 128:(kc + 1) * 128], u2b[:, kc, ti * 128:(ti + 1) * 128], ident_b)
                        S.copy("act", ut.full(), pt.full())
                        r0 = s * NT + t0 + ti * 128
                        S.dma("pool", u2tok_d[r0:r0 + 128, :], ut.full())
            phase_end("p3")

        IDXT = sb([128, 5, E], I32, "IDXT")
        SELW = sb([128, 5, E], F32, "SELW")

        def p4(l):
            with ExitStack() as ps:
                work = sb([48, NL], F32, "work", ps)
                workc = sb([48, NCX], F32, "workc", ps)
                vals = sb([48, 288], F32, "vals", ps)
                idxu = sb([48, 288], U32, "idxu", ps)
                idxf = sb([48, 288], F32, "idxf", ps)
                lo = sb([16, 2, 288], F32, "lo", ps)
                tmpi = sb([128, E], F32, "tmpi", ps)
                S.memset("dve", work.full(), 0.0)
                S.memset("dve", workc.full(), 0.0)
                S.memset("dve", vals.full(), 0.0)
                S.memset("dve", idxu.full(), 0)
                for s in range(2):
                    S.dma("sp", work[32 * s:32 * s + 16, :], aff_d[s, :, NCX:NT])
                    if l == 0:
                        S.dma("sp", workc[32 * s:32 * s + 16, :], aff_d[s, :, 0:NCX])
                jobs = [(work, 0, 32)] + ([(workc, 256, 4)] if l == 0 else [])
                for (wt, slot0, rounds) in jobs:
                    for r_ in range(rounds):
                        sl = slice(slot0 + r_ * 8, slot0 + r_ * 8 + 8)
                        mx, ix, wk = _ap(vals[:, sl]), _ap(idxu[:, sl]), _ap(wt.full())
                        S.op("dve", lambda e, mx=mx, wk=wk: e.max(out=mx, in_=wk), reads=[wt.full()], writes=[vals.full()])
                        S.op("dve", lambda e, mx=mx, ix=ix, wk=wk: e.max_index(out=ix, in_max=mx, in_values=wk),
                             reads=[wt.full(), vals.full()], writes=[idxu.full()])
                        S.op("dve", lambda e, mx=mx, wk=wk: e.match_replace(out=wk, in_to_replace=mx, in_values=wk, imm_value=-1.0),
                             reads=[wt.full(), vals.full()], writes=[wt.full()])
                S.copy("dve", idxf.full(), idxu.full())
                S.ts("dve", idxf[0:16, 0:256], idxf[0:16, 0:256], float(NCX), ALU.add)
                S.ts("dve", idxf[32:48, 0:256], idxf[32:48, 0:256], float(NT + NCX), ALU.add)
                S.ts("dve", idxf[32:48, 256:288], idxf[32:48, 256:288], float(NT), ALU.add)
                S.copy("dve", lo[:, 0, :], idxf[32:48, :])
                S.copy("dve", lo[:, 1, :], vals[32:48, :])
                srcs = [(idxf[0:16, :], vals[0:16, :]), (lo[:, 0, :], lo[:, 1, :])]
                n = 0
                for s in range(2):
                    si, sv = srcs[s]
                    for half in range(2):
                        st = s * 2 + half
                        pt = PSF[n % 2]
                        n += 1
                        S.tr(pt[:, 0:16], si[:, half * 128:(half + 1) * 128], ident_f[0:16, 0:16])
                        S.copy("dve", tmpi.full(), pt[:, 0:16])
                        S.copy("dve", IDXT[:, st, :], tmpi.full())
                        S.tr(pt[:, 16:32], sv[:, half * 128:(half + 1) * 128], ident_f[0:16, 0:16])
                        S.copy("dve", SELW[:, st, :], pt[:, 16:32])
                    if l == 0:
                        pt = PSF[n % 2]
                        n += 1
                        S.tr(pt[0:32, 0:16], si[:, 256:288], ident_f[0:16, 0:16])
                        S.copy("dve", tmpi[0:32, :], pt[0:32, 0:16])
                        S.copy("dve", IDXT[32 * s:32 * s + 32, 4, :], tmpi[0:32, :])
                        S.tr(pt[0:32, 16:32], sv[:, 256:288], ident_f[0:16, 0:16])
                        S.copy("dve", SELW[32 * s:32 * s + 32, 4, :], pt[0:32, 16:32])
            phase_end("p4")

        def p5_weights(l, ws):
            wg = [sb([128, KC, D], BF16, "wg%d" % i, ws) for i in range(2)]
            wu = [sb([128, KC, D], BF16, "wu%d" % i, ws) for i in range(2)]
            wd = [sb([128, KC, D], BF16, "wd%d" % i, ws) for i in range(2)]
            for wt, src in ((wg[0], w_gate), (wu[0], w_up), (wd[0], w_down)):
                S.dma("pool", wt.full(), src[l, 0, :, :].rearrange("(kc p) n -> p kc n", p=128))
            return wg, wu, wd

        def p5(l, wts):
            wg, wu, wd = wts
            with ExitStack() as ps:
                xs = [sb([128, 5, D], BF16, "xs%d" % i, ps) for i in range(2)]
                xsT = sb([128, KC, 640], BF16, "xsT", ps)
                actT = sb([128, KC, 640], BF16, "actT", ps)
                sa = [sb([128, 512], F32, "sa%d" % i, ps) for i in range(2)]
                ysb = [sb([128, D], F32, "ysb%d" % i, ps) for i in range(2)]
                zer = sb([128, 2, D], F32, "zer", ps)
                FFN = T(ffn_d, "ffn_d")
                U2 = T(u2tok_d, "u2tok_d")
                S.memset("dve", zer.full(), 0.0)
                for i in range(2 * NT // 256):
                    S.dma("sp", V(FFN, ffn_d[i * 256:(i + 1) * 256, :].rearrange("(a p) d -> p a d", p=128)), zer.full())
                sts = [(0, 128), (1, 128), (2, 128), (3, 128)] + ([(4, 64)] if l == 0 else [])
                nsl = 576 if l == 0 else 512
                cgs = [(0, 512)] + ([(512, 576)] if l == 0 else [])

                def loadw(e_):
                    b_ = e_ % 2
                    for wt, src in ((wg[b_], w_gate), (wu[b_], w_up), (wd[b_], w_down)):
                        S.dma("pool", wt.full(), src[l, e_, :, :].rearrange("(kc p) n -> p kc n", p=128))

                def gathers(e_):
                    x_ = xs[e_ % 2]
                    for (st, np_) in sts:
                        o_ = _ap(x_[0:np_, st, :])
                        ix = _ap(IDXT[0:np_, st, e_:e_ + 1])

                        def fn(eng, o_=o_, ix=ix):
                            return eng.indirect_dma_start(out=o_, out_offset=None, in_=u2tok_d[:, :],
                                                          in_offset=bass.IndirectOffsetOnAxis(ap=ix, axis=0))
                        S.dma("pool", x_[0:np_, st, :], V(U2, None), extra_reads=[IDXT.full()], fn=fn)
                gathers(0)
                ny = 0
                for e_ in range(E):
                    if e_ + 1 < E:
                        loadw(e_ + 1)
                        gathers(e_ + 1)
                    b_ = e_ % 2
                    x_ = xs[b_]
                    for (st, np_) in sts:
                        pt = PSB[st % 2]
                        for kc in range(KC):
                            S.tr(pt[:, kc * 128:kc * 128 + np_], x_[0:np_, st, kc * 128:(kc + 1) * 128], ident_b[0:np_, 0:np_])
                        S.copy("act" if st % 2 == 0 else "dve", xsT[:, :, st * 128:st * 128 + np_],
                               pt.full().re("p (k t) -> p k t", t=128)[:, :, 0:np_])
                    for fc in range(KC):
                        for (c0, c1) in cgs:
                            pa, pu = PSF[(fc % 2) * 2], PSF[(fc % 2) * 2 + 1]
                            w_ = c1 - c0
                            for kc in range(KC):
                                S.mm(pa[:, 0:w_], wg[b_][:, kc, fc * 128:(fc + 1) * 128], xsT[:, kc, c0:c1],
                                     start=(kc == 0), stop=(kc == KC - 1))
                            for kc in range(KC):
                                S.mm(pu[:, 0:w_], wu[b_][:, kc, fc * 128:(fc + 1) * 128], xsT[:, kc, c0:c1],
                                     start=(kc == 0), stop=(kc == KC - 1))
                            s_ = sa[fc % 2]
                            S.act(s_[:, 0:w_], pa[:, 0:w_], AF.Silu)
                            S.tt("dve", actT[:, fc, c0:c1], s_[:, 0:w_], pu[:, 0:w_], ALU.mult)
                    for (st, np_) in sts:
                        y_ = ysb[ny % 2]
                        ny += 1
                        for dh in range(2):
                            py = PSF[4 + dh]
                            for fc in range(KC):
                                S.mm(py[0:np_, :], actT[:, fc, st * 128:st * 128 + np_], wd[b_][:, fc, dh * 512:(dh + 1) * 512],
                                     start=(fc == 0), stop=(fc == KC - 1))
                            S.act(y_[0:np_, dh * 512:(dh + 1) * 512], py[0:np_, :], AF.Copy, scale=SELW[0:np_, st, e_:e_ + 1])
                        yi = _ap(y_[0:np_, :])
                        ix = _ap(IDXT[0:np_, st, e_:e_ + 1])

                        def fn(eng, yi=yi, ix=ix):
                            return eng.indirect_dma_start(out=ffn_d[:, :], out_offset=bass.IndirectOffsetOnAxis(ap=ix, axis=0),
                                                          in_=yi, in_offset=None, compute_op=ALU.add)
                        S.dma("pool", V(FFN, None), y_[0:np_, :], extra_reads=[IDXT.full(), V(FFN, None)], fn=fn)
            phase_end("p5")

        def p6(s, l):
            with ExitStack() as ps:
                ftok = sb([128, 4, D], F32, "ftok", ps)
                hTc = sb([128, KC, 512], F32, "hTc6", ps)
                zT = sb([128, KC, 512], F32, "zT6", ps)
                zsq = [sb([128, 512], F32, "zsq6%d" % i, ps) for i in range(2)]
                mean_sb = sb([128, 512], F32, "mean6", ps)
                var_sb = sb([128, 512], F32, "var6", ps)
                accs = sb([128, 512], F32, "accs6", ps)
                accq = sb([128, 512], F32, "accq6", ps)
                otok = [sb([128, D], F32, "otok%d" % i, ps) for i in range(2)]
                chunks = CH512 if l == 0 else CH512[1:]
                for (t0, Tn) in chunks:
                    j = 2 if t0 == 0 else s
                    nt = Tn // 128
                    r0 = s * NT + t0
                    S.dma("sp", ftok[:, 0:nt, :], ffn_d[r0:r0 + Tn, :].rearrange("(ti p) d -> p ti d", p=128))
                    S.dma("sp", hTc[:, :, 0:Tn], hT_d[s, :, :, t0:t0 + Tn])
                    for kc in range(KC):
                        pk = PSF[kc % 4]
                        for ti in range(nt):
                            S.tr(pk[:, ti * 128:(ti + 1) * 128], ftok[:, ti, kc * 128:(kc + 1) * 128], ident_f)
                        S.stt(zT[:, kc, 0:Tn], pk[:, 0:Tn], mod(l, 5, kc, j), hTc[:, kc, 0:Tn], ALU.mult, ALU.add)
                    layernorm(zT, Tn, l, 2, (zsq, mean_sb, var_sb, accs, accq))
                    if l < L - 1:
                        S.dma("pool", hT_d[s, :, :, t0:t0 + Tn], zT[:, :, 0:Tn])
                    else:
                        for ti in range(nt):
                            ot = otok[ti % 2]
                            for hf in range(2):
                                po = PSF[hf]
                                for k4 in range(4):
                                    kc = hf * 4 + k4
                                    S.tr(po[:, k4 * 128:(k4 + 1) * 128], zT[:, kc, ti * 128:(ti + 1) * 128], ident_f)
                                S.copy("act" if hf == 0 else "dve", ot[:, hf * 512:(hf + 1) * 512], po.full())
                            S.dma("pool", out[s, t0 - NCX + ti * 128:t0 - NCX + (ti + 1) * 128, :], ot.full())
            phase_end("p6")

        done = False
        for l in range(L):
            p1(l, (0,) if stop_after == "p1" else (0, 1))
            if stop_after == "p1":
                break
            for s in range(2):
                p2(s, l)
                with ExitStack() as ws:
                    wts = p3_weights(l, ws) if stop_after != "p2" else None
                    p2b(s, l)
                    if stop_after == "p2":
                        done = True
                        break
                    p3(s, l, wts)
                if stop_after == "p3":
                    done = True
                    break
            if done:
                break
            with ExitStack() as ws:
                wts5 = p5_weights(l, ws)
                p4(l)
                p5(l, wts5)
            if stop_after == "p5":
                break
            for s in range(2):
                p6(s, l)
            if stop_after == "l0":
                break
        S.barrier()
        S.emit()
        print("bass ops:", S.nops, {e: len(S.items[e]) for e in ENG})
        _CACHE["marks"] = marks
    return nc


_CACHE = {}


def _host_shared(inp):
    f = np.float32
    sh = dict(_consts())
    cols, vcols = _wina_cols()
    w_in = np.asarray(inp["w_in"], f)
    sh["w_inA"] = np.ascontiguousarray(w_in[:, :, cols])
    sh["w_inV"] = np.ascontiguousarray(w_in[:, :, vcols])
    sh["w_inG"] = np.ascontiguousarray(w_in[:, :, 1792:])
    sh["w_mod"] = np.asarray(inp["w_mod"], f)
    sh["b_modT"] = np.ascontiguousarray(np.asarray(inp["b_mod"], f).reshape(L, 48, 128).transpose(2, 0, 1))
    g = np.asarray(inp["qk_gain"], f)
    d_ = np.array([_ORD[p % 64] for p in range(128)])
    gt = np.stack([g[:, 0, d_], g[:, 0, d_], g[:, 1, d_], g[:, 1, d_]], -1)
    sh["gainT"] = np.ascontiguousarray(gt.transpose(1, 0, 2))
    sk = np.asarray(inp["sink_logit"], f).reshape(1, L * 4)
    sh["sinkB"] = np.ascontiguousarray(np.broadcast_to(sk, (128, L * 4)))
    rpb = np.asarray(inp["na_rpb"], f)
    p = np.arange(128)
    kcol = (p % 64)[:, None, None]
    i_ = (p // 64)[:, None, None]
    m_ = np.arange(15)[None, :, None]
    c_ = np.arange(64)[None, None, :]
    a_ = np.clip(14 - m_ + i_, 0, 14) + 0 * c_
    oc = np.clip(kcol - c_ + 15, 0, 30) + 0 * m_
    tb = rpb[:, :, a_, oc]
    sh["tbs"] = np.ascontiguousarray(tb.transpose(0, 2, 1, 3, 4).reshape(L, 128, 4 * 15 * 64))
    wb = np.asarray(inp["w_branch"], f)
    perm = np.concatenate([np.arange(0, 64), np.arange(128, 192), np.arange(64, 128), np.arange(192, 256)])
    wbp = wb.copy()
    for i in (0, 2, 3):
        wbp[:, i] = wb[:, i][:, perm]
    sh["w_brP"] = np.ascontiguousarray(wbp.reshape(L, 1024, D))
    sh["w_out"] = np.asarray(inp["w_out"], f)
    ln = np.stack([inp["ln1_g"], inp["ln1_b"], inp["ln2_g"], inp["ln2_b"]], 1).astype(f)
    sh["lnT"] = np.ascontiguousarray(ln.reshape(L, 4, KC, 128).transpose(3, 0, 1, 2).reshape(128, L * 4 * KC))
    sh["w_router"] = np.asarray(inp["w_router"], f)
    sh["w_gate"] = np.asarray(inp["w_gate"], f)
    sh["w_up"] = np.asarray(inp["w_up"], f)
    sh["w_down"] = np.asarray(inp["w_down"], f)
    return sh


def _host_core(inp, core):
    f = np.float32
    b0 = core * 2
    x = np.asarray(inp["x"], f)
    ctx = np.asarray(inp["ctx"], f)
    c = np.asarray(inp["c"], f)
    cc = np.asarray(inp["c_ctx"], f)
    d = {}
    d["x2"] = np.ascontiguousarray(np.concatenate([ctx[b0:b0 + 2], x[b0:b0 + 2]], axis=1))
    cv = np.stack([c[b0], c[b0 + 1], cc, np.zeros_like(cc)], -1)
    d["cT"] = np.ascontiguousarray(cv.reshape(KC, 128, 4).transpose(1, 0, 2))
    return d


def kernel(**inputs):
    n = 8
    if "nc" not in _CACHE:
        _CACHE["nc"] = build_program(debug=False)
    nc = _CACHE["nc"]
    sh = _host_shared(inputs)
    in_maps = []
    for core in range(n):
        m = dict(sh)
        m.update(_host_core(inputs, core))
        in_maps.append(m)
    res = run_bass_kernel_spmd(nc, in_maps, core_ids=list(range(n)))
    _CACHE["last"] = res
    outs = [np.asarray(r["out"], np.float32) for r in res.results]
    return np.concatenate(outs, axis=0)
```

```python
import numpy as np
from contextlib import ExitStack
import ml_dtypes
import concourse.bass as bass
import concourse.mybir as mybir
from concourse.bass_utils import run_bass_kernel_spmd

F32 = mybir.dt.float32
BF16 = mybir.dt.bfloat16
U32 = mybir.dt.uint32
I32 = mybir.dt.int32
ALU = mybir.AluOpType
AF = mybir.ActivationFunctionType

L = 2
D = 1024
KC = 8
NT = 2304
NCX = 256
NL = 2048
E = 16
ALPHA = (2 * L) ** 0.25
LN_EPS_S = 1e-6 / (ALPHA * ALPHA)
RMS_EPS = 1e-6
NA_COLS = 11 * 128
ENG = ("pe", "act", "dve", "pool", "sp")


class T:
    __slots__ = ("h", "w", "r", "name")

    def __init__(self, h, name=""):
        self.h = h
        self.w = {}
        self.r = {}
        self.name = name

    def __getitem__(self, idx):
        return V(self, self.h[idx])

    def full(self):
        return V(self, self.h[:])


class V:
    __slots__ = ("t", "ap")

    def __init__(self, t, ap):
        self.t = t
        self.ap = ap

    def re(self, pat, **kw):
        return V(self.t, self.ap.rearrange(pat, **kw))

    def __getitem__(self, idx):
        return V(self.t, self.ap[idx])


def _ap(x):
    return x.ap if isinstance(x, V) else x


def _ts(xs):
    return [x.t for x in xs if isinstance(x, V)]


class Sched:
    def __init__(self, nc, es, n_dma_sems=(("sp", 12), ("pool", 10), ("act", 4))):
        self.nc = nc
        self.es = es
        self.items = {e: [] for e in ENG}
        self.sems = {}
        self.cnt = {}
        self.seen = {e: {} for e in ENG}
        for e in ENG:
            self._mk("c_" + e)
        self.dma_pool = {}
        self.dma_rr = {}
        for e, n in n_dma_sems:
            self.dma_pool[e] = [self._mk("d_%s%d" % (e, i)) for i in range(n)]
            self.dma_rr[e] = 0
        self.nops = 0

    def _mk(self, key):
        self.sems[key] = self.es.enter_context(self.nc.semaphore(key))
        self.cnt[key] = 0
        return key

    def _need(self, e, key, val):
        if val <= 0 or self.seen[e].get(key, 0) >= val:
            return
        self.seen[e][key] = val
        self.items[e].append(("wait", key, val))

    def _deps(self, e, reads, writes, is_pe=False, is_dma=False):
        own = "c_" + e if not is_dma else "__none__"
        for t in reads:
            for k, v in t.w.items():
                if is_pe and k == own:
                    continue
                self._need(e, k, v)
        for t in writes:
            for k, v in t.r.items():
                if k != own:
                    self._need(e, k, v)
            for k, v in t.w.items():
                if k != own:
                    self._need(e, k, v)

    def _mark(self, key, val, reads, writes):
        for t in reads:
            t.r[key] = max(t.r.get(key, 0), val)
        for t in writes:
            if t.r:
                t.r = {}
                t.w = {}
            t.w[key] = max(t.w.get(key, 0), val)

    def op(self, e, fn, reads=(), writes=()):
        reads = _ts(reads)
        writes = _ts(writes)
        self._deps(e, reads, writes, is_pe=(e == "pe"))
        key = "c_" + e
        self.cnt[key] += 1
        self.items[e].append(("op", fn, key, 1))
        self._mark(key, self.cnt[key], reads, writes)
        self.nops += 1

    def dma(self, e, out, in_, extra_reads=(), fn=None, **kw):
        reads = _ts([in_] + list(extra_reads))
        writes = _ts([out])
        pool = self.dma_pool[e]
        key = pool[self.dma_rr[e] % len(pool)]
        self.dma_rr[e] += 1
        self._need(e, key, self.cnt[key])
        self._deps(e, reads, writes, is_dma=True)
        self.cnt[key] += 16
        if fn is None:
            o, i = _ap(out), _ap(in_)

            def fn(eng, o=o, i=i, kw=kw):
                return eng.dma_start(out=o, in_=i, **kw)
        self.items[e].append(("op", fn, key, 16))
        self._mark(key, self.cnt[key], reads, writes)
        self.nops += 1

    def barrier(self):
        for e in ENG:
            for k, v in self.cnt.items():
                if k != "c_" + e:
                    self._need(e, k, v)

    def emit(self):
        nc = self.nc
        with nc.Block() as block:
            def run(e):
                def body(eng):
                    for it in self.items[e]:
                        if it[0] == "wait":
                            eng.wait_ge(self.sems[it[1]], it[2])
                        else:
                            it[1](eng).then_inc(self.sems[it[2]], it[3])
                return body
            block.tensor(run("pe"))
            block.scalar(run("act"))
            block.vector(run("dve"))
            block.gpsimd(run("pool"))
            block.sync(run("sp"))

    def mm(self, out, lhsT, rhs, start=True, stop=True):
        o, a, b = _ap(out), _ap(lhsT), _ap(rhs)
        self.op("pe", lambda e: e.matmul(o, a, b, start=start, stop=stop), reads=[lhsT, rhs], writes=[out])

    def tr(self, out, in_, ident):
        o, a, b = _ap(out), _ap(in_), _ap(ident)
        self.op("pe", lambda e: e.transpose(o, a, b), reads=[in_, ident], writes=[out])

    def act(self, out, in_, func, scale=1.0, bias=0.0):
        o, i = _ap(out), _ap(in_)
        sc, bi = _ap(scale), _ap(bias)
        self.op("act", lambda e: e.activation(out=o, in_=i, func=func, bias=bi, scale=sc),
                reads=[in_, scale, bias], writes=[out])

    def tt(self, eng, out, a, b, op):
        o, x, y = _ap(out), _ap(a), _ap(b)
        self.op(eng, lambda e: e.tensor_tensor(o, x, y, op), reads=[a, b], writes=[out])

    def ts(self, eng, out, a, s1, op0, s2=None, op1=None):
        o, x, p1, p2 = _ap(out), _ap(a), _ap(s1), _ap(s2)
        if op1 is None:
            self.op(eng, lambda e: e.tensor_scalar(o, x, p1, None, op0), reads=[a, s1], writes=[out])
        else:
            self.op(eng, lambda e: e.tensor_scalar(o, x, p1, p2, op0, op1), reads=[a, s1, s2], writes=[out])

    def stt(self, out, a, s, b, op0, op1):
        o, x, p, y = _ap(out), _ap(a), _ap(s), _ap(b)
        self.op("dve", lambda e: e.scalar_tensor_tensor(o, x, p, y, op0, op1), reads=[a, s, b], writes=[out])

    def copy(self, eng, out, in_):
        o, i = _ap(out), _ap(in_)
        if eng == "act":
            self.op("act", lambda e: e.activation(out=o, in_=i, func=AF.Copy), reads=[in_], writes=[out])
        else:
            self.op(eng, lambda e: e.tensor_copy(o, i), reads=[in_], writes=[out])

    def recip(self, out, in_):
        o, i = _ap(out), _ap(in_)
        self.op("dve", lambda e: e.reciprocal(o, i), reads=[in_], writes=[out])

    def memset(self, eng, out, val):
        o = _ap(out)
        self.op(eng, lambda e: e.memset(o, val), writes=[out])


def _partner(d):
    return d + 16 if (d % 32) < 16 else d - 16


_ORD = list(range(0, 16)) + list(range(32, 48)) + list(range(16, 32)) + list(range(48, 64))


def _rope_tables():
    cos = np.ones((128, NT), np.float32)
    sin = np.zeros((128, NT), np.float32)
    t = np.arange(NL)
    inv = (np.float32(10000.0) ** (-np.arange(0, 32, 2, dtype=np.float32) / np.float32(32))).astype(np.float32)
    for p in range(128):
        d = _ORD[p % 64]
        pos = (t // 64) if d < 32 else (t % 64)
        ang = pos.astype(np.float32) * inv[d % 16]
        cos[p, NCX:] = np.cos(ang).astype(np.float32)
        sgn = -1.0 if (d % 32) < 16 else 1.0
        sin[p ^ 32, NCX:] = sgn * np.sin(ang).astype(np.float32)
    return cos, sin


def _na_masks():
    rows, W, kh, kw = 32, 64, 8, 16
    t = np.arange(NL)
    r, c = t // W, t % W
    r0 = np.clip(r - kh // 2, 0, rows - kh)
    c0 = np.clip(c - kw // 2, 0, W - kw)
    k = np.arange(NL)
    kr, kcol = k // W, k % W
    valid = ((kr[None, :] >= r0[:, None]) & (kr[None, :] < r0[:, None] + kh) &
             (kcol[None, :] >= c0[:, None]) & (kcol[None, :] < c0[:, None] + kw))
    full = np.zeros((16, 7, 128, 128), np.float32)
    for b in range(16):
        for dl in range(-3, 4):
            kci = b + dl
            if 0 <= kci < 16:
                full[b, dl + 3] = valid[b * 128:(b + 1) * 128, kci * 128:(kci + 1) * 128].T
    types = [0, 1] + [2] * 12 + [3, 4]
    rep = {0: 0, 1: 1, 2: 5, 3: 14, 4: 15}
    for b in range(16):
        assert np.array_equal(full[b], full[rep[types[b]]]), b
    namc = np.zeros((3, 7, 128, 512), np.float32)
    for qt, b0 in enumerate((0, 4, 12)):
        for bi in range(4):
            namc[qt, :, :, bi * 128:(bi + 1) * 128] = full[b0 + bi]
    for b0 in (4, 8):
        for bi in range(4):
            assert np.array_equal(full[b0 + bi], full[5])
    qdl = [[dl for dl in range(-3, 4) if namc[qt, dl + 3].any()] for qt in range(3)]
    return namc, qdl


def _dft(n):
    t = np.arange(n, dtype=np.int64)
    m = (t[:, None] * t[None, :]) % n
    ang = 2.0 * np.pi * m.astype(np.float64) / n
    return np.cos(ang), np.sin(ang)


_NA_MASKC, _NA_QDELTAS = _na_masks()


def _consts():
    c = {}
    cos, sin = _rope_tables()
    c["rope"] = np.stack([cos, sin], 1).copy()
    ident = np.eye(128, dtype=np.float32)
    onesbd = np.zeros((128, 128), np.float32)
    onesbd[:64, :64] = 1.0
    onesbd[64:, 64:] = 1.0
    c64, s64 = _dft(64)
    cbd = np.zeros((128, 128), np.float64)
    sbd = np.zeros((128, 128), np.float64)
    for g in range(2):
        cbd[g * 64:(g + 1) * 64, g * 64:(g + 1) * 64] = c64 / 8.0
        sbd[g * 64:(g + 1) * 64, g * 64:(g + 1) * 64] = s64 / 8.0
    win = np.zeros((3, 128, 128), np.float32)
    i = np.arange(128)[:, None]
    j = np.arange(128)[None, :]
    win[0] = (j <= i)
    win[1] = 1.0
    win[2] = (i <= j)
    permm = np.zeros((128, 128), np.float32)
    for m_ in range(128):
        permm[(m_ // 64) * 64 + _partner(m_ % 64), m_] = 1.0
    cb = np.concatenate([ident, onesbd, cbd.astype(np.float32), sbd.astype(np.float32), permm], axis=1)
    c["cbf"] = cb.astype(ml_dtypes.bfloat16)
    winc = np.zeros((3, 3, 128, 512), np.float32)
    for qt, b0 in enumerate((0, 4, 12)):
        for bi in range(4):
            for dl in (-1, 0, 1):
                if 0 <= b0 + bi + dl < 16:
                    winc[qt, dl + 1, :, bi * 128:(bi + 1) * 128] = win[dl + 1]
    mk = np.concatenate([winc[qt, d_] for qt in range(3) for d_ in range(3)] +
                        [_NA_MASKC[qt, d_] for qt in range(3) for d_ in range(7)], axis=1)
    c["maskc"] = mk.astype(ml_dtypes.bfloat16)
    cf = np.concatenate([ident, np.full((128, 128), 1.0 / D, np.float32), np.ones((128, 128), np.float32)], axis=1)
    c["cf32"] = cf.astype(np.float32)
    cs, ss = _dft(NL)
    c["dftc"] = (cs / np.sqrt(NL)).astype(ml_dtypes.bfloat16)
    c["dfts"] = (-ss / np.sqrt(NL)).astype(ml_dtypes.bfloat16)
    cs, ss = _dft(NCX)
    c["dftc_c"] = (cs / np.sqrt(NCX)).astype(ml_dtypes.bfloat16)
    c["dfts_c"] = (-ss / np.sqrt(NCX)).astype(ml_dtypes.bfloat16)
    return c


def _wina_cols():
    offs = {"qA": 0, "kA": 256, "vA": 384, "f": 512, "qC": 768, "kC": 1024, "vC": 1152,
            "qD": 1280, "kD": 1536, "vD": 1664}

    def qch(base, pair):
        return [base + hq * 64 + _ORD[p] for hq in pair for p in range(64)]
    cols = []
    for mname in ("A", "C", "D"):
        qb, kb = offs["q" + mname], offs["k" + mname]
        cols += qch(qb, (0, 2)) + qch(qb, (1, 3)) + qch(kb, (0, 1))
    cols += list(range(offs["f"], offs["f"] + 256))
    vcols = list(range(384, 512)) + list(range(1152, 1280)) + list(range(1664, 1792))
    return np.array(cols), np.array(vcols)


def build_program(debug=False, stop_after=None):
    nc = bass.Bass("TRN2", target_bir_lowering=False)

    def din(name, shape, dt=F32):
        return nc.dram_tensor(name, list(shape), dt, kind="ExternalInput").ap()

    def dscr(name, shape, dt):
        return nc.dram_tensor(name, list(shape), dt, kind=("ExternalOutput" if debug else "Internal")).ap()

    x2 = din("x2", [2, NT, D])
    cT = din("cT", [128, KC, 4])
    w_mod = din("w_mod", [L, D, 6 * D])
    b_modT = din("b_modT", [128, L, 48])
    w_inA = din("w_inA", [L, D, NA_COLS])
    w_inV = din("w_inV", [L, D, 384])
    w_inG = din("w_inG", [L, D, 4096])
    gainT = din("gainT", [128, L, 4])
    sinkB = din("sinkB", [128, L * 4])
    tbs = din("tbs", [L, 128, 4 * 15 * 64])
    rope = din("rope", [128, 2, NT])
    cbf = din("cbf", [128, 5 * 128], BF16)
    maskc = din("maskc", [128, 30 * 512], BF16)
    cf32 = din("cf32", [128, 384])
    dftc = din("dftc", [NL, NL], BF16)
    dfts = din("dfts", [NL, NL], BF16)
    dftc_c = din("dftc_c", [NCX, NCX], BF16)
    dfts_c = din("dfts_c", [NCX, NCX], BF16)
    w_brP = din("w_brP", [L, 1024, D])
    w_out = din("w_out", [L, D, D])
    lnT = din("lnT", [128, L * 4 * KC])
    w_router = din("w_router", [L, D, E])
    w_gate = din("w_gate", [L, E, D, D])
    w_up = din("w_up", [L, E, D, D])
    w_down = din("w_down", [L, E, D, D])
    out = nc.dram_tensor("out", [2, NL, D], F32, kind="ExternalOutput").ap()

    hT_d = dscr("hT_d", [2, 128, KC, NT], F32)
    uT_d = dscr("uT_d", [2, 128, KC, NT], BF16)
    qT_d = dscr("qT_d", [2, 128, 6, NT], BF16)
    kT_d = dscr("kT_d", [2, 128, 3, NT], BF16)
    v_d = dscr("v_d", [2, 128, 18, 768], BF16)
    fT_d = dscr("fT_d", [2, 128, 2, NT], BF16)
    brT_d = dscr("brT_d", [2, 128, 8, NT], BF16)
    u2tok_d = dscr("u2tok_d", [2 * NT, D], BF16)
    ffn_d = dscr("ffn_d", [2 * NT, D], F32)
    aff_d = dscr("aff_d", [2, 16, NT], F32)

    CH512 = [(0, 256)] + [(256 + i * 512, 512) for i in range(4)]
    CH256 = [(i * 256, 256) for i in range(9)]

    with ExitStack() as es:
        S = Sched(nc, es)

        uid = [0]

        def sb(shape, dt, name, stack=es):
            uid[0] += 1
            nm = "%s_%d" % (name, uid[0])
            return T(stack.enter_context(nc.sbuf_tensor(nm, list(shape), dt)), nm)

        PSF = [T(es.enter_context(nc.psum_tensor("psf%d" % i, [128, 512], F32)), "psf%d" % i) for i in range(6)]
        PSB = [T(es.enter_context(nc.psum_tensor("psb%d" % i, [128, 1024], BF16)), "psb%d" % i) for i in range(2)]

        CB = sb([128, 5 * 128], BF16, "CB")
        CF = sb([128, 384], F32, "CF")
        MOD = sb([128, L, 48, 4], F32, "MOD")
        LNT = sb([128, L * 4 * KC], F32, "LNT")
        GAIN = sb([128, L, 4], F32, "GAIN")
        SINKE = sb([128, L * 4], F32, "SINKE")
        S.dma("sp", CB.full(), cbf[:, :])
        S.dma("sp", CF.full(), cf32[:, :])
        S.dma("sp", LNT.full(), lnT[:, :])
        S.dma("sp", GAIN.full(), gainT[:, :, :])
        S.dma("sp", SINKE.full(), sinkB[:, :])
        S.act(SINKE.full(), SINKE.full(), AF.Exp)
        ident_b = CB[:, 0:128]
        onesbd = CB[:, 128:256]
        c64bd = CB[:, 256:384]
        s64bd = CB[:, 384:512]
        permm = CB[:, 512:640]

        ident_f = CF[:, 0:128]
        ones_mean = CF[:, 128:256]
        ones_f = CF[:, 256:384]

        def lnv(l, which, kc):
            i_ = (l * 4 + which) * KC + kc
            return LNT[:, i_:i_ + 1]

        def mod(l, kind, kc, j):
            return MOD[:, l, kind * 8 + kc, j:j + 1]

        def cast_load(dst, src_ap_fn, ncols, eng="pool", step=1024):
            for c0 in range(0, ncols, step):
                c1 = min(ncols, c0 + step)
                S.dma(eng, dst[:, :, c0:c1], src_ap_fn(c0, c1))

        marks = []

        def phase_end(name="?"):
            S.barrier()
            marks.append((name, S.cnt["c_pe"]))

        with ExitStack() as ps:
            cTs = sb([128, KC, 4], F32, "cTs", ps)
            bm = sb([128, L, 48], F32, "bm", ps)
            wm = [sb([128, KC, 768], F32, "wm%d" % i, ps) for i in range(2)]
            S.dma("sp", cTs.full(), cT[:, :, :])
            S.dma("sp", bm.full(), b_modT[:, :, :])
            S.act(cTs.full(), cTs.full(), AF.Silu)
            n = 0
            for l in range(L):
                pm = PSF[l]
                for blk in range(8):
                    w = wm[n % 2]
                    n += 1
                    S.dma("sp", w.full(), w_mod[l, :, blk * 768:(blk + 1) * 768].rearrange("(kc p) n -> p kc n", p=128))
                    for cc in range(6):
                        ccg = blk * 6 + cc
                        for kc in range(KC):
                            S.mm(pm[:, ccg * 4:ccg * 4 + 4], w[:, kc, cc * 128:(cc + 1) * 128], cTs[:, kc, :],
                                 start=(kc == 0), stop=(kc == KC - 1))
                for ccg in range(48):
                    S.ts("dve", MOD[:, l, ccg, :], pm[:, ccg * 4:ccg * 4 + 4], bm[:, l, ccg:ccg + 1], ALU.add)
                for kind in (1, 4):
                    S.ts("dve", MOD[:, l, kind * 8:(kind + 1) * 8, :], MOD[:, l, kind * 8:(kind + 1) * 8, :], 1.0, ALU.add)
                for kind in (2, 5):
                    S.ts("dve", MOD[:, l, kind * 8:(kind + 1) * 8, :], MOD[:, l, kind * 8:(kind + 1) * 8, :],
                         1.0 / ALPHA, ALU.mult)
            phase_end("p0")

        with ExitStack() as ps:
            xin = [sb([128, 4, D], F32, "xin%d" % i, ps) for i in range(2)]
            hst = [sb([128, KC, 512], F32, "hst%d" % i, ps) for i in range(2)]
            n = 0
            for s in range(2):
                for (t0, Tn) in CH512:
                    nt = Tn // 128
                    xi, hs = xin[n % 2], hst[n % 2]
                    n += 1
                    S.dma("sp", xi[:, 0:nt, :], x2[s, t0:t0 + Tn, :].rearrange("(ti p) d -> p ti d", p=128))
                    for kc in range(KC):
                        pt = PSF[kc % 4]
                        for ti in range(nt):
                            S.tr(pt[:, ti * 128:(ti + 1) * 128], xi[:, ti, kc * 128:(kc + 1) * 128], ident_f)
                        S.copy("act" if kc % 2 == 0 else "dve", hs[:, kc, 0:Tn], pt[:, 0:Tn])
                    S.dma("pool", hT_d[s, :, :, t0:t0 + Tn], hs[:, :, 0:Tn])
            phase_end("pin")

        if stop_after == "pin":
            S.barrier()
            S.emit()
            return nc

        def layernorm(zT, Tn, l, which_g, tmp):
            zsq, mean_sb, var_sb, accs, accq = tmp
            pm, pq = PSF[4], PSF[5]
            for kc in range(KC):
                q = zsq[kc % 2]
                S.act(q[:, 0:Tn], zT[:, kc, 0:Tn], AF.Square)
                if kc == 1:
                    S.tt("pool", accs[:, 0:Tn], zT[:, 0, 0:Tn], zT[:, 1, 0:Tn], ALU.add)
                    S.tt("dve", accq[:, 0:Tn], zsq[0][:, 0:Tn], zsq[1][:, 0:Tn], ALU.add)
                elif kc > 1:
                    S.tt("pool", accs[:, 0:Tn], accs[:, 0:Tn], zT[:, kc, 0:Tn], ALU.add)
                    S.tt("dve", accq[:, 0:Tn], accq[:, 0:Tn], q[:, 0:Tn], ALU.add)
            S.mm(pm[:, 0:Tn], ones_mean, accs[:, 0:Tn])
            S.mm(pq[:, 0:Tn], ones_mean, accq[:, 0:Tn])
            S.copy("act", mean_sb[:, 0:Tn], pm[:, 0:Tn])
            S.tt("dve", var_sb[:, 0:Tn], mean_sb[:, 0:Tn], mean_sb[:, 0:Tn], ALU.mult)
            S.tt("dve", var_sb[:, 0:Tn], pq[:, 0:Tn], var_sb[:, 0:Tn], ALU.subtract)
            S.act(var_sb[:, 0:Tn], var_sb[:, 0:Tn], AF.Sqrt, bias=EPSV[:, 0:1])
            S.recip(var_sb[:, 0:Tn], var_sb[:, 0:Tn])
            for kc in range(KC):
                S.tt("pool", zT[:, kc, 0:Tn], zT[:, kc, 0:Tn], mean_sb[:, 0:Tn], ALU.subtract)
                S.tt("dve", zT[:, kc, 0:Tn], zT[:, kc, 0:Tn], var_sb[:, 0:Tn], ALU.mult)
                S.act(zT[:, kc, 0:Tn], zT[:, kc, 0:Tn], AF.Identity, scale=lnv(l, which_g, kc), bias=lnv(l, which_g + 1, kc))

        EPSV = sb([128, 2], F32, "EPSV")
        S.memset("dve", EPSV[:, 0:1], LN_EPS_S)
        S.memset("dve", EPSV[:, 1:2], RMS_EPS)

        def p1(l, samples):
            with ExitStack() as ps:
                wA = sb([128, KC, NA_COLS], BF16, "wA", ps)
                wV = sb([128, KC, 384], BF16, "wV", ps)
                ROPE = sb([128, 2, NT], F32, "ROPE", ps)
                hTc = [sb([128, KC, 512], F32, "hTc%d" % i, ps) for i in range(2)]
                uTc = [sb([128, KC, 512], BF16, "uTc%d" % i, ps) for i in range(2)]
                stg = [sb([128, 512], BF16, "stg%d" % i, ps) for i in range(4)]
                sq = sb([128, 512], BF16, "sq", ps)
                qb_ = sb([128, 512], BF16, "qb_", ps)
                rs = sb([128, 512], F32, "rs", ps)
                t1 = sb([128, 512], F32, "t1", ps)
                t2 = sb([128, 512], F32, "t2", ps)
                vst = [sb([128, 4, 6, 128], BF16, "vst%d" % i, ps) for i in range(2)]
                cast_load(wA, lambda c0, c1: w_inA[l, :, c0:c1].rearrange("(kc p) n -> p kc n", p=128), NA_COLS)
                cast_load(wV, lambda c0, c1: w_inV[l, :, c0:c1].rearrange("(kc p) n -> p kc n", p=128), 384)
                S.dma("sp", ROPE.full(), rope[:, :, :])
                for i in range(2):
                    S.memset("pool", vst[i].full(), 1.0)
                nst = 0
                for s in samples:
                    for ci, (t0, Tn) in enumerate(CH512):
                        j = 2 if t0 == 0 else s
                        h, u = hTc[ci % 2], uTc[ci % 2]
                        S.dma("sp", h[:, :, 0:Tn], hT_d[s, :, :, t0:t0 + Tn])
                        for kc in range(KC):
                            S.act(u[:, kc, 0:Tn], h[:, kc, 0:Tn], AF.Identity, scale=mod(l, 1, kc, j), bias=mod(l, 0, kc, j))
                        S.dma("pool", uT_d[s, :, :, t0:t0 + Tn], u[:, :, 0:Tn])

                        def proj(ps_t, cc):
                            for kc in range(KC):
                                S.mm(ps_t[:, 0:Tn], wA[:, kc, cc * 128:(cc + 1) * 128], u[:, kc, 0:Tn],
                                     start=(kc == 0), stop=(kc == KC - 1))

                        def store(dst_ap, v):
                            S.dma("pool", dst_ap, v)

                        plain = [(6, qT_d[s, :, 4, t0:t0 + Tn]), (7, qT_d[s, :, 5, t0:t0 + Tn]),
                                 (8, kT_d[s, :, 2, t0:t0 + Tn]), (9, fT_d[s, :, 0, t0:t0 + Tn]),
                                 (10, fT_d[s, :, 1, t0:t0 + Tn])]
                        for n_, (cc, dst) in enumerate(plain):
                            pt = PSF[n_ % 2]
                            proj(pt, cc)
                            st = stg[nst % 4]
                            nst += 1
                            S.copy("act", st[:, 0:Tn], pt[:, 0:Tn])
                            store(dst, st[:, 0:Tn])
                        roped = [(0, 0, True, qT_d[s, :, 0, t0:t0 + Tn]), (1, 0, True, qT_d[s, :, 1, t0:t0 + Tn]),
                                 (2, 2, True, kT_d[s, :, 0, t0:t0 + Tn]),
                                 (3, None, False, qT_d[s, :, 2, t0:t0 + Tn]),
                                 (4, None, False, qT_d[s, :, 3, t0:t0 + Tn]),
                                 (5, None, False, kT_d[s, :, 1, t0:t0 + Tn])]
                        for n_, (cb_, g0, norm, dst) in enumerate(roped):
                            pa = PSF[n_ % 4]
                            proj(pa, cb_)
                            cosv = ROPE[:, 0, t0:t0 + Tn]
                            st = stg[nst % 4]
                            nst += 1
                            if norm:
                                S.act(sq[:, 0:Tn], pa[:, 0:Tn], AF.Square)
                                S.mm(PSF[4][:, 0:Tn], onesbd, sq[:, 0:Tn])
                                S.act(rs[:, 0:Tn], PSF[4][:, 0:Tn], AF.Sqrt, scale=1.0 / 64.0, bias=EPSV[:, 1:2])
                                S.recip(rs[:, 0:Tn], rs[:, 0:Tn])
                                S.stt(t1[:, 0:Tn], pa[:, 0:Tn], GAIN[:, l, g0:g0 + 1], cosv, ALU.mult, ALU.mult)
                                for B_ in (0, 32, 64, 96):
                                    Bp = B_ ^ 32
                                    S.stt(t2[B_:B_ + 32, 0:Tn], pa[Bp:Bp + 32, 0:Tn], GAIN[Bp:Bp + 32, l, g0:g0 + 1],
                                          ROPE[Bp:Bp + 32, 1, t0:t0 + Tn], ALU.mult, ALU.mult)
                                S.tt("pool", t1[:, 0:Tn], t1[:, 0:Tn], t2[:, 0:Tn], ALU.add)
                                S.tt("dve", st[:, 0:Tn], t1[:, 0:Tn], rs[:, 0:Tn], ALU.mult)
                            else:
                                S.tt("dve", t1[:, 0:Tn], pa[:, 0:Tn], cosv, ALU.mult)
                                for B_ in (0, 32, 64, 96):
                                    Bp = B_ ^ 32
                                    S.tt("dve", t2[B_:B_ + 32, 0:Tn], pa[Bp:Bp + 32, 0:Tn],
                                         ROPE[Bp:Bp + 32, 1, t0:t0 + Tn], ALU.mult)
                                S.tt("pool", st[:, 0:Tn], t1[:, 0:Tn], t2[:, 0:Tn], ALU.add)
                            store(dst, st[:, 0:Tn])
                        vs = vst[ci % 2]
                        nt = Tn // 128
                        for ti in range(nt):
                            pv = PSF[5]
                            for kc in range(KC):
                                S.mm(pv[:, 0:384], u[:, kc, ti * 128:(ti + 1) * 128], wV[:, kc, :],
                                     start=(kc == 0), stop=(kc == KC - 1))
                            S.copy("act", vs[:, ti, :, 0:64], pv[:, 0:384].re("p (m d) -> p m d", d=64))
                        S.dma("pool", v_d[s, :, t0 // 128:t0 // 128 + nt, :].rearrange("p t (m d) -> p t m d", d=128),
                              vs[:, 0:nt, :, :])
            phase_end("p1")

        def p2(s, l):
            with ExitStack() as ps:
                kT = sb([128, 3, NT], BF16, "kT", ps)
                Vt = sb([128, 18, 768], BF16, "Vt", ps)
                TB = sb([128, 4 * 15 * 64], F32, "TB", ps)
                MK = sb([128, 30 * 512], BF16, "MK", ps)
                qcb = [sb([128, 6, 512], BF16, "qc%d" % i, ps) for i in range(2)]
                Pb = [sb([128, 512], BF16, "Pb%d" % i, ps) for i in range(4)]
                Pe = [sb([128, 512], BF16, "Pe%d" % i, ps) for i in range(4)]
                Pm = [sb([128, 512], BF16, "Pm%d" % i, ps) for i in range(3)]
                sbias = [sb([128, 512], F32, "sbias%d" % i, ps) for i in range(3)]
                rd = sb([128, 512], F32, "rd", ps)
                dsb = sb([128, 512], F32, "dsb", ps)
                brst = [sb([128, 512], BF16, "brst%d" % i, ps) for i in range(2)]
                S.dma("sp", kT.full(), kT_d[s, :, :, :])
                S.dma("sp", Vt.full(), v_d[s, :, :, :])
                S.dma("sp", TB.full(), tbs[l, :, :])
                S.dma("sp", MK.full(), maskc[:, :])
                TBv = TB.full().re("p (h m c) -> p h m c", h=4, m=15)

                def winc(qt, dl):
                    i_ = qt * 3 + dl + 1
                    return MK[:, i_ * 512:(i_ + 1) * 512]

                def namc(qt, dl):
                    i_ = 9 + qt * 7 + dl + 3
                    return MK[:, i_ * 512:(i_ + 1) * 512]
                qch = ([(0, 256, True)] if l == 0 else []) + [(256 + i * 512, 512, False) for i in range(4)]
                cnt = {}

                def nxt(k, lst):
                    c_ = cnt.get(k, 0)
                    cnt[k] = c_ + 1
                    return lst[c_ % len(lst)]
                for qi, (t0, Tn, is_ctx) in enumerate(qch):
                    qc = qcb[qi % 2]
                    S.dma("sp", qc[:, :, 0:Tn], qT_d[s, :, :, t0:t0 + Tn])
                    b0 = (t0 - NCX) // 128
                    qt = 0 if b0 == 0 else (2 if b0 == 12 else 1)
                    for m in range(3):
                        for qcnk in range(2):
                            bst = nxt("b", brst)
                            for ph in range(2):
                                hq = qcnk + 2 * ph
                                O = nxt("o", [PSF[3], PSF[4]])
                                p0, p1_ = ph * 64, (ph + 1) * 64
                                qv = qc[p0:p1_, m * 2 + qcnk, 0:Tn]
                                vo = (m * 2 + ph) * 128
                                if is_ctx:
                                    steps = [("g", 0), ("g", 1)]
                                elif m == 0:
                                    steps = [("g", kc) for kc in range(18)]
                                else:
                                    dls = [-1, 0, 1] if m == 1 else _NA_QDELTAS[qt]
                                    steps = [("g", 0), ("g", 1)] + [("b", dl) for dl in dls]

                                def emit_S(st):
                                    Sp = nxt("s", [PSF[0], PSF[1], PSF[2], PSF[5]])
                                    if st[0] == "g":
                                        kc = st[1]
                                        S.mm(Sp[:, 0:Tn], kT[p0:p1_, m, kc * 128:(kc + 1) * 128], qv)
                                    else:
                                        for bi in range(4):
                                            kc = 2 + min(15, max(0, b0 + bi + st[1]))
                                            S.mm(Sp[:, bi * 128:(bi + 1) * 128], kT[p0:p1_, m, kc * 128:(kc + 1) * 128],
                                                 qc[p0:p1_, m * 2 + qcnk, bi * 128:(bi + 1) * 128])
                                    return Sp

                                def emit_rest(st, Sp, first, last):
                                    if st[0] == "g":
                                        kc = st[1]
                                        P = nxt("p", Pb)
                                        S.act(P[:, 0:Tn], Sp[:, 0:Tn], AF.Exp, scale=0.125)
                                        S.mm(O[:, 0:Tn], Vt[:, kc, vo:vo + 128], P[:, 0:Tn], start=first, stop=last)
                                        return
                                    dl = st[1]
                                    pe_ = nxt("e", Pe)
                                    if m == 2:
                                        sbv = nxt("e2", sbias)
                                        m0 = 7 - 2 * dl
                                        for bi in range(4):
                                            S.stt(sbv[:, bi * 128:(bi + 1) * 128].re("p (j c) -> p j c", c=64),
                                                  Sp[:, bi * 128:(bi + 1) * 128].re("p (j c) -> p j c", c=64), 0.125,
                                                  TBv[:, hq, m0:m0 + 2, :], ALU.mult, ALU.add)
                                        S.act(pe_.full(), sbv.full(), AF.Exp)
                                        pfin = nxt("m", Pm)
                                        S.tt("pool", pfin.full(), pe_.full(), namc(qt, dl), ALU.mult)
                                    else:
                                        S.act(pe_.full(), Sp.full(), AF.Exp, scale=0.125)
                                        if dl == 0:
                                            pfin = pe_
                                        else:
                                            pfin = nxt("m", Pm)
                                            S.tt("dve", pfin.full(), pe_.full(), winc(qt, dl), ALU.mult)
                                    valid = [bi for bi in range(4) if 0 <= b0 + bi + dl < 16]
                                    for bi in valid:
                                        kc = 2 + b0 + bi + dl
                                        S.mm(O[:, bi * 128:(bi + 1) * 128], Vt[:, kc, vo:vo + 128],
                                             pfin[:, bi * 128:(bi + 1) * 128], start=False, stop=(last and bi == valid[-1]))
                                n_ = len(steps)
                                LA = 2
                                sps = [None] * n_
                                for i_ in range(min(LA, n_)):
                                    sps[i_] = emit_S(steps[i_])
                                for i_ in range(n_):
                                    if i_ + LA < n_:
                                        sps[i_ + LA] = emit_S(steps[i_ + LA])
                                    emit_rest(steps[i_], sps[i_], i_ == 0, i_ == n_ - 1)
                                if m == 1:
                                    S.ts("dve", dsb[64:128, 0:Tn], O[64:128, 0:Tn],
                                         SINKE[64:128, l * 4 + hq:l * 4 + hq + 1], ALU.add)
                                    S.recip(rd[0:64, 0:Tn], dsb[64:128, 0:Tn])
                                else:
                                    S.recip(rd[0:64, 0:Tn], O[64:128, 0:Tn])
                                S.tt("dve", bst[p0:p1_, 0:Tn], O[0:64, 0:Tn], rd[0:64, 0:Tn], ALU.mult)
                            cidx = (0, 4, 6)[m] + qcnk
                            S.dma("pool", brT_d[s, :, cidx, t0:t0 + Tn], bst[:, 0:Tn])
            phase_end("p2")

        def p2b(s, l):
            with ExitStack() as ps:
                fT = sb([128, 2, NT], BF16, "fT", ps)
                gcs = sb([128, 18, 4, 128], BF16, "gcs", ps)
                Cc = [sb([128, 16, 512], BF16, "Cc%d" % i, ps) for i in range(2)]
                Sc = [sb([128, 16, 512], BF16, "Sc%d" % i, ps) for i in range(2)]
                ost = [sb([128, 512], BF16, "ost%d" % i, ps) for i in range(2)]
                S.dma("sp", fT.full(), fT_d[s, :, :, :])
                tcs = range(18) if l == 0 else range(2, 18)
                for tc in tcs:
                    pg = PSF[tc % 2]
                    for fc in range(2):
                        S.mm(pg[:, (fc * 2) * 128:(fc * 2 + 1) * 128], fT[:, fc, tc * 128:(tc + 1) * 128], c64bd)
                        S.mm(pg[:, (fc * 2 + 1) * 128:(fc * 2 + 2) * 128], fT[:, fc, tc * 128:(tc + 1) * 128], s64bd)
                    S.copy("act" if tc % 2 == 0 else "dve", gcs[:, tc, :, :], pg[:, 0:512].re("p (a c) -> p a c", c=128))
                n = 0
                jobs = [(256 + i * 512, 512, 2, 16, dftc, dfts) for i in range(4)]
                if l == 0:
                    jobs = [(0, 256, 0, 2, dftc_c, dfts_c)] + jobs
                for ji, (t0, Tn, tc0, ntc, mc, msn) in enumerate(jobs):
                    c_, s_ = Cc[ji % 2], Sc[ji % 2]
                    col0 = 0 if t0 == 0 else t0 - NCX
                    S.dma("sp", c_[:, 0:ntc, 0:Tn], mc[:, col0:col0 + Tn].rearrange("(tc p) n -> p tc n", p=128))
                    S.dma("sp", s_[:, 0:ntc, 0:Tn], msn[:, col0:col0 + Tn].rearrange("(tc p) n -> p tc n", p=128))
                    for fc in range(2):
                        po = PSF[2 + (n % 2)]
                        for k_ in range(ntc):
                            S.mm(po[:, 0:Tn], gcs[:, tc0 + k_, fc * 2, :], c_[:, k_, 0:Tn], start=(k_ == 0), stop=False)
                            S.mm(po[:, 0:Tn], gcs[:, tc0 + k_, fc * 2 + 1, :], s_[:, k_, 0:Tn], start=False, stop=(k_ == ntc - 1))
                        o_ = ost[n % 2]
                        n += 1
                        S.copy("act", o_[:, 0:Tn], po[:, 0:Tn])
                        S.dma("pool", brT_d[s, :, 2 + fc, t0:t0 + Tn], o_[:, 0:Tn])
            phase_end("p2b")

        def p3_weights(l, ws):
            wG = sb([128, KC, 4096], BF16, "wG", ws)
            wB = sb([128, KC, D], BF16, "wB", ws)
            wO = sb([128, KC, D], BF16, "wO", ws)
            wR = sb([128, KC, E], F32, "wR", ws)
            cast_load(wG, lambda c0, c1: w_inG[l, :, c0:c1].rearrange("(kc p) n -> p kc n", p=128), 4096)
            cast_load(wB, lambda c0, c1: w_brP[l, :, c0:c1].rearrange("(kc p) n -> p kc n", p=128), D)
            cast_load(wO, lambda c0, c1: w_out[l, :, c0:c1].rearrange("(kc p) n -> p kc n", p=128), D)
            S.dma("sp", wR.full(), w_router[l, :, :].rearrange("(kc p) n -> p kc n", p=128))
            return wG, wB, wO, wR

        def p3(s, l, wts):
            wG, wB, wO, wR = wts
            with ExitStack() as ps:
                TT = 512
                uTc = sb([128, KC, TT], BF16, "uTc3", ps)
                brc = sb([128, 8, TT], BF16, "brc", ps)
                hTc = sb([128, KC, TT], F32, "hTc3", ps)
                zT = sb([128, KC, TT], F32, "zT", ps)
                mrg = sb([128, KC, TT], BF16, "mrg", ps)
                u2f = hTc
                u2b = brc
                gate = [sb([128, TT], F32, "gate%d" % i, ps) for i in range(2)]
                tmpm = [sb([128, TT], F32, "tmpm%d" % i, ps) for i in range(2)]
                acc = sb([128, TT], F32, "acc", ps)
                zsq = [sb([128, TT], F32, "zsq%d" % i, ps) for i in range(2)]
                mean_sb = sb([128, TT], F32, "mean_sb", ps)
                var_sb = sb([128, TT], F32, "var_sb", ps)
                accs = sb([128, TT], F32, "accs", ps)
                accq = sb([128, TT], F32, "accq", ps)
                ex = sb([16, TT], F32, "ex", ps)
                rsum = sb([16, TT], F32, "rsum", ps)
                utok = [sb([128, D], BF16, "utok%d" % i, ps) for i in range(2)]
                chunks = CH512 if l == 0 else CH512[1:]
                ng = 0
                for (t0, Tn) in chunks:
                    j = 2 if t0 == 0 else s
                    S.dma("sp", uTc[:, :, 0:Tn], uT_d[s, :, :, t0:t0 + Tn])
                    S.dma("sp", brc[:, :, 0:Tn], brT_d[s, :, :, t0:t0 + Tn])
                    S.dma("sp", hTc[:, :, 0:Tn], hT_d[s, :, :, t0:t0 + Tn])
                    for dc in range(KC):
                        for i in range(4):
                            pg = PSF[ng % 2]
                            pb = PSF[2 + ng % 2]
                            g_ = gate[ng % 2]
                            tm = tmpm[ng % 2]
                            ng += 1
                            for kc in range(KC):
                                S.mm(pg[:, 0:Tn], wG[:, kc, i * D + dc * 128:i * D + (dc + 1) * 128], uTc[:, kc, 0:Tn],
                                     start=(kc == 0), stop=(kc == KC - 1))
                            S.act(g_[:, 0:Tn], pg[:, 0:Tn], AF.Sigmoid)
                            for hf in range(2):
                                S.mm(pb[:, 0:Tn], wB[:, i * 2 + hf, dc * 128:(dc + 1) * 128], brc[:, i * 2 + hf, 0:Tn],
                                     start=(hf == 0), stop=(hf == 1))
                            if i == 0:
                                S.tt("dve", acc[:, 0:Tn], g_[:, 0:Tn], pb[:, 0:Tn], ALU.mult)
                            else:
                                S.tt("dve", tm[:, 0:Tn], g_[:, 0:Tn], pb[:, 0:Tn], ALU.mult)
                                if i < 3:
                                    S.tt("pool", acc[:, 0:Tn], acc[:, 0:Tn], tm[:, 0:Tn], ALU.add)
                                else:
                                    S.tt("pool", mrg[:, dc, 0:Tn], acc[:, 0:Tn], tm[:, 0:Tn], ALU.add)
                    for dc in range(KC):
                        py = PSF[dc % 2]
                        for kc in range(KC):
                            S.mm(py[:, 0:Tn], wO[:, kc, dc * 128:(dc + 1) * 128], mrg[:, kc, 0:Tn],
                                 start=(kc == 0), stop=(kc == KC - 1))
                        S.stt(zT[:, dc, 0:Tn], py[:, 0:Tn], mod(l, 2, dc, j), hTc[:, dc, 0:Tn], ALU.mult, ALU.add)
                    layernorm(zT, Tn, l, 0, (zsq, mean_sb, var_sb, accs, accq))
                    S.dma("pool", hT_d[s, :, :, t0:t0 + Tn], zT[:, :, 0:Tn])
                    for kc in range(KC):
                        S.act(u2f[:, kc, 0:Tn], zT[:, kc, 0:Tn], AF.Identity, scale=mod(l, 4, kc, j), bias=mod(l, 3, kc, j))
                        S.copy("pool", u2b[:, kc, 0:Tn], u2f[:, kc, 0:Tn])
                    pl = PSF[2]
                    for kc in range(KC):
                        S.mm(pl[0:16, 0:Tn], wR[:, kc, :], u2f[:, kc, 0:Tn], start=(kc == 0), stop=(kc == KC - 1))
                    S.act(ex[:, 0:Tn], pl[0:16, 0:Tn], AF.Exp)
                    S.mm(PSF[3][0:16, 0:Tn], ones_f[0:16, 0:16], ex[:, 0:Tn])
                    S.recip(rsum[:, 0:Tn], PSF[3][0:16, 0:Tn])
                    S.tt("dve", ex[:, 0:Tn], ex[:, 0:Tn], rsum[:, 0:Tn], ALU.mult)
                    S.dma("pool", aff_d[s, :, t0:t0 + Tn], ex[:, 0:Tn])
                    for ti in range(Tn // 128):
                        pt = PSB[ti % 2]
                        ut = utok[ti % 2]
                        for kc in range(KC):
                            S.tr(pt[:, kc * 128:(kc + 1) * 128], u2b[:, kc, ti * 128:(ti + 1) * 128], ident_b)
                        S.copy("act", ut.full(), pt.full())
                        r0 = s * NT + t0 + ti * 128
                        S.dma("pool", u2tok_d[r0:r0 + 128, :], ut.full())
            phase_end("p3")

        IDXT = sb([128, 5, E], I32, "IDXT")
        SELW = sb([128, 5, E], F32, "SELW")

        def p4(l):
            with ExitStack() as ps:
                work = sb([48, NL], F32, "work", ps)
                workc = sb([48, NCX], F32, "workc", ps)
                vals = sb([48, 288], F32, "vals", ps)
                idxu = sb([48, 288], U32, "idxu", ps)
                idxf = sb([48, 288], F32, "idxf", ps)
                lo = sb([16, 2, 288], F32, "lo", ps)
                tmpi = sb([128, E], F32, "tmpi", ps)
                S.memset("dve", work.full(), 0.0)
                S.memset("dve", workc.full(), 0.0)
                S.memset("dve", vals.full(), 0.0)
                S.memset("dve", idxu.full(), 0)
                for s in range(2):
                    S.dma("sp", work[32 * s:32 * s + 16, :], aff_d[s, :, NCX:NT])
                    if l == 0:
                        S.dma("sp", workc[32 * s:32 * s + 16, :], aff_d[s, :, 0:NCX])
                jobs = [(work, 0, 32)] + ([(workc, 256, 4)] if l == 0 else [])
                for (wt, slot0, rounds) in jobs:
                    for r_ in range(rounds):
                        sl = slice(slot0 + r_ * 8, slot0 + r_ * 8 + 8)
                        mx, ix, wk = _ap(vals[:, sl]), _ap(idxu[:, sl]), _ap(wt.full())
                        S.op("dve", lambda e, mx=mx, wk=wk: e.max(out=mx, in_=wk), reads=[wt.full()], writes=[vals.full()])
                        S.op("dve", lambda e, mx=mx, ix=ix, wk=wk: e.max_index(out=ix, in_max=mx, in_values=wk),
                             reads=[wt.full(), vals.full()], writes=[idxu.full()])
                        S.op("dve", lambda e, mx=mx, wk=wk: e.match_replace(out=wk, in_to_replace=mx, in_values=wk, imm_value=-1.0),
                             reads=[wt.full(), vals.full()], writes=[wt.full()])
                S.copy("dve", idxf.full(), idxu.full())
                S.ts("dve", idxf[0:16, 0:256], idxf[0:16, 0:256], float(NCX), ALU.add)
                S.ts("dve", idxf[32:48, 0:256], idxf[32:48, 0:256], float(NT + NCX), ALU.add)
                S.ts("dve", idxf[32:48, 256:288], idxf[32:48, 256:288], float(NT), ALU.add)
                S.copy("dve", lo[:, 0, :], idxf[32:48, :])
                S.copy("dve", lo[:, 1, :], vals[32:48, :])
                srcs = [(idxf[0:16, :], vals[0:16, :]), (lo[:, 0, :], lo[:, 1, :])]
                n = 0
                for s in range(2):
                    si, sv = srcs[s]
                    for half in range(2):
                        st = s * 2 + half
                        pt = PSF[n % 2]
                        n += 1
                        S.tr(pt[:, 0:16], si[:, half * 128:(half + 1) * 128], ident_f[0:16, 0:16])
                        S.copy("dve", tmpi.full(), pt[:, 0:16])
                        S.copy("dve", IDXT[:, st, :], tmpi.full())
                        S.tr(pt[:, 16:32], sv[:, half * 128:(half + 1) * 128], ident_f[0:16, 0:16])
                        S.copy("dve", SELW[:, st, :], pt[:, 16:32])
                    if l == 0:
                        pt = PSF[n % 2]
                        n += 1
                        S.tr(pt[0:32, 0:16], si[:, 256:288], ident_f[0:16, 0:16])
                        S.copy("dve", tmpi[0:32, :], pt[0:32, 0:16])
                        S.copy("dve", IDXT[32 * s:32 * s + 32, 4, :], tmpi[0:32, :])
                        S.tr(pt[0:32, 16:32], sv[:, 256:288], ident_f[0:16, 0:16])
                        S.copy("dve", SELW[32 * s:32 * s + 32, 4, :], pt[0:32, 16:32])
            phase_end("p4")

        def p5_weights(l, ws):
            wg = [sb([128, KC, D], BF16, "wg%d" % i, ws) for i in range(2)]
            wu = [sb([128, KC, D], BF16, "wu%d" % i, ws) for i in range(2)]
            wd = [sb([128, KC, D], BF16, "wd%d" % i, ws) for i in range(2)]
            for wt, src in ((wg[0], w_gate), (wu[0], w_up), (wd[0], w_down)):
                S.dma("pool", wt.full(), src[l, 0, :, :].rearrange("(kc p) n -> p kc n", p=128))
            return wg, wu, wd

        def p5(l, wts):
            wg, wu, wd = wts
            with ExitStack() as ps:
                xs = [sb([128, 5, D], BF16, "xs%d" % i, ps) for i in range(2)]
                xsT = sb([128, KC, 640], BF16, "xsT", ps)
                actT = sb([128, KC, 640], BF16, "actT", ps)
                sa = [sb([128, 512], F32, "sa%d" % i, ps) for i in range(2)]
                ysb = [sb([128, D], F32, "ysb%d" % i, ps) for i in range(2)]
                zer = sb([128, 2, D], F32, "zer", ps)
                FFN = T(ffn_d, "ffn_d")
                U2 = T(u2tok_d, "u2tok_d")
                S.memset("dve", zer.full(), 0.0)
                for i in range(2 * NT // 256):
                    S.dma("sp", V(FFN, ffn_d[i * 256:(i + 1) * 256, :].rearrange("(a p) d -> p a d", p=128)), zer.full())
                sts = [(0, 128), (1, 128), (2, 128), (3, 128)] + ([(4, 64)] if l == 0 else [])
                nsl = 576 if l == 0 else 512
                cgs = [(0, 512)] + ([(512, 576)] if l == 0 else [])

                def loadw(e_):
                    b_ = e_ % 2
                    for wt, src in ((wg[b_], w_gate), (wu[b_], w_up), (wd[b_], w_down)):
                        S.dma("pool", wt.full(), src[l, e_, :, :].rearrange("(kc p) n -> p kc n", p=128))

                def gathers(e_):
                    x_ = xs[e_ % 2]
                    for (st, np_) in sts:
                        o_ = _ap(x_[0:np_, st, :])
                        ix = _ap(IDXT[0:np_, st, e_:e_ + 1])

                        def fn(eng, o_=o_, ix=ix):
                            return eng.indirect_dma_start(out=o_, out_offset=None, in_=u2tok_d[:, :],
                                                          in_offset=bass.IndirectOffsetOnAxis(ap=ix, axis=0))
                        S.dma("pool", x_[0:np_, st, :], V(U2, None), extra_reads=[IDXT.full()], fn=fn)
                gathers(0)
                ny = 0
                for e_ in range(E):
                    if e_ + 1 < E:
                        loadw(e_ + 1)
                        gathers(e_ + 1)
                    b_ = e_ % 2
                    x_ = xs[b_]
                    for (st, np_) in sts:
                        pt = PSB[st % 2]
                        for kc in range(KC):
                            S.tr(pt[:, kc * 128:kc * 128 + np_], x_[0:np_, st, kc * 128:(kc + 1) * 128], ident_b[0:np_, 0:np_])
                        S.copy("act" if st % 2 == 0 else "dve", xsT[:, :, st * 128:st * 128 + np_],
                               pt.full().re("p (k t) -> p k t", t=128)[:, :, 0:np_])
                    for fc in range(KC):
                        for (c0, c1) in cgs:
                            pa, pu = PSF[(fc % 2) * 2], PSF[(fc % 2) * 2 + 1]
                            w_ = c1 - c0
                            for kc in range(KC):
                                S.mm(pa[:, 0:w_], wg[b_][:, kc, fc * 128:(fc + 1) * 128], xsT[:, kc, c0:c1],
                                     start=(kc == 0), stop=(kc == KC - 1))
                            for kc in range(KC):
                                S.mm(pu[:, 0:w_], wu[b_][:, kc, fc * 128:(fc + 1) * 128], xsT[:, kc, c0:c1],
                                     start=(kc == 0), stop=(kc == KC - 1))
                            s_ = sa[fc % 2]
                            S.act(s_[:, 0:w_], pa[:, 0:w_], AF.Silu)
                            S.tt("dve", actT[:, fc, c0:c1], s_[:, 0:w_], pu[:, 0:w_], ALU.mult)
                    for (st, np_) in sts:
                        y_ = ysb[ny % 2]
                        ny += 1
                        for dh in range(2):
                            py = PSF[4 + dh]
                            for fc in range(KC):
                                S.mm(py[0:np_, :], actT[:, fc, st * 128:st * 128 + np_], wd[b_][:, fc, dh * 512:(dh + 1) * 512],
                                     start=(fc == 0), stop=(fc == KC - 1))
                            S.act(y_[0:np_, dh * 512:(dh + 1) * 512], py[0:np_, :], AF.Copy, scale=SELW[0:np_, st, e_:e_ + 1])
                        yi = _ap(y_[0:np_, :])
                        ix = _ap(IDXT[0:np_, st, e_:e_ + 1])

                        def fn(eng, yi=yi, ix=ix):
                            return eng.indirect_dma_start(out=ffn_d[:, :], out_offset=bass.IndirectOffsetOnAxis(ap=ix, axis=0),
                                                          in_=yi, in_offset=None, compute_op=ALU.add)
                        S.dma("pool", V(FFN, None), y_[0:np_, :], extra_reads=[IDXT.full(), V(FFN, None)], fn=fn)
            phase_end("p5")

        def p6(s, l):
            with ExitStack() as ps:
                ftok = sb([128, 4, D], F32, "ftok", ps)
                hTc = sb([128, KC, 512], F32, "hTc6", ps)
                zT = sb([128, KC, 512], F32, "zT6", ps)
                zsq = [sb([128, 512], F32, "zsq6%d" % i, ps) for i in range(2)]
                mean_sb = sb([128, 512], F32, "mean6", ps)
                var_sb = sb([128, 512], F32, "var6", ps)
                accs = sb([128, 512], F32, "accs6", ps)
                accq = sb([128, 512], F32, "accq6", ps)
                otok = [sb([128, D], F32, "otok%d" % i, ps) for i in range(2)]
                chunks = CH512 if l == 0 else CH512[1:]
                for (t0, Tn) in chunks:
                    j = 2 if t0 == 0 else s
                    nt = Tn // 128
                    r0 = s * NT + t0
                    S.dma("sp", ftok[:, 0:nt, :], ffn_d[r0:r0 + Tn, :].rearrange("(ti p) d -> p ti d", p=128))
                    S.dma("sp", hTc[:, :, 0:Tn], hT_d[s, :, :, t0:t0 + Tn])
                    for kc in range(KC):
                        pk = PSF[kc % 4]
                        for ti in range(nt):
                            S.tr(pk[:, ti * 128:(ti + 1) * 128], ftok[:, ti, kc * 128:(kc + 1) * 128], ident_f)
                        S.stt(zT[:, kc, 0:Tn], pk[:, 0:Tn], mod(l, 5, kc, j), hTc[:, kc, 0:Tn], ALU.mult, ALU.add)
                    layernorm(zT, Tn, l, 2, (zsq, mean_sb, var_sb, accs, accq))
                    if l < L - 1:
                        S.dma("pool", hT_d[s, :, :, t0:t0 + Tn], zT[:, :, 0:Tn])
                    else:
                        for ti in range(nt):
                            ot = otok[ti % 2]
                            for hf in range(2):
                                po = PSF[hf]
                                for k4 in range(4):
                                    kc = hf * 4 + k4
                                    S.tr(po[:, k4 * 128:(k4 + 1) * 128], zT[:, kc, ti * 128:(ti + 1) * 128], ident_f)
                                S.copy("act" if hf == 0 else "dve", ot[:, hf * 512:(hf + 1) * 512], po.full())
                            S.dma("pool", out[s, t0 - NCX + ti * 128:t0 - NCX + (ti + 1) * 128, :], ot.full())
            phase_end("p6")

        done = False
        for l in range(L):
            p1(l, (0,) if stop_after == "p1" else (0, 1))
            if stop_after == "p1":
                break
            for s in range(2):
                p2(s, l)
                with ExitStack() as ws:
                    wts = p3_weights(l, ws) if stop_after != "p2" else None
                    p2b(s, l)
                    if stop_after == "p2":
                        done = True
                        break
                    p3(s, l, wts)
                if stop_after == "p3":
                    done = True
                    break
            if done:
                break
            with ExitStack() as ws:
                wts5 = p5_weights(l, ws)
                p4(l)
                p5(l, wts5)
            if stop_after == "p5":
                break
            for s in range(2):
                p6(s, l)
            if stop_after == "l0":
                break
        S.barrier()
        S.emit()
        print("bass ops:", S.nops, {e: len(S.items[e]) for e in ENG})
        _CACHE["marks"] = marks
    return nc


_CACHE = {}


def _host_shared(inp):
    f = np.float32
    sh = dict(_consts())
    cols, vcols = _wina_cols()
    w_in = np.asarray(inp["w_in"], f)
    sh["w_inA"] = np.ascontiguousarray(w_in[:, :, cols])
    sh["w_inV"] = np.ascontiguousarray(w_in[:, :, vcols])
    sh["w_inG"] = np.ascontiguousarray(w_in[:, :, 1792:])
    sh["w_mod"] = np.asarray(inp["w_mod"], f)
    sh["b_modT"] = np.ascontiguousarray(np.asarray(inp["b_mod"], f).reshape(L, 48, 128).transpose(2, 0, 1))
    g = np.asarray(inp["qk_gain"], f)
    d_ = np.array([_ORD[p % 64] for p in range(128)])
    gt = np.stack([g[:, 0, d_], g[:, 0, d_], g[:, 1, d_], g[:, 1, d_]], -1)
    sh["gainT"] = np.ascontiguousarray(gt.transpose(1, 0, 2))
    sk = np.asarray(inp["sink_logit"], f).reshape(1, L * 4)
    sh["sinkB"] = np.ascontiguousarray(np.broadcast_to(sk, (128, L * 4)))
    rpb = np.asarray(inp["na_rpb"], f)
    p = np.arange(128)
    kcol = (p % 64)[:, None, None]
    i_ = (p // 64)[:, None, None]
    m_ = np.arange(15)[None, :, None]
    c_ = np.arange(64)[None, None, :]
    a_ = np.clip(14 - m_ + i_, 0, 14) + 0 * c_
    oc = np.clip(kcol - c_ + 15, 0, 30) + 0 * m_
    tb = rpb[:, :, a_, oc]
    sh["tbs"] = np.ascontiguousarray(tb.transpose(0, 2, 1, 3, 4).reshape(L, 128, 4 * 15 * 64))
    wb = np.asarray(inp["w_branch"], f)
    perm = np.concatenate([np.arange(0, 64), np.arange(128, 192), np.arange(64, 128), np.arange(192, 256)])
    wbp = wb.copy()
    for i in (0, 2, 3):
        wbp[:, i] = wb[:, i][:, perm]
    sh["w_brP"] = np.ascontiguousarray(wbp.reshape(L, 1024, D))
    sh["w_out"] = np.asarray(inp["w_out"], f)
    ln = np.stack([inp["ln1_g"], inp["ln1_b"], inp["ln2_g"], inp["ln2_b"]], 1).astype(f)
    sh["lnT"] = np.ascontiguousarray(ln.reshape(L, 4, KC, 128).transpose(3, 0, 1, 2).reshape(128, L * 4 * KC))
    sh["w_router"] = np.asarray(inp["w_router"], f)
    sh["w_gate"] = np.asarray(inp["w_gate"], f)
    sh["w_up"] = np.asarray(inp["w_up"], f)
    sh["w_down"] = np.asarray(inp["w_down"], f)
    return sh


def _host_core(inp, core):
    f = np.float32
    b0 = core * 2
    x = np.asarray(inp["x"], f)
    ctx = np.asarray(inp["ctx"], f)
    c = np.asarray(inp["c"], f)
    cc = np.asarray(inp["c_ctx"], f)
    d = {}
    d["x2"] = np.ascontiguousarray(np.concatenate([ctx[b0:b0 + 2], x[b0:b0 + 2]], axis=1))
    cv = np.stack([c[b0], c[b0 + 1], cc, np.zeros_like(cc)], -1)
    d["cT"] = np.ascontiguousarray(cv.reshape(KC, 128, 4).transpose(1, 0, 2))
    return d


def kernel(**inputs):
    n = 8
    if "nc" not in _CACHE:
        _CACHE["nc"] = build_program(debug=False)
    nc = _CACHE["nc"]
    sh = _host_shared(inputs)
    in_maps = []
    for core in range(n):
        m = dict(sh)
        m.update(_host_core(inputs, core))
        in_maps.append(m)
    res = run_bass_kernel_spmd(nc, in_maps, core_ids=list(range(n)))
    _CACHE["last"] = res
    outs = [np.asarray(r["out"], np.float32) for r in res.results]
    return np.concatenate(outs, axis=0)
```

```python
import numpy as np
from contextlib import ExitStack
import ml_dtypes
import concourse.bass as bass
import concourse.mybir as mybir
from concourse.bass_utils import run_bass_kernel_spmd

F32 = mybir.dt.float32
BF16 = mybir.dt.bfloat16
U32 = mybir.dt.uint32
I32 = mybir.dt.int32
ALU = mybir.AluOpType
AF = mybir.ActivationFunctionType

L = 2
D = 1024
KC = 8
NT = 2304
NCX = 256
NL = 2048
E = 16
ALPHA = (2 * L) ** 0.25
LN_EPS_S = 1e-6 / (ALPHA * ALPHA)
RMS_EPS = 1e-6
NA_COLS = 11 * 128
ENG = ("pe", "act", "dve", "pool", "sp")


class T:
    __slots__ = ("h", "w", "r", "name")

    def __init__(self, h, name=""):
        self.h = h
        self.w = {}
        self.r = {}
        self.name = name

    def __getitem__(self, idx):
        return V(self, self.h[idx])

    def full(self):
        return V(self, self.h[:])


class V:
    __slots__ = ("t", "ap")

    def __init__(self, t, ap):
        self.t = t
        self.ap = ap

    def re(self, pat, **kw):
        return V(self.t, self.ap.rearrange(pat, **kw))

    def __getitem__(self, idx):
        return V(self.t, self.ap[idx])


def _ap(x):
    return x.ap if isinstance(x, V) else x


def _ts(xs):
    return [x.t for x in xs if isinstance(x, V)]


class Sched:
    def __init__(self, nc, es, n_dma_sems=(("sp", 12), ("pool", 10), ("act", 4))):
        self.nc = nc
        self.es = es
        self.items = {e: [] for e in ENG}
        self.sems = {}
        self.cnt = {}
        self.seen = {e: {} for e in ENG}
        for e in ENG:
            self._mk("c_" + e)
        self.dma_pool = {}
        self.dma_rr = {}
        for e, n in n_dma_sems:
            self.dma_pool[e] = [self._mk("d_%s%d" % (e, i)) for i in range(n)]
            self.dma_rr[e] = 0
        self.nops = 0

    def _mk(self, key):
        self.sems[key] = self.es.enter_context(self.nc.semaphore(key))
        self.cnt[key] = 0
        return key

    def _need(self, e, key, val):
        if val <= 0 or self.seen[e].get(key, 0) >= val:
            return
        self.seen[e][key] = val
        self.items[e].append(("wait", key, val))

    def _deps(self, e, reads, writes, is_pe=False, is_dma=False):
        own = "c_" + e if not is_dma else "__none__"
        for t in reads:
            for k, v in t.w.items():
                if is_pe and k == own:
                    continue
                self._need(e, k, v)
        for t in writes:
            for k, v in t.r.items():
                if k != own:
                    self._need(e, k, v)
            for k, v in t.w.items():
                if k != own:
                    self._need(e, k, v)

    def _mark(self, key, val, reads, writes):
        for t in reads:
            t.r[key] = max(t.r.get(key, 0), val)
        for t in writes:
            if t.r:
                t.r = {}
                t.w = {}
            t.w[key] = max(t.w.get(key, 0), val)

    def op(self, e, fn, reads=(), writes=()):
        reads = _ts(reads)
        writes = _ts(writes)
        self._deps(e, reads, writes, is_pe=(e == "pe"))
        key = "c_" + e
        self.cnt[key] += 1
        self.items[e].append(("op", fn, key, 1))
        self._mark(key, self.cnt[key], reads, writes)
        self.nops += 1

    def dma(self, e, out, in_, extra_reads=(), fn=None, **kw):
        reads = _ts([in_] + list(extra_reads))
        writes = _ts([out])
        pool = self.dma_pool[e]
        key = pool[self.dma_rr[e] % len(pool)]
        self.dma_rr[e] += 1
        self._need(e, key, self.cnt[key])
        self._deps(e, reads, writes, is_dma=True)
        self.cnt[key] += 16
        if fn is None:
            o, i = _ap(out), _ap(in_)

            def fn(eng, o=o, i=i, kw=kw):
                return eng.dma_start(out=o, in_=i, **kw)
        self.items[e].append(("op", fn, key, 16))
        self._mark(key, self.cnt[key], reads, writes)
        self.nops += 1

    def barrier(self):
        for e in ENG:
            for k, v in self.cnt.items():
                if k != "c_" + e:
                    self._need(e, k, v)

    def emit(self):
        nc = self.nc
        with nc.Block() as block:
            def run(e):
                def body(eng):
                    for it in self.items[e]:
                        if it[0] == "wait":
                            eng.wait_ge(self.sems[it[1]], it[2])
                        else:
                            it[1](eng).then_inc(self.sems[it[2]], it[3])
                return body
            block.tensor(run("pe"))
            block.scalar(run("act"))
            block.vector(run("dve"))
            block.gpsimd(run("pool"))
            block.sync(run("sp"))

    def mm(self, out, lhsT, rhs, start=True, stop=True):
        o, a, b = _ap(out), _ap(lhsT), _ap(rhs)
        self.op("pe", lambda e: e.matmul(o, a, b, start=start, stop=stop), reads=[lhsT, rhs], writes=[out])

    def tr(self, out, in_, ident):
        o, a, b = _ap(out), _ap(in_), _ap(ident)
        self.op("pe", lambda e: e.transpose(o, a, b), reads=[in_, ident], writes=[out])

    def act(self, out, in_, func, scale=1.0, bias=0.0):
        o, i = _ap(out), _ap(in_)
        sc, bi = _ap(scale), _ap(bias)
        self.op("act", lambda e: e.activation(out=o, in_=i, func=func, bias=bi, scale=sc),
                reads=[in_, scale, bias], writes=[out])

    def tt(self, eng, out, a, b, op):
        o, x, y = _ap(out), _ap(a), _ap(b)
        self.op(eng, lambda e: e.tensor_tensor(o, x, y, op), reads=[a, b], writes=[out])

    def ts(self, eng, out, a, s1, op0, s2=None, op1=None):
        o, x, p1, p2 = _ap(out), _ap(a), _ap(s1), _ap(s2)
        if op1 is None:
            self.op(eng, lambda e: e.tensor_scalar(o, x, p1, None, op0), reads=[a, s1], writes=[out])
        else:
            self.op(eng, lambda e: e.tensor_scalar(o, x, p1, p2, op0, op1), reads=[a, s1, s2], writes=[out])

    def stt(self, out, a, s, b, op0, op1):
        o, x, p, y = _ap(out), _ap(a), _ap(s), _ap(b)
        self.op("dve", lambda e: e.scalar_tensor_tensor(o, x, p, y, op0, op1), reads=[a, s, b], writes=[out])

    def copy(self, eng, out, in_):
        o, i = _ap(out), _ap(in_)
        if eng == "act":
            self.op("act", lambda e: e.activation(out=o, in_=i, func=AF.Copy), reads=[in_], writes=[out])
        else:
            self.op(eng, lambda e: e.tensor_copy(o, i), reads=[in_], writes=[out])

    def recip(self, out, in_):
        o, i = _ap(out), _ap(in_)
        self.op("dve", lambda e: e.reciprocal(o, i), reads=[in_], writes=[out])

    def memset(self, eng, out, val):
        o = _ap(out)
        self.op(eng, lambda e: e.memset(o, val), writes=[out])


def _partner(d):
    return d + 16 if (d % 32) < 16 else d - 16


_ORD = list(range(0, 16)) + list(range(32, 48)) + list(range(16, 32)) + list(range(48, 64))


def _rope_tables():
    cos = np.ones((128, NT), np.float32)
    sin = np.zeros((128, NT), np.float32)
    t = np.arange(NL)
    inv = (np.float32(10000.0) ** (-np.arange(0, 32, 2, dtype=np.float32) / np.float32(32))).astype(np.float32)
    for p in range(128):
        d = _ORD[p % 64]
        pos = (t // 64) if d < 32 else (t % 64)
        ang = pos.astype(np.float32) * inv[d % 16]
        cos[p, NCX:] = np.cos(ang).astype(np.float32)
        sgn = -1.0 if (d % 32) < 16 else 1.0
        sin[p ^ 32, NCX:] = sgn * np.sin(ang).astype(np.float32)
    return cos, sin


def _na_masks():
    rows, W, kh, kw = 32, 64, 8, 16
    t = np.arange(NL)
    r, c = t // W, t % W
    r0 = np.clip(r - kh // 2, 0, rows - kh)
    c0 = np.clip(c - kw // 2, 0, W - kw)
    k = np.arange(NL)
    kr, kcol = k // W, k % W
    valid = ((kr[None, :] >= r0[:, None]) & (kr[None, :] < r0[:, None] + kh) &
             (kcol[None, :] >= c0[:, None]) & (kcol[None, :] < c0[:, None] + kw))
    full = np.zeros((16, 7, 128, 128), np.float32)
    for b in range(16):
        for dl in range(-3, 4):
            kci = b + dl
            if 0 <= kci < 16:
                full[b, dl + 3] = valid[b * 128:(b + 1) * 128, kci * 128:(kci + 1) * 128].T
    types = [0, 1] + [2] * 12 + [3, 4]
    rep = {0: 0, 1: 1, 2: 5, 3: 14, 4: 15}
    for b in range(16):
        assert np.array_equal(full[b], full[rep[types[b]]]), b
    namc = np.zeros((3, 7, 128, 512), np.float32)
    for qt, b0 in enumerate((0, 4, 12)):
        for bi in range(4):
            namc[qt, :, :, bi * 128:(bi + 1) * 128] = full[b0 + bi]
    for b0 in (4, 8):
        for bi in range(4):
            assert np.array_equal(full[b0 + bi], full[5])
    qdl = [[dl for dl in range(-3, 4) if namc[qt, dl + 3].any()] for qt in range(3)]
    return namc, qdl


def _dft(n):
    t = np.arange(n, dtype=np.int64)
    m = (t[:, None] * t[None, :]) % n
    ang = 2.0 * np.pi * m.astype(np.float64) / n
    return np.cos(ang), np.sin(ang)


_NA_MASKC, _NA_QDELTAS = _na_masks()


def _consts():
    c = {}
    cos, sin = _rope_tables()
    c["rope"] = np.stack([cos, sin], 1).copy()
    ident = np.eye(128, dtype=np.float32)
    onesbd = np.zeros((128, 128), np.float32)
    onesbd[:64, :64] = 1.0
    onesbd[64:, 64:] = 1.0
    c64, s64 = _dft(64)
    cbd = np.zeros((128, 128), np.float64)
    sbd = np.zeros((128, 128), np.float64)
    for g in range(2):
        cbd[g * 64:(g + 1) * 64, g * 64:(g + 1) * 64] = c64 / 8.0
        sbd[g * 64:(g + 1) * 64, g * 64:(g + 1) * 64] = s64 / 8.0
    win = np.zeros((3, 128, 128), np.float32)
    i = np.arange(128)[:, None]
    j = np.arange(128)[None, :]
    win[0] = (j <= i)
    win[1] = 1.0
    win[2] = (i <= j)
    permm = np.zeros((128, 128), np.float32)
    for m_ in range(128):
        permm[(m_ // 64) * 64 + _partner(m_ % 64), m_] = 1.0
    cb = np.concatenate([ident, onesbd, cbd.astype(np.float32), sbd.astype(np.float32), permm], axis=1)
    c["cbf"] = cb.astype(ml_dtypes.bfloat16)
    winc = np.zeros((3, 3, 128, 512), np.float32)
    for qt, b0 in enumerate((0, 4, 12)):
        for bi in range(4):
            for dl in (-1, 0, 1):
                if 0 <= b0 + bi + dl < 16:
                    winc[qt, dl + 1, :, bi * 128:(bi + 1) * 128] = win[dl + 1]
    mk = np.concatenate([winc[qt, d_] for qt in range(3) for d_ in range(3)] +
                        [_NA_MASKC[qt, d_] for qt in range(3) for d_ in range(7)], axis=1)
    c["maskc"] = mk.astype(ml_dtypes.bfloat16)
    cf = np.concatenate([ident, np.full((128, 128), 1.0 / D, np.float32), np.ones((128, 128), np.float32)], axis=1)
    c["cf32"] = cf.astype(np.float32)
    cs, ss = _dft(NL)
    c["dftc"] = (cs / np.sqrt(NL)).astype(ml_dtypes.bfloat16)
    c["dfts"] = (-ss / np.sqrt(NL)).astype(ml_dtypes.bfloat16)
    cs, ss = _dft(NCX)
    c["dftc_c"] = (cs / np.sqrt(NCX)).astype(ml_dtypes.bfloat16)
    c["dfts_c"] = (-ss / np.sqrt(NCX)).astype(ml_dtypes.bfloat16)
    return c


def _wina_cols():
    offs = {"qA": 0, "kA": 256, "vA": 384, "f": 512, "qC": 768, "kC": 1024, "vC": 1152,
            "qD": 1280, "kD": 1536, "vD": 1664}

    def qch(base, pair):
        return [base + hq * 64 + _ORD[p] for hq in pair for p in range(64)]
    cols = []
    for mname in ("A", "C", "D"):
        qb, kb = offs["q" + mname], offs["k" + mname]
        cols += qch(qb, (0, 2)) + qch(qb, (1, 3)) + qch(kb, (0, 1))
    cols += list(range(offs["f"], offs["f"] + 256))
    vcols = list(range(384, 512)) + list(range(1152, 1280)) + list(range(1664, 1792))
    return np.array(cols), np.array(vcols)


def build_program(debug=False, stop_after=None):
    nc = bass.Bass("TRN2", target_bir_lowering=False)

    def din(name, shape, dt=F32):
        return nc.dram_tensor(name, list(shape), dt, kind="ExternalInput").ap()

    def dscr(name, shape, dt):
        return nc.dram_tensor(name, list(shape), dt, kind=("ExternalOutput" if debug else "Internal")).ap()

    x2 = din("x2", [2, NT, D])
    cT = din("cT", [128, KC, 4])
    w_mod = din("w_mod", [L, D, 6 * D])
    b_modT = din("b_modT", [128, L, 48])
    w_inA = din("w_inA", [L, D, NA_COLS])
    w_inV = din("w_inV", [L, D, 384])
    w_inG = din("w_inG", [L, D, 4096])
    gainT = din("gainT", [128, L, 4])
    sinkB = din("sinkB", [128, L * 4])
    tbs = din("tbs", [L, 128, 4 * 15 * 64])
    rope = din("rope", [128, 2, NT])
    cbf = din("cbf", [128, 5 * 128], BF16)
    maskc = din("maskc", [128, 30 * 512], BF16)
    cf32 = din("cf32", [128, 384])
    dftc = din("dftc", [NL, NL], BF16)
    dfts = din("dfts", [NL, NL], BF16)
    dftc_c = din("dftc_c", [NCX, NCX], BF16)
    dfts_c = din("dfts_c", [NCX, NCX], BF16)
    w_brP = din("w_brP", [L, 1024, D])
    w_out = din("w_out", [L, D, D])
    lnT = din("lnT", [128, L * 4 * KC])
    w_router = din("w_router", [L, D, E])
    w_gate = din("w_gate", [L, E, D, D])
    w_up = din("w_up", [L, E, D, D])
    w_down = din("w_down", [L, E, D, D])
    out = nc.dram_tensor("out", [2, NL, D], F32, kind="ExternalOutput").ap()

    hT_d = dscr("hT_d", [2, 128, KC, NT], F32)
    uT_d = dscr("uT_d", [2, 128, KC, NT], BF16)
    qT_d = dscr("qT_d", [2, 128, 6, NT], BF16)
    kT_d = dscr("kT_d", [2, 128, 3, NT], BF16)
    v_d = dscr("v_d", [2, 128, 18, 768], BF16)
    fT_d = dscr("fT_d", [2, 128, 2, NT], BF16)
    brT_d = dscr("brT_d", [2, 128, 8, NT], BF16)
    u2tok_d = dscr("u2tok_d", [2 * NT, D], BF16)
    ffn_d = dscr("ffn_d", [2 * NT, D], F32)
    aff_d = dscr("aff_d", [2, 16, NT], F32)

    CH512 = [(0, 256)] + [(256 + i * 512, 512) for i in range(4)]
    CH256 = [(i * 256, 256) for i in range(9)]

    with ExitStack() as es:
        S = Sched(nc, es)

        uid = [0]

        def sb(shape, dt, name, stack=es):
            uid[0] += 1
            nm = "%s_%d" % (name, uid[0])
            return T(stack.enter_context(nc.sbuf_tensor(nm, list(shape), dt)), nm)

        PSF = [T(es.enter_context(nc.psum_tensor("psf%d" % i, [128, 512], F32)), "psf%d" % i) for i in range(6)]
        PSB = [T(es.enter_context(nc.psum_tensor("psb%d" % i, [128, 1024], BF16)), "psb%d" % i) for i in range(2)]

        CB = sb([128, 5 * 128], BF16, "CB")
        CF = sb([128, 384], F32, "CF")
        MOD = sb([128, L, 48, 4], F32, "MOD")
        LNT = sb([128, L * 4 * KC], F32, "LNT")
        GAIN = sb([128, L, 4], F32, "GAIN")
        SINKE = sb([128, L * 4], F32, "SINKE")
        S.dma("sp", CB.full(), cbf[:, :])
        S.dma("sp", CF.full(), cf32[:, :])
        S.dma("sp", LNT.full(), lnT[:, :])
        S.dma("sp", GAIN.full(), gainT[:, :, :])
        S.dma("sp", SINKE.full(), sinkB[:, :])
        S.act(SINKE.full(), SINKE.full(), AF.Exp)
        ident_b = CB[:, 0:128]
        onesbd = CB[:, 128:256]
        c64bd = CB[:, 256:384]
        s64bd = CB[:, 384:512]
        permm = CB[:, 512:640]

        ident_f = CF[:, 0:128]
        ones_mean = CF[:, 128:256]
        ones_f = CF[:, 256:384]

        def lnv(l, which, kc):
            i_ = (l * 4 + which) * KC + kc
            return LNT[:, i_:i_ + 1]

        def mod(l, kind, kc, j):
            return MOD[:, l, kind * 8 + kc, j:j + 1]

        def cast_load(dst, src_ap_fn, ncols, eng="pool", step=1024):
            for c0 in range(0, ncols, step):
                c1 = min(ncols, c0 + step)
                S.dma(eng, dst[:, :, c0:c1], src_ap_fn(c0, c1))

        marks = []

        def phase_end(name="?"):
            S.barrier()
            marks.append((name, S.cnt["c_pe"]))

        with ExitStack() as ps:
            cTs = sb([128, KC, 4], F32, "cTs", ps)
            bm = sb([128, L, 48], F32, "bm", ps)
            wm = [sb([128, KC, 768], F32, "wm%d" % i, ps) for i in range(2)]
            S.dma("sp", cTs.full(), cT[:, :, :])
            S.dma("sp", bm.full(), b_modT[:, :, :])
            S.act(cTs.full(), cTs.full(), AF.Silu)
            n = 0
            for l in range(L):
                pm = PSF[l]
                for blk in range(8):
                    w = wm[n % 2]
                    n += 1
                    S.dma("sp", w.full(), w_mod[l, :, blk * 768:(blk + 1) * 768].rearrange("(kc p) n -> p kc n", p=128))
                    for cc in range(6):
                        ccg = blk * 6 + cc
                        for kc in range(KC):
                            S.mm(pm[:, ccg * 4:ccg * 4 + 4], w[:, kc, cc * 128:(cc + 1) * 128], cTs[:, kc, :],
                                 start=(kc == 0), stop=(kc == KC - 1))
                for ccg in range(48):
                    S.ts("dve", MOD[:, l, ccg, :], pm[:, ccg * 4:ccg * 4 + 4], bm[:, l, ccg:ccg + 1], ALU.add)
                for kind in (1, 4):
                    S.ts("dve", MOD[:, l, kind * 8:(kind + 1) * 8, :], MOD[:, l, kind * 8:(kind + 1) * 8, :], 1.0, ALU.add)
                for kind in (2, 5):
                    S.ts("dve", MOD[:, l, kind * 8:(kind + 1) * 8, :], MOD[:, l, kind * 8:(kind + 1) * 8, :],
                         1.0 / ALPHA, ALU.mult)
            phase_end("p0")

        with ExitStack() as ps:
            xin = [sb([128, 4, D], F32, "xin%d" % i, ps) for i in range(2)]
            hst = [sb([128, KC, 512], F32, "hst%d" % i, ps) for i in range(2)]
            n = 0
            for s in range(2):
                for (t0, Tn) in CH512:
                    nt = Tn // 128
                    xi, hs = xin[n % 2], hst[n % 2]
                    n += 1
                    S.dma("sp", xi[:, 0:nt, :], x2[s, t0:t0 + Tn, :].rearrange("(ti p) d -> p ti d", p=128))
                    for kc in range(KC):
                        pt = PSF[kc % 4]
                        for ti in range(nt):
                            S.tr(pt[:, ti * 128:(ti + 1) * 128], xi[:, ti, kc * 128:(kc + 1) * 128], ident_f)
                        S.copy("act" if kc % 2 == 0 else "dve", hs[:, kc, 0:Tn], pt[:, 0:Tn])
                    S.dma("pool", hT_d[s, :, :, t0:t0 + Tn], hs[:, :, 0:Tn])
            phase_end("pin")

        if stop_after == "pin":
            S.barrier()
            S.emit()
            return nc

        def layernorm(zT, Tn, l, which_g, tmp):
            zsq, mean_sb, var_sb, accs, accq = tmp
            pm, pq = PSF[4], PSF[5]
            for kc in range(KC):
                q = zsq[kc % 2]
                S.act(q[:, 0:Tn], zT[:, kc, 0:Tn], AF.Square)
                if kc == 1:
                    S.tt("pool", accs[:, 0:Tn], zT[:, 0, 0:Tn], zT[:, 1, 0:Tn], ALU.add)
                    S.tt("dve", accq[:, 0:Tn], zsq[0][:, 0:Tn], zsq[1][:, 0:Tn], ALU.add)
                elif kc > 1:
                    S.tt("pool", accs[:, 0:Tn], accs[:, 0:Tn], zT[:, kc, 0:Tn], ALU.add)
                    S.tt("dve", accq[:, 0:Tn], accq[:, 0:Tn], q[:, 0:Tn], ALU.add)
            S.mm(pm[:, 0:Tn], ones_mean, accs[:, 0:Tn])
            S.mm(pq[:, 0:Tn], ones_mean, accq[:, 0:Tn])
            S.copy("act", mean_sb[:, 0:Tn], pm[:, 0:Tn])
            S.tt("dve", var_sb[:, 0:Tn], mean_sb[:, 0:Tn], mean_sb[:, 0:Tn], ALU.mult)
            S.tt("dve", var_sb[:, 0:Tn], pq[:, 0:Tn], var_sb[:, 0:Tn], ALU.subtract)
            S.act(var_sb[:, 0:Tn], var_sb[:, 0:Tn], AF.Sqrt, bias=EPSV[:, 0:1])
            S.recip(var_sb[:, 0:Tn], var_sb[:, 0:Tn])
            for kc in range(KC):
                S.tt("pool", zT[:, kc, 0:Tn], zT[:, kc, 0:Tn], mean_sb[:, 0:Tn], ALU.subtract)
                S.tt("dve", zT[:, kc, 0:Tn], zT[:, kc, 0:Tn], var_sb[:, 0:Tn], ALU.mult)
                S.act(zT[:, kc, 0:Tn], zT[:, kc, 0:Tn], AF.Identity, scale=lnv(l, which_g, kc), bias=lnv(l, which_g + 1, kc))

        EPSV = sb([128, 2], F32, "EPSV")
        S.memset("dve", EPSV[:, 0:1], LN_EPS_S)
        S.memset("dve", EPSV[:, 1:2], RMS_EPS)

        def p1(l, samples):
            with ExitStack() as ps:
                wA = sb([128, KC, NA_COLS], BF16, "wA", ps)
                wV = sb([128, KC, 384], BF16, "wV", ps)
                ROPE = sb([128, 2, NT], F32, "ROPE", ps)
                hTc = [sb([128, KC, 512], F32, "hTc%d" % i, ps) for i in range(2)]
                uTc = [sb([128, KC, 512], BF16, "uTc%d" % i, ps) for i in range(2)]
                stg = [sb([128, 512], BF16, "stg%d" % i, ps) for i in range(4)]
                sq = sb([128, 512], BF16, "sq", ps)
                qb_ = sb([128, 512], BF16, "qb_", ps)
                rs = sb([128, 512], F32, "rs", ps)
                t1 = sb([128, 512], F32, "t1", ps)
                t2 = sb([128, 512], F32, "t2", ps)
                vst = [sb([128, 4, 6, 128], BF16, "vst%d" % i, ps) for i in range(2)]
                cast_load(wA, lambda c0, c1: w_inA[l, :, c0:c1].rearrange("(kc p) n -> p kc n", p=128), NA_COLS)
                cast_load(wV, lambda c0, c1: w_inV[l, :, c0:c1].rearrange("(kc p) n -> p kc n", p=128), 384)
                S.dma("sp", ROPE.full(), rope[:, :, :])
                for i in range(2):
                    S.memset("pool", vst[i].full(), 1.0)
                nst = 0
                for s in samples:
                    for ci, (t0, Tn) in enumerate(CH512):
                        j = 2 if t0 == 0 else s
                        h, u = hTc[ci % 2], uTc[ci % 2]
                        S.dma("sp", h[:, :, 0:Tn], hT_d[s, :, :, t0:t0 + Tn])
                        for kc in range(KC):
                            S.act(u[:, kc, 0:Tn], h[:, kc, 0:Tn], AF.Identity, scale=mod(l, 1, kc, j), bias=mod(l, 0, kc, j))
                        S.dma("pool", uT_d[s, :, :, t0:t0 + Tn], u[:, :, 0:Tn])

                        def proj(ps_t, cc):
                            for kc in range(KC):
                                S.mm(ps_t[:, 0:Tn], wA[:, kc, cc * 128:(cc + 1) * 128], u[:, kc, 0:Tn],
                                     start=(kc == 0), stop=(kc == KC - 1))

                        def store(dst_ap, v):
                            S.dma("pool", dst_ap, v)

                        plain = [(6, qT_d[s, :, 4, t0:t0 + Tn]), (7, qT_d[s, :, 5, t0:t0 + Tn]),
                                 (8, kT_d[s, :, 2, t0:t0 + Tn]), (9, fT_d[s, :, 0, t0:t0 + Tn]),
                                 (10, fT_d[s, :, 1, t0:t0 + Tn])]
                        for n_, (cc, dst) in enumerate(plain):
                            pt = PSF[n_ % 2]
                            proj(pt, cc)
                            st = stg[nst % 4]
                            nst += 1
                            S.copy("act", st[:, 0:Tn], pt[:, 0:Tn])
                            store(dst, st[:, 0:Tn])
                        roped = [(0, 0, True, qT_d[s, :, 0, t0:t0 + Tn]), (1, 0, True, qT_d[s, :, 1, t0:t0 + Tn]),
                                 (2, 2, True, kT_d[s, :, 0, t0:t0 + Tn]),
                                 (3, None, False, qT_d[s, :, 2, t0:t0 + Tn]),
                                 (4, None, False, qT_d[s, :, 3, t0:t0 + Tn]),
                                 (5, None, False, kT_d[s, :, 1, t0:t0 + Tn])]
                        for n_, (cb_, g0, norm, dst) in enumerate(roped):
                            pa = PSF[n_ % 4]
                            proj(pa, cb_)
                            cosv = ROPE[:, 0, t0:t0 + Tn]
                            st = stg[nst % 4]
                            nst += 1
                            if norm:
                                S.act(sq[:, 0:Tn], pa[:, 0:Tn], AF.Square)
                                S.mm(PSF[4][:, 0:Tn], onesbd, sq[:, 0:Tn])
                                S.act(rs[:, 0:Tn], PSF[4][:, 0:Tn], AF.Sqrt, scale=1.0 / 64.0, bias=EPSV[:, 1:2])
                                S.recip(rs[:, 0:Tn], rs[:, 0:Tn])
                                S.stt(t1[:, 0:Tn], pa[:, 0:Tn], GAIN[:, l, g0:g0 + 1], cosv, ALU.mult, ALU.mult)
                                for B_ in (0, 32, 64, 96):
                                    Bp = B_ ^ 32
                                    S.stt(t2[B_:B_ + 32, 0:Tn], pa[Bp:Bp + 32, 0:Tn], GAIN[Bp:Bp + 32, l, g0:g0 + 1],
                                          ROPE[Bp:Bp + 32, 1, t0:t0 + Tn], ALU.mult, ALU.mult)
                                S.tt("pool", t1[:, 0:Tn], t1[:, 0:Tn], t2[:, 0:Tn], ALU.add)
                                S.tt("dve", st[:, 0:Tn], t1[:, 0:Tn], rs[:, 0:Tn], ALU.mult)
                            else:
                                S.tt("dve", t1[:, 0:Tn], pa[:, 0:Tn], cosv, ALU.mult)
                                for B_ in (0, 32, 64, 96):
                                    Bp = B_ ^ 32
                                    S.tt("dve", t2[B_:B_ + 32, 0:Tn], pa[Bp:Bp + 32, 0:Tn],
                                         ROPE[Bp:Bp + 32, 1, t0:t0 + Tn], ALU.mult)
                                S.tt("pool", st[:, 0:Tn], t1[:, 0:Tn], t2[:, 0:Tn], ALU.add)
                            store(dst, st[:, 0:Tn])
                        vs = vst[ci % 2]
                        nt = Tn // 128
                        for ti in range(nt):
                            pv = PSF[5]
                            for kc in range(KC):
                                S.mm(pv[:, 0:384], u[:, kc, ti * 128:(ti + 1) * 128], wV[:, kc, :],
                                     start=(kc == 0), stop=(kc == KC - 1))
                            S.copy("act", vs[:, ti, :, 0:64], pv[:, 0:384].re("p (m d) -> p m d", d=64))
                        S.dma("pool", v_d[s, :, t0 // 128:t0 // 128 + nt, :].rearrange("p t (m d) -> p t m d", d=128),
                              vs[:, 0:nt, :, :])
            phase_end("p1")

        def p2(s, l):
            with ExitStack() as ps:
                kT = sb([128, 3, NT], BF16, "kT", ps)
                Vt = sb([128, 18, 768], BF16, "Vt", ps)
                TB = sb([128, 4 * 15 * 64], F32, "TB", ps)
                MK = sb([128, 30 * 512], BF16, "MK", ps)
                qcb = [sb([128, 6, 512], BF16, "qc%d" % i, ps) for i in range(2)]
                Pb = [sb([128, 512], BF16, "Pb%d" % i, ps) for i in range(4)]
                Pe = [sb([128, 512], BF16, "Pe%d" % i, ps) for i in range(4)]
                Pm = [sb([128, 512], BF16, "Pm%d" % i, ps) for i in range(3)]
                sbias = [sb([128, 512], F32, "sbias%d" % i, ps) for i in range(3)]
                rd = sb([128, 512], F32, "rd", ps)
                dsb = sb([128, 512], F32, "dsb", ps)
                brst = [sb([128, 512], BF16, "brst%d" % i, ps) for i in range(2)]
                S.dma("sp", kT.full(), kT_d[s, :, :, :])
                S.dma("sp", Vt.full(), v_d[s, :, :, :])
                S.dma("sp", TB.full(), tbs[l, :, :])
                S.dma("sp", MK.full(), maskc[:, :])
                TBv = TB.full().re("p (h m c) -> p h m c", h=4, m=15)

                def winc(qt, dl):
                    i_ = qt * 3 + dl + 1
                    return MK[:, i_ * 512:(i_ + 1) * 512]

                def namc(qt, dl):
                    i_ = 9 + qt * 7 + dl + 3
                    return MK[:, i_ * 512:(i_ + 1) * 512]
                qch = ([(0, 256, True)] if l == 0 else []) + [(256 + i * 512, 512, False) for i in range(4)]
                cnt = {}

                def nxt(k, lst):
                    c_ = cnt.get(k, 0)
                    cnt[k] = c_ + 1
                    return lst[c_ % len(lst)]
                for qi, (t0, Tn, is_ctx) in enumerate(qch):
                    qc = qcb[qi % 2]
                    S.dma("sp", qc[:, :, 0:Tn], qT_d[s, :, :, t0:t0 + Tn])
                    b0 = (t0 - NCX) // 128
                    qt = 0 if b0 == 0 else (2 if b0 == 12 else 1)
                    for m in range(3):
                        for qcnk in range(2):
                            bst = nxt("b", brst)
                            for ph in range(2):
                                hq = qcnk + 2 * ph
                                O = nxt("o", [PSF[3], PSF[4]])
                                p0, p1_ = ph * 64, (ph + 1) * 64
                                qv = qc[p0:p1_, m * 2 + qcnk, 0:Tn]
                                vo = (m * 2 + ph) * 128
                                if is_ctx:
                                    steps = [("g", 0), ("g", 1)]
                                elif m == 0:
                                    steps = [("g", kc) for kc in range(18)]
                                else:
                                    dls = [-1, 0, 1] if m == 1 else _NA_QDELTAS[qt]
                                    steps = [("g", 0), ("g", 1)] + [("b", dl) for dl in dls]

                                def emit_S(st):
                                    Sp = nxt("s", [PSF[0], PSF[1], PSF[2], PSF[5]])
                                    if st[0] == "g":
                                        kc = st[1]
                                        S.mm(Sp[:, 0:Tn], kT[p0:p1_, m, kc * 128:(kc + 1) * 128], qv)
                                    else:
                                        for bi in range(4):
                                            kc = 2 + min(15, max(0, b0 + bi + st[1]))
                                            S.mm(Sp[:, bi * 128:(bi + 1) * 128], kT[p0:p1_, m, kc * 128:(kc + 1) * 128],
                                                 qc[p0:p1_, m * 2 + qcnk, bi * 128:(bi + 1) * 128])
                                    return Sp

                                def emit_rest(st, Sp, first, last):
                                    if st[0] == "g":
                                        kc = st[1]
                                        P = nxt("p", Pb)
                                        S.act(P[:, 0:Tn], Sp[:, 0:Tn], AF.Exp, scale=0.125)
                                        S.mm(O[:, 0:Tn], Vt[:, kc, vo:vo + 128], P[:, 0:Tn], start=first, stop=last)
                                        return
                                    dl = st[1]
                                    pe_ = nxt("e", Pe)
                                    if m == 2:
                                        sbv = nxt("e2", sbias)
                                        m0 = 7 - 2 * dl
                                        for bi in range(4):
                                            S.stt(sbv[:, bi * 128:(bi + 1) * 128].re("p (j c) -> p j c", c=64),
                                                  Sp[:, bi * 128:(bi + 1) * 128].re("p (j c) -> p j c", c=64), 0.125,
                                                  TBv[:, hq, m0:m0 + 2, :], ALU.mult, ALU.add)
                                        S.act(pe_.full(), sbv.full(), AF.Exp)
                                        pfin = nxt("m", Pm)
                                        S.tt("pool", pfin.full(), pe_.full(), namc(qt, dl), ALU.mult)
                                    else:
                                        S.act(pe_.full(), Sp.full(), AF.Exp, scale=0.125)
                                        if dl == 0:
                                            pfin = pe_
                                        else:
                                            pfin = nxt("m", Pm)
                                            S.tt("dve", pfin.full(), pe_.full(), winc(qt, dl), ALU.mult)
                                    valid = [bi for bi in range(4) if 0 <= b0 + bi + dl < 16]
                                    for bi in valid:
                                        kc = 2 + b0 + bi + dl
                                        S.mm(O[:, bi * 128:(bi + 1) * 128], Vt[:, kc, vo:vo + 128],
                                             pfin[:, bi * 128:(bi + 1) * 128], start=False, stop=(last and bi == valid[-1]))
                                n_ = len(steps)
                                LA = 3
                                sps = [None] * n_
                                for i_ in range(min(LA, n_)):
                                    sps[i_] = emit_S(steps[i_])
                                for i_ in range(n_):
                                    if i_ + LA < n_:
                                        sps[i_ + LA] = emit_S(steps[i_ + LA])
                                    emit_rest(steps[i_], sps[i_], i_ == 0, i_ == n_ - 1)
                                if m == 1:
                                    S.ts("dve", dsb[64:128, 0:Tn], O[64:128, 0:Tn],
                                         SINKE[64:128, l * 4 + hq:l * 4 + hq + 1], ALU.add)
                                    S.recip(rd[0:64, 0:Tn], dsb[64:128, 0:Tn])
                                else:
                                    S.recip(rd[0:64, 0:Tn], O[64:128, 0:Tn])
                                S.tt("dve", bst[p0:p1_, 0:Tn], O[0:64, 0:Tn], rd[0:64, 0:Tn], ALU.mult)
                            cidx = (0, 4, 6)[m] + qcnk
                            S.dma("pool", brT_d[s, :, cidx, t0:t0 + Tn], bst[:, 0:Tn])
            phase_end("p2")

        def p2b(s, l):
            with ExitStack() as ps:
                fT = sb([128, 2, NT], BF16, "fT", ps)
                gcs = sb([128, 18, 4, 128], BF16, "gcs", ps)
                Cc = [sb([128, 16, 512], BF16, "Cc%d" % i, ps) for i in range(2)]
                Sc = [sb([128, 16, 512], BF16, "Sc%d" % i, ps) for i in range(2)]
                ost = [sb([128, 512], BF16, "ost%d" % i, ps) for i in range(2)]
                S.dma("sp", fT.full(), fT_d[s, :, :, :])
                tcs = range(18) if l == 0 else range(2, 18)
                for tc in tcs:
                    pg = PSF[tc % 2]
                    for fc in range(2):
                        S.mm(pg[:, (fc * 2) * 128:(fc * 2 + 1) * 128], fT[:, fc, tc * 128:(tc + 1) * 128], c64bd)
                        S.mm(pg[:, (fc * 2 + 1) * 128:(fc * 2 + 2) * 128], fT[:, fc, tc * 128:(tc + 1) * 128], s64bd)
                    S.copy("act" if tc % 2 == 0 else "dve", gcs[:, tc, :, :], pg[:, 0:512].re("p (a c) -> p a c", c=128))
                n = 0
                jobs = [(256 + i * 512, 512, 2, 16, dftc, dfts) for i in range(4)]
                if l == 0:
                    jobs = [(0, 256, 0, 2, dftc_c, dfts_c)] + jobs
                for ji, (t0, Tn, tc0, ntc, mc, msn) in enumerate(jobs):
                    c_, s_ = Cc[ji % 2], Sc[ji % 2]
                    col0 = 0 if t0 == 0 else t0 - NCX
                    S.dma("sp", c_[:, 0:ntc, 0:Tn], mc[:, col0:col0 + Tn].rearrange("(tc p) n -> p tc n", p=128))
                    S.dma("sp", s_[:, 0:ntc, 0:Tn], msn[:, col0:col0 + Tn].rearrange("(tc p) n -> p tc n", p=128))
                    for fc in range(2):
                        po = PSF[2 + (n % 2)]
                        for k_ in range(ntc):
                            S.mm(po[:, 0:Tn], gcs[:, tc0 + k_, fc * 2, :], c_[:, k_, 0:Tn], start=(k_ == 0), stop=False)
                            S.mm(po[:, 0:Tn], gcs[:, tc0 + k_, fc * 2 + 1, :], s_[:, k_, 0:Tn], start=False, stop=(k_ == ntc - 1))
                        o_ = ost[n % 2]
                        n += 1
                        S.copy("act", o_[:, 0:Tn], po[:, 0:Tn])
                        S.dma("pool", brT_d[s, :, 2 + fc, t0:t0 + Tn], o_[:, 0:Tn])
            phase_end("p2b")

        def p3_weights(l, ws):
            wG = sb([128, KC, 4096], BF16, "wG", ws)
            wB = sb([128, KC, D], BF16, "wB", ws)
            wO = sb([128, KC, D], BF16, "wO", ws)
            wR = sb([128, KC, E], F32, "wR", ws)
            cast_load(wG, lambda c0, c1: w_inG[l, :, c0:c1].rearrange("(kc p) n -> p kc n", p=128), 4096)
            cast_load(wB, lambda c0, c1: w_brP[l, :, c0:c1].rearrange("(kc p) n -> p kc n", p=128), D)
            cast_load(wO, lambda c0, c1: w_out[l, :, c0:c1].rearrange("(kc p) n -> p kc n", p=128), D)
            S.dma("sp", wR.full(), w_router[l, :, :].rearrange("(kc p) n -> p kc n", p=128))
            return wG, wB, wO, wR

        def p3(s, l, wts):
            wG, wB, wO, wR = wts
            with ExitStack() as ps:
                TT = 512
                uTc = sb([128, KC, TT], BF16, "uTc3", ps)
                brc = sb([128, 8, TT], BF16, "brc", ps)
                hTc = sb([128, KC, TT], F32, "hTc3", ps)
                zT = sb([128, KC, TT], F32, "zT", ps)
                mrg = sb([128, KC, TT], BF16, "mrg", ps)
                u2f = hTc
                u2b = brc
                gate = [sb([128, TT], F32, "gate%d" % i, ps) for i in range(3)]
                tmpm = [sb([128, TT], F32, "tmpm%d" % i, ps) for i in range(3)]
                acc = sb([128, TT], F32, "acc", ps)
                zsq = [sb([128, TT], F32, "zsq%d" % i, ps) for i in range(2)]
                mean_sb = sb([128, TT], F32, "mean_sb", ps)
                var_sb = sb([128, TT], F32, "var_sb", ps)
                accs = sb([128, TT], F32, "accs", ps)
                accq = sb([128, TT], F32, "accq", ps)
                ex = sb([16, TT], F32, "ex", ps)
                rsum = sb([16, TT], F32, "rsum", ps)
                utok = [sb([128, D], BF16, "utok%d" % i, ps) for i in range(2)]
                chunks = CH512 if l == 0 else CH512[1:]
                ng = 0
                for (t0, Tn) in chunks:
                    j = 2 if t0 == 0 else s
                    S.dma("sp", uTc[:, :, 0:Tn], uT_d[s, :, :, t0:t0 + Tn])
                    S.dma("sp", brc[:, :, 0:Tn], brT_d[s, :, :, t0:t0 + Tn])
                    S.dma("sp", hTc[:, :, 0:Tn], hT_d[s, :, :, t0:t0 + Tn])
                    for dc in range(KC):
                        for i in range(4):
                            pg = PSF[ng % 3]
                            pb = PSF[3 + ng % 3]
                            g_ = gate[ng % 3]
                            tm = tmpm[ng % 3]
                            ng += 1
                            for kc in range(KC):
                                S.mm(pg[:, 0:Tn], wG[:, kc, i * D + dc * 128:i * D + (dc + 1) * 128], uTc[:, kc, 0:Tn],
                                     start=(kc == 0), stop=(kc == KC - 1))
                            S.act(g_[:, 0:Tn], pg[:, 0:Tn], AF.Sigmoid)
                            for hf in range(2):
                                S.mm(pb[:, 0:Tn], wB[:, i * 2 + hf, dc * 128:(dc + 1) * 128], brc[:, i * 2 + hf, 0:Tn],
                                     start=(hf == 0), stop=(hf == 1))
                            if i == 0:
                                S.tt("dve", acc[:, 0:Tn], g_[:, 0:Tn], pb[:, 0:Tn], ALU.mult)
                            else:
                                S.tt("dve", tm[:, 0:Tn], g_[:, 0:Tn], pb[:, 0:Tn], ALU.mult)
                                if i < 3:
                                    S.tt("pool", acc[:, 0:Tn], acc[:, 0:Tn], tm[:, 0:Tn], ALU.add)
                                else:
                                    S.tt("pool", mrg[:, dc, 0:Tn], acc[:, 0:Tn], tm[:, 0:Tn], ALU.add)
                    for dc in range(KC):
                        py = PSF[dc % 2]
                        for kc in range(KC):
                            S.mm(py[:, 0:Tn], wO[:, kc, dc * 128:(dc + 1) * 128], mrg[:, kc, 0:Tn],
                                 start=(kc == 0), stop=(kc == KC - 1))
                        S.stt(zT[:, dc, 0:Tn], py[:, 0:Tn], mod(l, 2, dc, j), hTc[:, dc, 0:Tn], ALU.mult, ALU.add)
                    layernorm(zT, Tn, l, 0, (zsq, mean_sb, var_sb, accs, accq))
                    S.dma("pool", hT_d[s, :, :, t0:t0 + Tn], zT[:, :, 0:Tn])
                    for kc in range(KC):
                        S.act(u2f[:, kc, 0:Tn], zT[:, kc, 0:Tn], AF.Identity, scale=mod(l, 4, kc, j), bias=mod(l, 3, kc, j))
                        S.copy("pool", u2b[:, kc, 0:Tn], u2f[:, kc, 0:Tn])
                    pl = PSF[2]
                    for kc in range(KC):
                        S.mm(pl[0:16, 0:Tn], wR[:, kc, :], u2f[:, kc, 0:Tn], start=(kc == 0), stop=(kc == KC - 1))
                    S.act(ex[:, 0:Tn], pl[0:16, 0:Tn], AF.Exp)
                    S.mm(PSF[3][0:16, 0:Tn], ones_f[0:16, 0:16], ex[:, 0:Tn])
                    S.recip(rsum[:, 0:Tn], PSF[3][0:16, 0:Tn])
                    S.tt("dve", ex[:, 0:Tn], ex[:, 0:Tn], rsum[:, 0:Tn], ALU.mult)
                    S.dma("pool", aff_d[s, :, t0:t0 + Tn], ex[:, 0:Tn])
                    for ti in range(Tn // 128):
                        pt = PSB[ti % 2]
                        ut = utok[ti % 2]
                        for kc in range(KC):
                            S.tr(pt[:, kc * 128:(kc + 1) * 128], u2b[:, kc, ti * 128:(ti + 1) * 128], ident_b)
                        S.copy("act", ut.full(), pt.full())
                        r0 = s * NT + t0 + ti * 128
                        S.dma("pool", u2tok_d[r0:r0 + 128, :], ut.full())
            phase_end("p3")

        IDXT = sb([128, 5, E], I32, "IDXT")
        SELW = sb([128, 5, E], F32, "SELW")

        def p4(l):
            with ExitStack() as ps:
                work = sb([48, NL], F32, "work", ps)
                workc = sb([48, NCX], F32, "workc", ps)
                vals = sb([48, 288], F32, "vals", ps)
                idxu = sb([48, 288], U32, "idxu", ps)
                idxf = sb([48, 288], F32, "idxf", ps)
                lo = sb([16, 2, 288], F32, "lo", ps)
                tmpi = sb([128, E], F32, "tmpi", ps)
                S.memset("dve", work.full(), 0.0)
                S.memset("dve", workc.full(), 0.0)
                S.memset("dve", vals.full(), 0.0)
                S.memset("dve", idxu.full(), 0)
                for s in range(2):
                    S.dma("sp", work[32 * s:32 * s + 16, :], aff_d[s, :, NCX:NT])
                    if l == 0:
                        S.dma("sp", workc[32 * s:32 * s + 16, :], aff_d[s, :, 0:NCX])
                jobs = [(work, 0, 32)] + ([(workc, 256, 4)] if l == 0 else [])
                for (wt, slot0, rounds) in jobs:
                    for r_ in range(rounds):
                        sl = slice(slot0 + r_ * 8, slot0 + r_ * 8 + 8)
                        mx, ix, wk = _ap(vals[:, sl]), _ap(idxu[:, sl]), _ap(wt.full())
                        S.op("dve", lambda e, mx=mx, wk=wk: e.max(out=mx, in_=wk), reads=[wt.full()], writes=[vals.full()])
                        S.op("dve", lambda e, mx=mx, ix=ix, wk=wk: e.max_index(out=ix, in_max=mx, in_values=wk),
                             reads=[wt.full(), vals.full()], writes=[idxu.full()])
                        S.op("dve", lambda e, mx=mx, wk=wk: e.match_replace(out=wk, in_to_replace=mx, in_values=wk, imm_value=-1.0),
                             reads=[wt.full(), vals.full()], writes=[wt.full()])
                S.copy("dve", idxf.full(), idxu.full())
                S.ts("dve", idxf[0:16, 0:256], idxf[0:16, 0:256], float(NCX), ALU.add)
                S.ts("dve", idxf[32:48, 0:256], idxf[32:48, 0:256], float(NT + NCX), ALU.add)
                S.ts("dve", idxf[32:48, 256:288], idxf[32:48, 256:288], float(NT), ALU.add)
                S.copy("dve", lo[:, 0, :], idxf[32:48, :])
                S.copy("dve", lo[:, 1, :], vals[32:48, :])
                srcs = [(idxf[0:16, :], vals[0:16, :]), (lo[:, 0, :], lo[:, 1, :])]
                n = 0
                for s in range(2):
                    si, sv = srcs[s]
                    for half in range(2):
                        st = s * 2 + half
                        pt = PSF[n % 2]
                        n += 1
                        S.tr(pt[:, 0:16], si[:, half * 128:(half + 1) * 128], ident_f[0:16, 0:16])
                        S.copy("dve", tmpi.full(), pt[:, 0:16])
                        S.copy("dve", IDXT[:, st, :], tmpi.full())
                        S.tr(pt[:, 16:32], sv[:, half * 128:(half + 1) * 128], ident_f[0:16, 0:16])
                        S.copy("dve", SELW[:, st, :], pt[:, 16:32])
                    if l == 0:
                        pt = PSF[n % 2]
                        n += 1
                        S.tr(pt[0:32, 0:16], si[:, 256:288], ident_f[0:16, 0:16])
                        S.copy("dve", tmpi[0:32, :], pt[0:32, 0:16])
                        S.copy("dve", IDXT[32 * s:32 * s + 32, 4, :], tmpi[0:32, :])
                        S.tr(pt[0:32, 16:32], sv[:, 256:288], ident_f[0:16, 0:16])
                        S.copy("dve", SELW[32 * s:32 * s + 32, 4, :], pt[0:32, 16:32])
            phase_end("p4")

        def p5_weights(l, ws):
            wg = [sb([128, KC, D], BF16, "wg%d" % i, ws) for i in range(2)]
            wu = [sb([128, KC, D], BF16, "wu%d" % i, ws) for i in range(2)]
            wd = [sb([128, KC, D], BF16, "wd%d" % i, ws) for i in range(2)]
            for wt, src in ((wg[0], w_gate), (wu[0], w_up), (wd[0], w_down)):
                S.dma("pool", wt.full(), src[l, 0, :, :].rearrange("(kc p) n -> p kc n", p=128))
            return wg, wu, wd

        def p5(l, wts):
            wg, wu, wd = wts
            with ExitStack() as ps:
                xs = [sb([128, 5, D], BF16, "xs%d" % i, ps) for i in range(2)]
                xsT = sb([128, KC, 640], BF16, "xsT", ps)
                actT = sb([128, KC, 640], BF16, "actT", ps)
                sa = [sb([128, 512], F32, "sa%d" % i, ps) for i in range(2)]
                ysb = [sb([128, D], F32, "ysb%d" % i, ps) for i in range(2)]
                zer = sb([128, 2, D], F32, "zer", ps)
                FFN = T(ffn_d, "ffn_d")
                U2 = T(u2tok_d, "u2tok_d")
                S.memset("dve", zer.full(), 0.0)
                for i in range(2 * NT // 256):
                    S.dma("sp", V(FFN, ffn_d[i * 256:(i + 1) * 256, :].rearrange("(a p) d -> p a d", p=128)), zer.full())
                sts = [(0, 128), (1, 128), (2, 128), (3, 128)] + ([(4, 64)] if l == 0 else [])
                nsl = 576 if l == 0 else 512
                cgs = [(0, 512)] + ([(512, 576)] if l == 0 else [])

                def loadw(e_):
                    b_ = e_ % 2
                    for wt, src in ((wg[b_], w_gate), (wu[b_], w_up), (wd[b_], w_down)):
                        S.dma("pool", wt.full(), src[l, e_, :, :].rearrange("(kc p) n -> p kc n", p=128))

                def gathers(e_):
                    x_ = xs[e_ % 2]
                    for (st, np_) in sts:
                        o_ = _ap(x_[0:np_, st, :])
                        ix = _ap(IDXT[0:np_, st, e_:e_ + 1])

                        def fn(eng, o_=o_, ix=ix):
                            return eng.indirect_dma_start(out=o_, out_offset=None, in_=u2tok_d[:, :],
                                                          in_offset=bass.IndirectOffsetOnAxis(ap=ix, axis=0))
                        S.dma("pool", x_[0:np_, st, :], V(U2, None), extra_reads=[IDXT.full()], fn=fn)
                gathers(0)
                ny = 0
                for e_ in range(E):
                    if e_ + 1 < E:
                        loadw(e_ + 1)
                        gathers(e_ + 1)
                    b_ = e_ % 2
                    x_ = xs[b_]
                    for (st, np_) in sts:
                        pt = PSB[st % 2]
                        for kc in range(KC):
                            S.tr(pt[:, kc * 128:kc * 128 + np_], x_[0:np_, st, kc * 128:(kc + 1) * 128], ident_b[0:np_, 0:np_])
                        S.copy("act" if st % 2 == 0 else "dve", xsT[:, :, st * 128:st * 128 + np_],
                               pt.full().re("p (k t) -> p k t", t=128)[:, :, 0:np_])
                    for fc in range(KC):
                        for (c0, c1) in cgs:
                            pa, pu = PSF[(fc % 2) * 2], PSF[(fc % 2) * 2 + 1]
                            w_ = c1 - c0
                            for kc in range(KC):
                                S.mm(pa[:, 0:w_], wg[b_][:, kc, fc * 128:(fc + 1) * 128], xsT[:, kc, c0:c1],
                                     start=(kc == 0), stop=(kc == KC - 1))
                            for kc in range(KC):
                                S.mm(pu[:, 0:w_], wu[b_][:, kc, fc * 128:(fc + 1) * 128], xsT[:, kc, c0:c1],
                                     start=(kc == 0), stop=(kc == KC - 1))
                            s_ = sa[fc % 2]
                            S.act(s_[:, 0:w_], pa[:, 0:w_], AF.Silu)
                            S.tt("dve", actT[:, fc, c0:c1], s_[:, 0:w_], pu[:, 0:w_], ALU.mult)
                    for (st, np_) in sts:
                        y_ = ysb[ny % 2]
                        ny += 1
                        for dh in range(2):
                            py = PSF[4 + dh]
                            for fc in range(KC):
                                S.mm(py[0:np_, :], actT[:, fc, st * 128:st * 128 + np_], wd[b_][:, fc, dh * 512:(dh + 1) * 512],
                                     start=(fc == 0), stop=(fc == KC - 1))
                            S.act(y_[0:np_, dh * 512:(dh + 1) * 512], py[0:np_, :], AF.Copy, scale=SELW[0:np_, st, e_:e_ + 1])
                        yi = _ap(y_[0:np_, :])
                        ix = _ap(IDXT[0:np_, st, e_:e_ + 1])

                        def fn(eng, yi=yi, ix=ix):
                            return eng.indirect_dma_start(out=ffn_d[:, :], out_offset=bass.IndirectOffsetOnAxis(ap=ix, axis=0),
                                                          in_=yi, in_offset=None, compute_op=ALU.add)
                        S.dma("pool", V(FFN, None), y_[0:np_, :], extra_reads=[IDXT.full(), V(FFN, None)], fn=fn)
            phase_end("p5")

        def p6(s, l):
            with ExitStack() as ps:
                ftok = sb([128, 4, D], F32, "ftok", ps)
                hTc = sb([128, KC, 512], F32, "hTc6", ps)
                zT = sb([128, KC, 512], F32, "zT6", ps)
                zsq = [sb([128, 512], F32, "zsq6%d" % i, ps) for i in range(2)]
                mean_sb = sb([128, 512], F32, "mean6", ps)
                var_sb = sb([128, 512], F32, "var6", ps)
                accs = sb([128, 512], F32, "accs6", ps)
                accq = sb([128, 512], F32, "accq6", ps)
                otok = [sb([128, D], F32, "otok%d" % i, ps) for i in range(2)]
                chunks = CH512 if l == 0 else CH512[1:]
                for (t0, Tn) in chunks:
                    j = 2 if t0 == 0 else s
                    nt = Tn // 128
                    r0 = s * NT + t0
                    S.dma("sp", ftok[:, 0:nt, :], ffn_d[r0:r0 + Tn, :].rearrange("(ti p) d -> p ti d", p=128))
                    S.dma("sp", hTc[:, :, 0:Tn], hT_d[s, :, :, t0:t0 + Tn])
                    for kc in range(KC):
                        pk = PSF[kc % 4]
                        for ti in range(nt):
                            S.tr(pk[:, ti * 128:(ti + 1) * 128], ftok[:, ti, kc * 128:(kc + 1) * 128], ident_f)
                        S.stt(zT[:, kc, 0:Tn], pk[:, 0:Tn], mod(l, 5, kc, j), hTc[:, kc, 0:Tn], ALU.mult, ALU.add)
                    layernorm(zT, Tn, l, 2, (zsq, mean_sb, var_sb, accs, accq))
                    if l < L - 1:
                        S.dma("pool", hT_d[s, :, :, t0:t0 + Tn], zT[:, :, 0:Tn])
                    else:
                        for ti in range(nt):
                            ot = otok[ti % 2]
                            for hf in range(2):
                                po = PSF[hf]
                                for k4 in range(4):
                                    kc = hf * 4 + k4
                                    S.tr(po[:, k4 * 128:(k4 + 1) * 128], zT[:, kc, ti * 128:(ti + 1) * 128], ident_f)
                                S.copy("act" if hf == 0 else "dve", ot[:, hf * 512:(hf + 1) * 512], po.full())
                            S.dma("pool", out[s, t0 - NCX + ti * 128:t0 - NCX + (ti + 1) * 128, :], ot.full())
            phase_end("p6")

        done = False
        for l in range(L):
            p1(l, (0,) if stop_after == "p1" else (0, 1))
            if stop_after == "p1":
                break
            for s in range(2):
                p2(s, l)
                with ExitStack() as ws:
                    wts = p3_weights(l, ws) if stop_after != "p2" else None
                    p2b(s, l)
                    if stop_after == "p2":
                        done = True
                        break
                    p3(s, l, wts)
                if stop_after == "p3":
                    done = True
                    break
            if done:
                break
            with ExitStack() as ws:
                wts5 = p5_weights(l, ws)
                p4(l)
                p5(l, wts5)
            if stop_after == "p5":
                break
            for s in range(2):
                p6(s, l)
            if stop_after == "l0":
                break
        S.barrier()
        S.emit()
        print("bass ops:", S.nops, {e: len(S.items[e]) for e in ENG})
        _CACHE["marks"] = marks
    return nc


_CACHE = {}


def _host_shared(inp):
    f = np.float32
    sh = dict(_consts())
    cols, vcols = _wina_cols()
    w_in = np.asarray(inp["w_in"], f)
    sh["w_inA"] = np.ascontiguousarray(w_in[:, :, cols])
    sh["w_inV"] = np.ascontiguousarray(w_in[:, :, vcols])
    sh["w_inG"] = np.ascontiguousarray(w_in[:, :, 1792:])
    sh["w_mod"] = np.asarray(inp["w_mod"], f)
    sh["b_modT"] = np.ascontiguousarray(np.asarray(inp["b_mod"], f).reshape(L, 48, 128).transpose(2, 0, 1))
    g = np.asarray(inp["qk_gain"], f)
    d_ = np.array([_ORD[p % 64] for p in range(128)])
    gt = np.stack([g[:, 0, d_], g[:, 0, d_], g[:, 1, d_], g[:, 1, d_]], -1)
    sh["gainT"] = np.ascontiguousarray(gt.transpose(1, 0, 2))
    sk = np.asarray(inp["sink_logit"], f).reshape(1, L * 4)
    sh["sinkB"] = np.ascontiguousarray(np.broadcast_to(sk, (128, L * 4)))
    rpb = np.asarray(inp["na_rpb"], f)
    p = np.arange(128)
    kcol = (p % 64)[:, None, None]
    i_ = (p // 64)[:, None, None]
    m_ = np.arange(15)[None, :, None]
    c_ = np.arange(64)[None, None, :]
    a_ = np.clip(14 - m_ + i_, 0, 14) + 0 * c_
    oc = np.clip(kcol - c_ + 15, 0, 30) + 0 * m_
    tb = rpb[:, :, a_, oc]
    sh["tbs"] = np.ascontiguousarray(tb.transpose(0, 2, 1, 3, 4).reshape(L, 128, 4 * 15 * 64))
    wb = np.asarray(inp["w_branch"], f)
    perm = np.concatenate([np.arange(0, 64), np.arange(128, 192), np.arange(64, 128), np.arange(192, 256)])
    wbp = wb.copy()
    for i in (0, 2, 3):
        wbp[:, i] = wb[:, i][:, perm]
    sh["w_brP"] = np.ascontiguousarray(wbp.reshape(L, 1024, D))
    sh["w_out"] = np.asarray(inp["w_out"], f)
    ln = np.stack([inp["ln1_g"], inp["ln1_b"], inp["ln2_g"], inp["ln2_b"]], 1).astype(f)
    sh["lnT"] = np.ascontiguousarray(ln.reshape(L, 4, KC, 128).transpose(3, 0, 1, 2).reshape(128, L * 4 * KC))
    sh["w_router"] = np.asarray(inp["w_router"], f)
    sh["w_gate"] = np.asarray(inp["w_gate"], f)
    sh["w_up"] = np.asarray(inp["w_up"], f)
    sh["w_down"] = np.asarray(inp["w_down"], f)
    return sh


def _host_core(inp, core):
    f = np.float32
    b0 = core * 2
    x = np.asarray(inp["x"], f)
    ctx = np.asarray(inp["ctx"], f)
    c = np.asarray(inp["c"], f)
    cc = np.asarray(inp["c_ctx"], f)
    d = {}
    d["x2"] = np.ascontiguousarray(np.concatenate([ctx[b0:b0 + 2], x[b0:b0 + 2]], axis=1))
    cv = np.stack([c[b0], c[b0 + 1], cc, np.zeros_like(cc)], -1)
    d["cT"] = np.ascontiguousarray(cv.reshape(KC, 128, 4).transpose(1, 0, 2))
    return d


def kernel(**inputs):
    n = 8
    if "nc" not in _CACHE:
        _CACHE["nc"] = build_program(debug=False)
    nc = _CACHE["nc"]
    sh = _host_shared(inputs)
    in_maps = []
    for core in range(n):
        m = dict(sh)
        m.update(_host_core(inputs, core))
        in_maps.append(m)
    res = run_bass_kernel_spmd(nc, in_maps, core_ids=list(range(n)))
    _CACHE["last"] = res
    outs = [np.asarray(r["out"], np.float32) for r in res.results]
    return np.concatenate(outs, axis=0)
```

```python
import numpy as np
from contextlib import ExitStack
import ml_dtypes
import concourse.bass as bass
import concourse.mybir as mybir
from concourse.bass_utils import run_bass_kernel_spmd

F32 = mybir.dt.float32
BF16 = mybir.dt.bfloat16
U32 = mybir.dt.uint32
I32 = mybir.dt.int32
ALU = mybir.AluOpType
AF = mybir.ActivationFunctionType

L = 2
D = 1024
KC = 8
NT = 2304
NCX = 256
NL = 2048
E = 16
ALPHA = (2 * L) ** 0.25
LN_EPS_S = 1e-6 / (ALPHA * ALPHA)
RMS_EPS = 1e-6
NA_COLS = 11 * 128
ENG = ("pe", "act", "dve", "pool", "sp")


class T:
    __slots__ = ("h", "w", "r", "name")

    def __init__(self, h, name=""):
        self.h = h
        self.w = {}
        self.r = {}
        self.name = name

    def __getitem__(self, idx):
        return V(self, self.h[idx])

    def full(self):
        return V(self, self.h[:])


class V:
    __slots__ = ("t", "ap")

    def __init__(self, t, ap):
        self.t = t
        self.ap = ap

    def re(self, pat, **kw):
        return V(self.t, self.ap.rearrange(pat, **kw))

    def __getitem__(self, idx):
        return V(self.t, self.ap[idx])


def _ap(x):
    return x.ap if isinstance(x, V) else x


def _ts(xs):
    return [x.t for x in xs if isinstance(x, V)]


class Sched:
    def __init__(self, nc, es, n_dma_sems=(("sp", 12), ("pool", 10), ("act", 4))):
        self.nc = nc
        self.es = es
        self.items = {e: [] for e in ENG}
        self.sems = {}
        self.cnt = {}
        self.seen = {e: {} for e in ENG}
        for e in ENG:
            self._mk("c_" + e)
        self.dma_pool = {}
        self.dma_rr = {}
        for e, n in n_dma_sems:
            self.dma_pool[e] = [self._mk("d_%s%d" % (e, i)) for i in range(n)]
            self.dma_rr[e] = 0
        self.nops = 0

    def _mk(self, key):
        self.sems[key] = self.es.enter_context(self.nc.semaphore(key))
        self.cnt[key] = 0
        return key

    def _need(self, e, key, val):
        if val <= 0 or self.seen[e].get(key, 0) >= val:
            return
        self.seen[e][key] = val
        self.items[e].append(("wait", key, val))

    def _deps(self, e, reads, writes, is_pe=False, is_dma=False):
        own = "c_" + e if not is_dma else "__none__"
        for t in reads:
            for k, v in t.w.items():
                if is_pe and k == own:
                    continue
                self._need(e, k, v)
        for t in writes:
            for k, v in t.r.items():
                if k != own:
                    self._need(e, k, v)
            for k, v in t.w.items():
                if k != own:
                    self._need(e, k, v)

    def _mark(self, key, val, reads, writes):
        for t in reads:
            t.r[key] = max(t.r.get(key, 0), val)
        for t in writes:
            if t.r:
                t.r = {}
                t.w = {}
            t.w[key] = max(t.w.get(key, 0), val)

    def op(self, e, fn, reads=(), writes=()):
        reads = _ts(reads)
        writes = _ts(writes)
        self._deps(e, reads, writes, is_pe=(e == "pe"))
        key = "c_" + e
        self.cnt[key] += 1
        self.items[e].append(("op", fn, key, 1))
        self._mark(key, self.cnt[key], reads, writes)
        self.nops += 1

    def dma(self, e, out, in_, extra_reads=(), fn=None, **kw):
        reads = _ts([in_] + list(extra_reads))
        writes = _ts([out])
        pool = self.dma_pool[e]
        key = pool[self.dma_rr[e] % len(pool)]
        self.dma_rr[e] += 1
        self._need(e, key, self.cnt[key])
        self._deps(e, reads, writes, is_dma=True)
        self.cnt[key] += 16
        if fn is None:
            o, i = _ap(out), _ap(in_)

            def fn(eng, o=o, i=i, kw=kw):
                return eng.dma_start(out=o, in_=i, **kw)
        self.items[e].append(("op", fn, key, 16))
        self._mark(key, self.cnt[key], reads, writes)
        self.nops += 1

    def barrier(self):
        for e in ENG:
            for k, v in self.cnt.items():
                if k != "c_" + e:
                    self._need(e, k, v)

    def emit(self):
        nc = self.nc
        with nc.Block() as block:
            def run(e):
                def body(eng):
                    for it in self.items[e]:
                        if it[0] == "wait":
                            eng.wait_ge(self.sems[it[1]], it[2])
                        else:
                            it[1](eng).then_inc(self.sems[it[2]], it[3])
                return body
            block.tensor(run("pe"))
            block.scalar(run("act"))
            block.vector(run("dve"))
            block.gpsimd(run("pool"))
            block.sync(run("sp"))

    def mm(self, out, lhsT, rhs, start=True, stop=True):
        o, a, b = _ap(out), _ap(lhsT), _ap(rhs)
        self.op("pe", lambda e: e.matmul(o, a, b, start=start, stop=stop), reads=[lhsT, rhs], writes=[out])

    def tr(self, out, in_, ident):
        o, a, b = _ap(out), _ap(in_), _ap(ident)
        self.op("pe", lambda e: e.transpose(o, a, b), reads=[in_, ident], writes=[out])

    def act(self, out, in_, func, scale=1.0, bias=0.0):
        o, i = _ap(out), _ap(in_)
        sc, bi = _ap(scale), _ap(bias)
        self.op("act", lambda e: e.activation(out=o, in_=i, func=func, bias=bi, scale=sc),
                reads=[in_, scale, bias], writes=[out])

    def tt(self, eng, out, a, b, op):
        o, x, y = _ap(out), _ap(a), _ap(b)
        self.op(eng, lambda e: e.tensor_tensor(o, x, y, op), reads=[a, b], writes=[out])

    def ts(self, eng, out, a, s1, op0, s2=None, op1=None):
        o, x, p1, p2 = _ap(out), _ap(a), _ap(s1), _ap(s2)
        if op1 is None:
            self.op(eng, lambda e: e.tensor_scalar(o, x, p1, None, op0), reads=[a, s1], writes=[out])
        else:
            self.op(eng, lambda e: e.tensor_scalar(o, x, p1, p2, op0, op1), reads=[a, s1, s2], writes=[out])

    def stt(self, out, a, s, b, op0, op1):
        o, x, p, y = _ap(out), _ap(a), _ap(s), _ap(b)
        self.op("dve", lambda e: e.scalar_tensor_tensor(o, x, p, y, op0, op1), reads=[a, s, b], writes=[out])

    def copy(self, eng, out, in_):
        o, i = _ap(out), _ap(in_)
        if eng == "act":
            self.op("act", lambda e: e.activation(out=o, in_=i, func=AF.Copy), reads=[in_], writes=[out])
        else:
            self.op(eng, lambda e: e.tensor_copy(o, i), reads=[in_], writes=[out])

    def recip(self, out, in_):
        o, i = _ap(out), _ap(in_)
        self.op("dve", lambda e: e.reciprocal(o, i), reads=[in_], writes=[out])

    def memset(self, eng, out, val):
        o = _ap(out)
        self.op(eng, lambda e: e.memset(o, val), writes=[out])


def _partner(d):
    return d + 16 if (d % 32) < 16 else d - 16


_ORD = list(range(0, 16)) + list(range(32, 48)) + list(range(16, 32)) + list(range(48, 64))


def _rope_tables():
    cos = np.ones((128, NT), np.float32)
    sin = np.zeros((128, NT), np.float32)
    t = np.arange(NL)
    inv = (np.float32(10000.0) ** (-np.arange(0, 32, 2, dtype=np.float32) / np.float32(32))).astype(np.float32)
    for p in range(128):
        d = _ORD[p % 64]
        pos = (t // 64) if d < 32 else (t % 64)
        ang = pos.astype(np.float32) * inv[d % 16]
        cos[p, NCX:] = np.cos(ang).astype(np.float32)
        sgn = -1.0 if (d % 32) < 16 else 1.0
        sin[p ^ 32, NCX:] = sgn * np.sin(ang).astype(np.float32)
    return cos, sin


def _na_masks():
    rows, W, kh, kw = 32, 64, 8, 16
    t = np.arange(NL)
    r, c = t // W, t % W
    r0 = np.clip(r - kh // 2, 0, rows - kh)
    c0 = np.clip(c - kw // 2, 0, W - kw)
    k = np.arange(NL)
    kr, kcol = k // W, k % W
    valid = ((kr[None, :] >= r0[:, None]) & (kr[None, :] < r0[:, None] + kh) &
             (kcol[None, :] >= c0[:, None]) & (kcol[None, :] < c0[:, None] + kw))
    full = np.zeros((16, 7, 128, 128), np.float32)
    for b in range(16):
        for dl in range(-3, 4):
            kci = b + dl
            if 0 <= kci < 16:
                full[b, dl + 3] = valid[b * 128:(b + 1) * 128, kci * 128:(kci + 1) * 128].T
    types = [0, 1] + [2] * 12 + [3, 4]
    rep = {0: 0, 1: 1, 2: 5, 3: 14, 4: 15}
    for b in range(16):
        assert np.array_equal(full[b], full[rep[types[b]]]), b
    namc = np.zeros((3, 7, 128, 512), np.float32)
    for qt, b0 in enumerate((0, 4, 12)):
        for bi in range(4):
            namc[qt, :, :, bi * 128:(bi + 1) * 128] = full[b0 + bi]
    for b0 in (4, 8):
        for bi in range(4):
            assert np.array_equal(full[b0 + bi], full[5])
    qdl = [[dl for dl in range(-3, 4) if namc[qt, dl + 3].any()] for qt in range(3)]
    return namc, qdl


def _dft(n):
    t = np.arange(n, dtype=np.int64)
    m = (t[:, None] * t[None, :]) % n
    ang = 2.0 * np.pi * m.astype(np.float64) / n
    return np.cos(ang), np.sin(ang)


_NA_MASKC, _NA_QDELTAS = _na_masks()


def _consts():
    c = {}
    cos, sin = _rope_tables()
    c["rope"] = np.stack([cos, sin], 1).copy()
    ident = np.eye(128, dtype=np.float32)
    onesbd = np.zeros((128, 128), np.float32)
    onesbd[:64, :64] = 1.0
    onesbd[64:, 64:] = 1.0
    c64, s64 = _dft(64)
    cbd = np.zeros((128, 128), np.float64)
    sbd = np.zeros((128, 128), np.float64)
    for g in range(2):
        cbd[g * 64:(g + 1) * 64, g * 64:(g + 1) * 64] = c64 / 8.0
        sbd[g * 64:(g + 1) * 64, g * 64:(g + 1) * 64] = s64 / 8.0
    win = np.zeros((3, 128, 128), np.float32)
    i = np.arange(128)[:, None]
    j = np.arange(128)[None, :]
    win[0] = (j <= i)
    win[1] = 1.0
    win[2] = (i <= j)
    permm = np.zeros((128, 128), np.float32)
    for m_ in range(128):
        permm[(m_ // 64) * 64 + _partner(m_ % 64), m_] = 1.0
    cb = np.concatenate([ident, onesbd, cbd.astype(np.float32), sbd.astype(np.float32), permm], axis=1)
    c["cbf"] = cb.astype(ml_dtypes.bfloat16)
    winc = np.zeros((3, 3, 128, 512), np.float32)
    for qt, b0 in enumerate((0, 4, 12)):
        for bi in range(4):
            for dl in (-1, 0, 1):
                if 0 <= b0 + bi + dl < 16:
                    winc[qt, dl + 1, :, bi * 128:(bi + 1) * 128] = win[dl + 1]
    mk = np.concatenate([winc[qt, d_] for qt in range(3) for d_ in range(3)] +
                        [_NA_MASKC[qt, d_] for qt in range(3) for d_ in range(7)], axis=1)
    c["maskc"] = mk.astype(ml_dtypes.bfloat16)
    cf = np.concatenate([ident, np.full((128, 128), 1.0 / D, np.float32), np.ones((128, 128), np.float32)], axis=1)
    c["cf32"] = cf.astype(np.float32)
    cs, ss = _dft(NL)
    c["dftc"] = (cs / np.sqrt(NL)).astype(ml_dtypes.bfloat16)
    c["dfts"] = (-ss / np.sqrt(NL)).astype(ml_dtypes.bfloat16)
    cs, ss = _dft(NCX)
    c["dftc_c"] = (cs / np.sqrt(NCX)).astype(ml_dtypes.bfloat16)
    c["dfts_c"] = (-ss / np.sqrt(NCX)).astype(ml_dtypes.bfloat16)
    return c


def _wina_cols():
    offs = {"qA": 0, "kA": 256, "vA": 384, "f": 512, "qC": 768, "kC": 1024, "vC": 1152,
            "qD": 1280, "kD": 1536, "vD": 1664}

    def qch(base, pair):
        return [base + hq * 64 + _ORD[p] for hq in pair for p in range(64)]
    cols = []
    for mname in ("A", "C", "D"):
        qb, kb = offs["q" + mname], offs["k" + mname]
        cols += qch(qb, (0, 2)) + qch(qb, (1, 3)) + qch(kb, (0, 1))
    cols += list(range(offs["f"], offs["f"] + 256))
    vcols = list(range(384, 512)) + list(range(1152, 1280)) + list(range(1664, 1792))
    return np.array(cols), np.array(vcols)


def build_program(debug=False, stop_after=None):
    nc = bass.Bass("TRN2", target_bir_lowering=False)

    def din(name, shape, dt=F32):
        return nc.dram_tensor(name, list(shape), dt, kind="ExternalInput").ap()

    def dscr(name, shape, dt):
        return nc.dram_tensor(name, list(shape), dt, kind=("ExternalOutput" if debug else "Internal")).ap()

    x2 = din("x2", [2, NT, D])
    cT = din("cT", [128, KC, 4])
    w_mod = din("w_mod", [L, D, 6 * D])
    b_modT = din("b_modT", [128, L, 48])
    w_inA = din("w_inA", [L, D, NA_COLS])
    w_inV = din("w_inV", [L, D, 384])
    w_inG = din("w_inG", [L, D, 4096])
    gainT = din("gainT", [128, L, 4])
    sinkB = din("sinkB", [128, L * 4])
    tbs = din("tbs", [L, 128, 4 * 15 * 64])
    rope = din("rope", [128, 2, NT])
    cbf = din("cbf", [128, 5 * 128], BF16)
    maskc = din("maskc", [128, 30 * 512], BF16)
    cf32 = din("cf32", [128, 384])
    dftc = din("dftc", [NL, NL], BF16)
    dfts = din("dfts", [NL, NL], BF16)
    dftc_c = din("dftc_c", [NCX, NCX], BF16)
    dfts_c = din("dfts_c", [NCX, NCX], BF16)
    w_brP = din("w_brP", [L, 1024, D])
    w_out = din("w_out", [L, D, D])
    lnT = din("lnT", [128, L * 4 * KC])
    w_router = din("w_router", [L, D, E])
    w_gate = din("w_gate", [L, E, D, D])
    w_up = din("w_up", [L, E, D, D])
    w_down = din("w_down", [L, E, D, D])
    out = nc.dram_tensor("out", [2, NL, D], F32, kind="ExternalOutput").ap()

    hT_d = dscr("hT_d", [2, 128, KC, NT], F32)
    uT_d = dscr("uT_d", [2, 128, KC, NT], BF16)
    qT_d = dscr("qT_d", [2, 128, 6, NT], BF16)
    kT_d = dscr("kT_d", [2, 128, 3, NT], BF16)
    v_d = dscr("v_d", [2, 128, 18, 768], BF16)
    fT_d = dscr("fT_d", [2, 128, 2, NT], BF16)
    brT_d = dscr("brT_d", [2, 128, 8, NT], BF16)
    u2tok_d = dscr("u2tok_d", [2 * NT, D], BF16)
    ffn_d = dscr("ffn_d", [2 * NT, D], F32)
    aff_d = dscr("aff_d", [2, 16, NT], F32)

    CH512 = [(0, 256)] + [(256 + i * 512, 512) for i in range(4)]
    CH256 = [(i * 256, 256) for i in range(9)]

    with ExitStack() as es:
        S = Sched(nc, es)

        uid = [0]

        def sb(shape, dt, name, stack=es):
            uid[0] += 1
            nm = "%s_%d" % (name, uid[0])
            return T(stack.enter_context(nc.sbuf_tensor(nm, list(shape), dt)), nm)

        PSF = [T(es.enter_context(nc.psum_tensor("psf%d" % i, [128, 512], F32)), "psf%d" % i) for i in range(6)]
        PSB = [T(es.enter_context(nc.psum_tensor("psb%d" % i, [128, 1024], BF16)), "psb%d" % i) for i in range(2)]

        CB = sb([128, 5 * 128], BF16, "CB")
        CF = sb([128, 384], F32, "CF")
        MOD = sb([128, L, 48, 4], F32, "MOD")
        LNT = sb([128, L * 4 * KC], F32, "LNT")
        GAIN = sb([128, L, 4], F32, "GAIN")
        SINKE = sb([128, L * 4], F32, "SINKE")
        S.dma("sp", CB.full(), cbf[:, :])
        S.dma("sp", CF.full(), cf32[:, :])
        S.dma("sp", LNT.full(), lnT[:, :])
        S.dma("sp", GAIN.full(), gainT[:, :, :])
        S.dma("sp", SINKE.full(), sinkB[:, :])
        S.act(SINKE.full(), SINKE.full(), AF.Exp)
        ident_b = CB[:, 0:128]
        onesbd = CB[:, 128:256]
        c64bd = CB[:, 256:384]
        s64bd = CB[:, 384:512]
        permm = CB[:, 512:640]

        ident_f = CF[:, 0:128]
        ones_mean = CF[:, 128:256]
        ones_f = CF[:, 256:384]

        def lnv(l, which, kc):
            i_ = (l * 4 + which) * KC + kc
            return LNT[:, i_:i_ + 1]

        def mod(l, kind, kc, j):
            return MOD[:, l, kind * 8 + kc, j:j + 1]

        def cast_load(dst, src_ap_fn, ncols, eng="pool", step=1024):
            for c0 in range(0, ncols, step):
                c1 = min(ncols, c0 + step)
                S.dma(eng, dst[:, :, c0:c1], src_ap_fn(c0, c1))

        marks = []

        def phase_end(name="?"):
            S.barrier()
            marks.append((name, S.cnt["c_pe"]))

        with ExitStack() as ps:
            cTs = sb([128, KC, 4], F32, "cTs", ps)
            bm = sb([128, L, 48], F32, "bm", ps)
            wm = [sb([128, KC, 768], F32, "wm%d" % i, ps) for i in range(2)]
            S.dma("sp", cTs.full(), cT[:, :, :])
            S.dma("sp", bm.full(), b_modT[:, :, :])
            S.act(cTs.full(), cTs.full(), AF.Silu)
            n = 0
            for l in range(L):
                pm = PSF[l]
                for blk in range(8):
                    w = wm[n % 2]
                    n += 1
                    S.dma("sp", w.full(), w_mod[l, :, blk * 768:(blk + 1) * 768].rearrange("(kc p) n -> p kc n", p=128))
                    for cc in range(6):
                        ccg = blk * 6 + cc
                        for kc in range(KC):
                            S.mm(pm[:, ccg * 4:ccg * 4 + 4], w[:, kc, cc * 128:(cc + 1) * 128], cTs[:, kc, :],
                                 start=(kc == 0), stop=(kc == KC - 1))
                for ccg in range(48):
                    S.ts("dve", MOD[:, l, ccg, :], pm[:, ccg * 4:ccg * 4 + 4], bm[:, l, ccg:ccg + 1], ALU.add)
                for kind in (1, 4):
                    S.ts("dve", MOD[:, l, kind * 8:(kind + 1) * 8, :], MOD[:, l, kind * 8:(kind + 1) * 8, :], 1.0, ALU.add)
                for kind in (2, 5):
                    S.ts("dve", MOD[:, l, kind * 8:(kind + 1) * 8, :], MOD[:, l, kind * 8:(kind + 1) * 8, :],
                         1.0 / ALPHA, ALU.mult)
            phase_end("p0")

        with ExitStack() as ps:
            xin = [sb([128, 4, D], F32, "xin%d" % i, ps) for i in range(2)]
            hst = [sb([128, KC, 512], F32, "hst%d" % i, ps) for i in range(2)]
            n = 0
            for s in range(2):
                for (t0, Tn) in CH512:
                    nt = Tn // 128
                    xi, hs = xin[n % 2], hst[n % 2]
                    n += 1
                    S.dma("sp", xi[:, 0:nt, :], x2[s, t0:t0 + Tn, :].rearrange("(ti p) d -> p ti d", p=128))
                    for kc in range(KC):
                        pt = PSF[kc % 4]
                        for ti in range(nt):
                            S.tr(pt[:, ti * 128:(ti + 1) * 128], xi[:, ti, kc * 128:(kc + 1) * 128], ident_f)
                        S.copy("act" if kc % 2 == 0 else "dve", hs[:, kc, 0:Tn], pt[:, 0:Tn])
                    S.dma("pool", hT_d[s, :, :, t0:t0 + Tn], hs[:, :, 0:Tn])
            phase_end("pin")

        if stop_after == "pin":
            S.barrier()
            S.emit()
            return nc

        def layernorm(zT, Tn, l, which_g, tmp):
            zsq, mean_sb, var_sb, accs, accq = tmp
            pm, pq = PSF[4], PSF[5]
            for kc in range(KC):
                q = zsq[kc % 2]
                S.act(q[:, 0:Tn], zT[:, kc, 0:Tn], AF.Square)
                if kc == 1:
                    S.tt("pool", accs[:, 0:Tn], zT[:, 0, 0:Tn], zT[:, 1, 0:Tn], ALU.add)
                    S.tt("dve", accq[:, 0:Tn], zsq[0][:, 0:Tn], zsq[1][:, 0:Tn], ALU.add)
                elif kc > 1:
                    S.tt("pool", accs[:, 0:Tn], accs[:, 0:Tn], zT[:, kc, 0:Tn], ALU.add)
                    S.tt("dve", accq[:, 0:Tn], accq[:, 0:Tn], q[:, 0:Tn], ALU.add)
            S.mm(pm[:, 0:Tn], ones_mean, accs[:, 0:Tn])
            S.mm(pq[:, 0:Tn], ones_mean, accq[:, 0:Tn])
            S.copy("act", mean_sb[:, 0:Tn], pm[:, 0:Tn])
            S.tt("dve", var_sb[:, 0:Tn], mean_sb[:, 0:Tn], mean_sb[:, 0:Tn], ALU.mult)
            S.tt("dve", var_sb[:, 0:Tn], pq[:, 0:Tn], var_sb[:, 0:Tn], ALU.subtract)
            S.act(var_sb[:, 0:Tn], var_sb[:, 0:Tn], AF.Sqrt, bias=EPSV[:, 0:1])
            S.recip(var_sb[:, 0:Tn], var_sb[:, 0:Tn])
            for kc in range(KC):
                S.tt("pool", zT[:, kc, 0:Tn], zT[:, kc, 0:Tn], mean_sb[:, 0:Tn], ALU.subtract)
                S.tt("dve", zT[:, kc, 0:Tn], zT[:, kc, 0:Tn], var_sb[:, 0:Tn], ALU.mult)
                S.act(zT[:, kc, 0:Tn], zT[:, kc, 0:Tn], AF.Identity, scale=lnv(l, which_g, kc), bias=lnv(l, which_g + 1, kc))

        EPSV = sb([128, 2], F32, "EPSV")
        S.memset("dve", EPSV[:, 0:1], LN_EPS_S)
        S.memset("dve", EPSV[:, 1:2], RMS_EPS)

        def p1(l, samples):
            with ExitStack() as ps:
                wA = sb([128, KC, NA_COLS], BF16, "wA", ps)
                wV = sb([128, KC, 384], BF16, "wV", ps)
                ROPE = sb([128, 2, NT], F32, "ROPE", ps)
                hTc = [sb([128, KC, 512], F32, "hTc%d" % i, ps) for i in range(2)]
                uTc = [sb([128, KC, 512], BF16, "uTc%d" % i, ps) for i in range(2)]
                stg = [sb([128, 512], BF16, "stg%d" % i, ps) for i in range(4)]
                sq = sb([128, 512], BF16, "sq", ps)
                qb_ = sb([128, 512], BF16, "qb_", ps)
                rs = sb([128, 512], F32, "rs", ps)
                t1 = sb([128, 512], F32, "t1", ps)
                t2 = sb([128, 512], F32, "t2", ps)
                vst = [sb([128, 4, 6, 128], BF16, "vst%d" % i, ps) for i in range(2)]
                cast_load(wA, lambda c0, c1: w_inA[l, :, c0:c1].rearrange("(kc p) n -> p kc n", p=128), NA_COLS)
                cast_load(wV, lambda c0, c1: w_inV[l, :, c0:c1].rearrange("(kc p) n -> p kc n", p=128), 384)
                S.dma("sp", ROPE.full(), rope[:, :, :])
                for i in range(2):
                    S.memset("pool", vst[i].full(), 1.0)
                nst = 0
                for s in samples:
                    for ci, (t0, Tn) in enumerate(CH512):
                        j = 2 if t0 == 0 else s
                        h, u = hTc[ci % 2], uTc[ci % 2]
                        S.dma("sp", h[:, :, 0:Tn], hT_d[s, :, :, t0:t0 + Tn])
                        for kc in range(KC):
                            S.act(u[:, kc, 0:Tn], h[:, kc, 0:Tn], AF.Identity, scale=mod(l, 1, kc, j), bias=mod(l, 0, kc, j))
                        S.dma("pool", uT_d[s, :, :, t0:t0 + Tn], u[:, :, 0:Tn])

                        def proj(ps_t, cc):
                            for kc in range(KC):
                                S.mm(ps_t[:, 0:Tn], wA[:, kc, cc * 128:(cc + 1) * 128], u[:, kc, 0:Tn],
                                     start=(kc == 0), stop=(kc == KC - 1))

                        def store(dst_ap, v):
                            S.dma("pool", dst_ap, v)

                        plain = [(6, qT_d[s, :, 4, t0:t0 + Tn]), (7, qT_d[s, :, 5, t0:t0 + Tn]),
                                 (8, kT_d[s, :, 2, t0:t0 + Tn]), (9, fT_d[s, :, 0, t0:t0 + Tn]),
                                 (10, fT_d[s, :, 1, t0:t0 + Tn])]
                        for n_, (cc, dst) in enumerate(plain):
                            pt = PSF[n_ % 2]
                            proj(pt, cc)
                            st = stg[nst % 4]
                            nst += 1
                            S.copy("act", st[:, 0:Tn], pt[:, 0:Tn])
                            store(dst, st[:, 0:Tn])
                        roped = [(0, 0, True, qT_d[s, :, 0, t0:t0 + Tn]), (1, 0, True, qT_d[s, :, 1, t0:t0 + Tn]),
                                 (2, 2, True, kT_d[s, :, 0, t0:t0 + Tn]),
                                 (3, None, False, qT_d[s, :, 2, t0:t0 + Tn]),
                                 (4, None, False, qT_d[s, :, 3, t0:t0 + Tn]),
                                 (5, None, False, kT_d[s, :, 1, t0:t0 + Tn])]
                        for n_, (cb_, g0, norm, dst) in enumerate(roped):
                            pa = PSF[n_ % 4]
                            proj(pa, cb_)
                            cosv = ROPE[:, 0, t0:t0 + Tn]
                            st = stg[nst % 4]
                            nst += 1
                            if norm:
                                S.act(sq[:, 0:Tn], pa[:, 0:Tn], AF.Square)
                                S.mm(PSF[4][:, 0:Tn], onesbd, sq[:, 0:Tn])
                                S.act(rs[:, 0:Tn], PSF[4][:, 0:Tn], AF.Sqrt, scale=1.0 / 64.0, bias=EPSV[:, 1:2])
                                S.recip(rs[:, 0:Tn], rs[:, 0:Tn])
                                S.stt(t1[:, 0:Tn], pa[:, 0:Tn], GAIN[:, l, g0:g0 + 1], cosv, ALU.mult, ALU.mult)
                                for B_ in (0, 32, 64, 96):
                                    Bp = B_ ^ 32
                                    S.stt(t2[B_:B_ + 32, 0:Tn], pa[Bp:Bp + 32, 0:Tn], GAIN[Bp:Bp + 32, l, g0:g0 + 1],
                                          ROPE[Bp:Bp + 32, 1, t0:t0 + Tn], ALU.mult, ALU.mult)
                                S.tt("pool", t1[:, 0:Tn], t1[:, 0:Tn], t2[:, 0:Tn], ALU.add)
                                S.tt("dve", st[:, 0:Tn], t1[:, 0:Tn], rs[:, 0:Tn], ALU.mult)
                            else:
                                S.tt("dve", t1[:, 0:Tn], pa[:, 0:Tn], cosv, ALU.mult)
                                for B_ in (0, 32, 64, 96):
                                    Bp = B_ ^ 32
                                    S.tt("dve", t2[B_:B_ + 32, 0:Tn], pa[Bp:Bp + 32, 0:Tn],
                                         ROPE[Bp:Bp + 32, 1, t0:t0 + Tn], ALU.mult)
                                S.tt("pool", st[:, 0:Tn], t1[:, 0:Tn], t2[:, 0:Tn], ALU.add)
                            store(dst, st[:, 0:Tn])
                        vs = vst[ci % 2]
                        nt = Tn // 128
                        for ti in range(nt):
                            pv = PSF[5]
                            for kc in range(KC):
                                S.mm(pv[:, 0:384], u[:, kc, ti * 128:(ti + 1) * 128], wV[:, kc, :],
                                     start=(kc == 0), stop=(kc == KC - 1))
                            S.copy("act", vs[:, ti, :, 0:64], pv[:, 0:384].re("p (m d) -> p m d", d=64))
                        S.dma("pool", v_d[s, :, t0 // 128:t0 // 128 + nt, :].rearrange("p t (m d) -> p t m d", d=128),
                              vs[:, 0:nt, :, :])
            phase_end("p1")

        def p2(s, l):
            with ExitStack() as ps:
                kT = sb([128, 3, NT], BF16, "kT", ps)
                Vt = sb([128, 18, 768], BF16, "Vt", ps)
                TB = sb([128, 4 * 15 * 64], F32, "TB", ps)
                MK = sb([128, 30 * 512], BF16, "MK", ps)
                qcb = [sb([128, 6, 512], BF16, "qc%d" % i, ps) for i in range(2)]
                Pb = [sb([128, 512], BF16, "Pb%d" % i, ps) for i in range(4)]
                Pe = [sb([128, 512], BF16, "Pe%d" % i, ps) for i in range(4)]
                Pm = [sb([128, 512], BF16, "Pm%d" % i, ps) for i in range(3)]
                sbias = [sb([128, 512], F32, "sbias%d" % i, ps) for i in range(3)]
                rd = sb([128, 512], F32, "rd", ps)
                dsb = sb([128, 512], F32, "dsb", ps)
                brst = [sb([128, 512], BF16, "brst%d" % i, ps) for i in range(2)]
                S.dma("sp", kT.full(), kT_d[s, :, :, :])
                S.dma("sp", Vt.full(), v_d[s, :, :, :])
                S.dma("sp", TB.full(), tbs[l, :, :])
                S.dma("sp", MK.full(), maskc[:, :])
                TBv = TB.full().re("p (h m c) -> p h m c", h=4, m=15)

                def winc(qt, dl):
                    i_ = qt * 3 + dl + 1
                    return MK[:, i_ * 512:(i_ + 1) * 512]

                def namc(qt, dl):
                    i_ = 9 + qt * 7 + dl + 3
                    return MK[:, i_ * 512:(i_ + 1) * 512]
                qch = ([(0, 256, True)] if l == 0 else []) + [(256 + i * 512, 512, False) for i in range(4)]
                cnt = {}

                def nxt(k, lst):
                    c_ = cnt.get(k, 0)
                    cnt[k] = c_ + 1
                    return lst[c_ % len(lst)]
                for qi, (t0, Tn, is_ctx) in enumerate(qch):
                    qc = qcb[qi % 2]
                    S.dma("sp", qc[:, :, 0:Tn], qT_d[s, :, :, t0:t0 + Tn])
                    b0 = (t0 - NCX) // 128
                    qt = 0 if b0 == 0 else (2 if b0 == 12 else 1)
                    for m in range(3):
                        for qcnk in range(2):
                            bst = nxt("b", brst)
                            for ph in range(2):
                                hq = qcnk + 2 * ph
                                O = nxt("o", [PSF[3], PSF[4]])
                                p0, p1_ = ph * 64, (ph + 1) * 64
                                qv = qc[p0:p1_, m * 2 + qcnk, 0:Tn]
                                vo = (m * 2 + ph) * 128
                                if is_ctx:
                                    steps = [("g", 0), ("g", 1)]
                                elif m == 0:
                                    steps = [("g", kc) for kc in range(18)]
                                else:
                                    dls = [-1, 0, 1] if m == 1 else _NA_QDELTAS[qt]
                                    steps = [("g", 0), ("g", 1)] + [("b", dl) for dl in dls]

                                def emit_S(st):
                                    Sp = nxt("s", [PSF[0], PSF[1], PSF[2], PSF[5]])
                                    if st[0] == "g":
                                        kc = st[1]
                                        S.mm(Sp[:, 0:Tn], kT[p0:p1_, m, kc * 128:(kc + 1) * 128], qv)
                                    else:
                                        for bi in range(4):
                                            kc = 2 + min(15, max(0, b0 + bi + st[1]))
                                            S.mm(Sp[:, bi * 128:(bi + 1) * 128], kT[p0:p1_, m, kc * 128:(kc + 1) * 128],
                                                 qc[p0:p1_, m * 2 + qcnk, bi * 128:(bi + 1) * 128])
                                    return Sp

                                def emit_rest(st, Sp, first, last):
                                    if st[0] == "g":
                                        kc = st[1]
                                        P = nxt("p", Pb)
                                        S.act(P[:, 0:Tn], Sp[:, 0:Tn], AF.Exp, scale=0.125)
                                        S.mm(O[:, 0:Tn], Vt[:, kc, vo:vo + 128], P[:, 0:Tn], start=first, stop=last)
                                        return
                                    dl = st[1]
                                    pe_ = nxt("e", Pe)
                                    if m == 2:
                                        sbv = nxt("e2", sbias)
                                        m0 = 7 - 2 * dl
                                        for bi in range(4):
                                            S.stt(sbv[:, bi * 128:(bi + 1) * 128].re("p (j c) -> p j c", c=64),
                                                  Sp[:, bi * 128:(bi + 1) * 128].re("p (j c) -> p j c", c=64), 0.125,
                                                  TBv[:, hq, m0:m0 + 2, :], ALU.mult, ALU.add)
                                        S.act(pe_.full(), sbv.full(), AF.Exp)
                                        pfin = nxt("m", Pm)
                                        S.tt("pool", pfin.full(), pe_.full(), namc(qt, dl), ALU.mult)
                                    else:
                                        S.act(pe_.full(), Sp.full(), AF.Exp, scale=0.125)
                                        if dl == 0:
                                            pfin = pe_
                                        else:
                                            pfin = nxt("m", Pm)
                                            S.tt("dve", pfin.full(), pe_.full(), winc(qt, dl), ALU.mult)
                                    valid = [bi for bi in range(4) if 0 <= b0 + bi + dl < 16]
                                    for bi in valid:
                                        kc = 2 + b0 + bi + dl
                                        S.mm(O[:, bi * 128:(bi + 1) * 128], Vt[:, kc, vo:vo + 128],
                                             pfin[:, bi * 128:(bi + 1) * 128], start=False, stop=(last and bi == valid[-1]))
                                n_ = len(steps)
                                LA = 3
                                sps = [None] * n_
                                for i_ in range(min(LA, n_)):
                                    sps[i_] = emit_S(steps[i_])
                                for i_ in range(n_):
                                    if i_ + LA < n_:
                                        sps[i_ + LA] = emit_S(steps[i_ + LA])
                                    emit_rest(steps[i_], sps[i_], i_ == 0, i_ == n_ - 1)
                                if m == 1:
                                    S.ts("dve", dsb[64:128, 0:Tn], O[64:128, 0:Tn],
                                         SINKE[64:128, l * 4 + hq:l * 4 + hq + 1], ALU.add)
                                    S.recip(rd[0:64, 0:Tn], dsb[64:128, 0:Tn])
                                else:
                                    S.recip(rd[0:64, 0:Tn], O[64:128, 0:Tn])
                                S.tt("dve", bst[p0:p1_, 0:Tn], O[0:64, 0:Tn], rd[0:64, 0:Tn], ALU.mult)
                            cidx = (0, 4, 6)[m] + qcnk
                            S.dma("pool", brT_d[s, :, cidx, t0:t0 + Tn], bst[:, 0:Tn])
            phase_end("p2")

        def p2b(s, l):
            with ExitStack() as ps:
                fT = sb([128, 2, NT], BF16, "fT", ps)
                gcs = sb([128, 18, 4, 128], BF16, "gcs", ps)
                Cc = [sb([128, 16, 512], BF16, "Cc%d" % i, ps) for i in range(2)]
                Sc = [sb([128, 16, 512], BF16, "Sc%d" % i, ps) for i in range(2)]
                ost = [sb([128, 512], BF16, "ost%d" % i, ps) for i in range(2)]
                S.dma("sp", fT.full(), fT_d[s, :, :, :])
                tcs = range(18) if l == 0 else range(2, 18)
                for tc in tcs:
                    pg = PSF[tc % 2]
                    for fc in range(2):
                        S.mm(pg[:, (fc * 2) * 128:(fc * 2 + 1) * 128], fT[:, fc, tc * 128:(tc + 1) * 128], c64bd)
                        S.mm(pg[:, (fc * 2 + 1) * 128:(fc * 2 + 2) * 128], fT[:, fc, tc * 128:(tc + 1) * 128], s64bd)
                    S.copy("act" if tc % 2 == 0 else "dve", gcs[:, tc, :, :], pg[:, 0:512].re("p (a c) -> p a c", c=128))
                n = 0
                jobs = [(256 + i * 512, 512, 2, 16, dftc, dfts) for i in range(4)]
                if l == 0:
                    jobs = [(0, 256, 0, 2, dftc_c, dfts_c)] + jobs
                for ji, (t0, Tn, tc0, ntc, mc, msn) in enumerate(jobs):
                    c_, s_ = Cc[ji % 2], Sc[ji % 2]
                    col0 = 0 if t0 == 0 else t0 - NCX
                    S.dma("sp", c_[:, 0:ntc, 0:Tn], mc[:, col0:col0 + Tn].rearrange("(tc p) n -> p tc n", p=128))
                    S.dma("sp", s_[:, 0:ntc, 0:Tn], msn[:, col0:col0 + Tn].rearrange("(tc p) n -> p tc n", p=128))
                    for fc in range(2):
                        po = PSF[2 + (n % 2)]
                        for k_ in range(ntc):
                            S.mm(po[:, 0:Tn], gcs[:, tc0 + k_, fc * 2, :], c_[:, k_, 0:Tn], start=(k_ == 0), stop=False)
                            S.mm(po[:, 0:Tn], gcs[:, tc0 + k_, fc * 2 + 1, :], s_[:, k_, 0:Tn], start=False, stop=(k_ == ntc - 1))
                        o_ = ost[n % 2]
                        n += 1
                        S.copy("act", o_[:, 0:Tn], po[:, 0:Tn])
                        S.dma("pool", brT_d[s, :, 2 + fc, t0:t0 + Tn], o_[:, 0:Tn])
            phase_end("p2b")

        def p3_weights(l, ws):
            wG = sb([128, KC, 4096], BF16, "wG", ws)
            wB = sb([128, KC, D], BF16, "wB", ws)
            wO = sb([128, KC, D], BF16, "wO", ws)
            wR = sb([128, KC, E], F32, "wR", ws)
            cast_load(wG, lambda c0, c1: w_inG[l, :, c0:c1].rearrange("(kc p) n -> p kc n", p=128), 4096)
            cast_load(wB, lambda c0, c1: w_brP[l, :, c0:c1].rearrange("(kc p) n -> p kc n", p=128), D)
            cast_load(wO, lambda c0, c1: w_out[l, :, c0:c1].rearrange("(kc p) n -> p kc n", p=128), D)
            S.dma("sp", wR.full(), w_router[l, :, :].rearrange("(kc p) n -> p kc n", p=128))
            return wG, wB, wO, wR

        def p3(s, l, wts):
            wG, wB, wO, wR = wts
            with ExitStack() as ps:
                TT = 512
                uTc = sb([128, KC, TT], BF16, "uTc3", ps)
                brc = sb([128, 8, TT], BF16, "brc", ps)
                hTc = sb([128, KC, TT], F32, "hTc3", ps)
                zT = sb([128, KC, TT], F32, "zT", ps)
                mrg = sb([128, KC, TT], BF16, "mrg", ps)
                u2f = hTc
                u2b = brc
                gate = [sb([128, TT], F32, "gate%d" % i, ps) for i in range(3)]
                tmpm = [sb([128, TT], F32, "tmpm%d" % i, ps) for i in range(3)]
                acc = sb([128, TT], F32, "acc", ps)
                zsq = [sb([128, TT], F32, "zsq%d" % i, ps) for i in range(2)]
                mean_sb = sb([128, TT], F32, "mean_sb", ps)
                var_sb = sb([128, TT], F32, "var_sb", ps)
                accs = sb([128, TT], F32, "accs", ps)
                accq = sb([128, TT], F32, "accq", ps)
                ex = sb([16, TT], F32, "ex", ps)
                rsum = sb([16, TT], F32, "rsum", ps)
                utok = [sb([128, D], BF16, "utok%d" % i, ps) for i in range(2)]
                chunks = CH512 if l == 0 else CH512[1:]
                ng = 0
                for (t0, Tn) in chunks:
                    j = 2 if t0 == 0 else s
                    S.dma("sp", uTc[:, :, 0:Tn], uT_d[s, :, :, t0:t0 + Tn])
                    S.dma("sp", brc[:, :, 0:Tn], brT_d[s, :, :, t0:t0 + Tn])
                    S.dma("sp", hTc[:, :, 0:Tn], hT_d[s, :, :, t0:t0 + Tn])
                    for dc in range(KC):
                        for i in range(4):
                            pg = PSF[ng % 3]
                            pb = PSF[3 + ng % 3]
                            g_ = gate[ng % 3]
                            tm = tmpm[ng % 3]
                            ng += 1
                            for kc in range(KC):
                                S.mm(pg[:, 0:Tn], wG[:, kc, i * D + dc * 128:i * D + (dc + 1) * 128], uTc[:, kc, 0:Tn],
                                     start=(kc == 0), stop=(kc == KC - 1))
                            S.act(g_[:, 0:Tn], pg[:, 0:Tn], AF.Sigmoid)
                            for hf in range(2):
                                S.mm(pb[:, 0:Tn], wB[:, i * 2 + hf, dc * 128:(dc + 1) * 128], brc[:, i * 2 + hf, 0:Tn],
                                     start=(hf == 0), stop=(hf == 1))
                            if i == 0:
                                S.tt("dve", acc[:, 0:Tn], g_[:, 0:Tn], pb[:, 0:Tn], ALU.mult)
                            else:
                                S.tt("dve", tm[:, 0:Tn], g_[:, 0:Tn], pb[:, 0:Tn], ALU.mult)
                                if i < 3:
                                    S.tt("pool", acc[:, 0:Tn], acc[:, 0:Tn], tm[:, 0:Tn], ALU.add)
                                else:
                                    S.tt("pool", mrg[:, dc, 0:Tn], acc[:, 0:Tn], tm[:, 0:Tn], ALU.add)
                    for dc in range(KC):
                        py = PSF[dc % 2]
                        for kc in range(KC):
                            S.mm(py[:, 0:Tn], wO[:, kc, dc * 128:(dc + 1) * 128], mrg[:, kc, 0:Tn],
                                 start=(kc == 0), stop=(kc == KC - 1))
                        S.stt(zT[:, dc, 0:Tn], py[:, 0:Tn], mod(l, 2, dc, j), hTc[:, dc, 0:Tn], ALU.mult, ALU.add)
                    layernorm(zT, Tn, l, 0, (zsq, mean_sb, var_sb, accs, accq))
                    S.dma("pool", hT_d[s, :, :, t0:t0 + Tn], zT[:, :, 0:Tn])
                    for kc in range(KC):
                        S.act(u2f[:, kc, 0:Tn], zT[:, kc, 0:Tn], AF.Identity, scale=mod(l, 4, kc, j), bias=mod(l, 3, kc, j))
                        S.copy("pool", u2b[:, kc, 0:Tn], u2f[:, kc, 0:Tn])
                    pl = PSF[2]
                    for kc in range(KC):
                        S.mm(pl[0:16, 0:Tn], wR[:, kc, :], u2f[:, kc, 0:Tn], start=(kc == 0), stop=(kc == KC - 1))
                    S.act(ex[:, 0:Tn], pl[0:16, 0:Tn], AF.Exp)
                    S.mm(PSF[3][0:16, 0:Tn], ones_f[0:16, 0:16], ex[:, 0:Tn])
                    S.recip(rsum[:, 0:Tn], PSF[3][0:16, 0:Tn])
                    S.tt("dve", ex[:, 0:Tn], ex[:, 0:Tn], rsum[:, 0:Tn], ALU.mult)
                    S.dma("pool", aff_d[s, :, t0:t0 + Tn], ex[:, 0:Tn])
                    for ti in range(Tn // 128):
                        pt = PSB[ti % 2]
                        ut = utok[ti % 2]
                        for kc in range(KC):
                            S.tr(pt[:, kc * 128:(kc + 1) * 128], u2b[:, kc, ti * 128:(ti + 1) * 128], ident_b)
                        S.copy("act", ut.full(), pt.full())
                        r0 = s * NT + t0 + ti * 128
                        S.dma("pool", u2tok_d[r0:r0 + 128, :], ut.full())
            phase_end("p3")

        IDXT = sb([128, 5, E], I32, "IDXT")
        SELW = sb([128, 5, E], F32, "SELW")

        def p4(l):
            with ExitStack() as ps:
                work = sb([48, NL], F32, "work", ps)
                workc = sb([48, NCX], F32, "workc", ps)
                vals = sb([48, 288], F32, "vals", ps)
                idxu = sb([48, 288], U32, "idxu", ps)
                idxf = sb([48, 288], F32, "idxf", ps)
                lo = sb([16, 2, 288], F32, "lo", ps)
                tmpi = sb([128, E], F32, "tmpi", ps)
                S.memset("dve", work.full(), 0.0)
                S.memset("dve", workc.full(), 0.0)
                S.memset("dve", vals.full(), 0.0)
                S.memset("dve", idxu.full(), 0)
                for s in range(2):
                    S.dma("sp", work[32 * s:32 * s + 16, :], aff_d[s, :, NCX:NT])
                    if l == 0:
                        S.dma("sp", workc[32 * s:32 * s + 16, :], aff_d[s, :, 0:NCX])
                jobs = [(work, 0, 32)] + ([(workc, 256, 4)] if l == 0 else [])
                for (wt, slot0, rounds) in jobs:
                    for r_ in range(rounds):
                        sl = slice(slot0 + r_ * 8, slot0 + r_ * 8 + 8)
                        mx, ix, wk = _ap(vals[:, sl]), _ap(idxu[:, sl]), _ap(wt.full())
                        S.op("dve", lambda e, mx=mx, wk=wk: e.max(out=mx, in_=wk), reads=[wt.full()], writes=[vals.full()])
                        S.op("dve", lambda e, mx=mx, ix=ix, wk=wk: e.max_index(out=ix, in_max=mx, in_values=wk),
                             reads=[wt.full(), vals.full()], writes=[idxu.full()])
                        S.op("dve", lambda e, mx=mx, wk=wk: e.match_replace(out=wk, in_to_replace=mx, in_values=wk, imm_value=-1.0),
                             reads=[wt.full(), vals.full()], writes=[wt.full()])
                S.copy("dve", idxf.full(), idxu.full())
                S.ts("dve", idxf[0:16, 0:256], idxf[0:16, 0:256], float(NCX), ALU.add)
                S.ts("dve", idxf[32:48, 0:256], idxf[32:48, 0:256], float(NT + NCX), ALU.add)
                S.ts("dve", idxf[32:48, 256:288], idxf[32:48, 256:288], float(NT), ALU.add)
                S.copy("dve", lo[:, 0, :], idxf[32:48, :])
                S.copy("dve", lo[:, 1, :], vals[32:48, :])
                srcs = [(idxf[0:16, :], vals[0:16, :]), (lo[:, 0, :], lo[:, 1, :])]
                n = 0
                for s in range(2):
                    si, sv = srcs[s]
                    for half in range(2):
                        st = s * 2 + half
                        pt = PSF[n % 2]
                        n += 1
                        S.tr(pt[:, 0:16], si[:, half * 128:(half + 1) * 128], ident_f[0:16, 0:16])
                        S.copy("dve", tmpi.full(), pt[:, 0:16])
                        S.copy("dve", IDXT[:, st, :], tmpi.full())
                        S.tr(pt[:, 16:32], sv[:, half * 128:(half + 1) * 128], ident_f[0:16, 0:16])
                        S.copy("dve", SELW[:, st, :], pt[:, 16:32])
                    if l == 0:
                        pt = PSF[n % 2]
                        n += 1
                        S.tr(pt[0:32, 0:16], si[:, 256:288], ident_f[0:16, 0:16])
                        S.copy("dve", tmpi[0:32, :], pt[0:32, 0:16])
                        S.copy("dve", IDXT[32 * s:32 * s + 32, 4, :], tmpi[0:32, :])
                        S.tr(pt[0:32, 16:32], sv[:, 256:288], ident_f[0:16, 0:16])
                        S.copy("dve", SELW[32 * s:32 * s + 32, 4, :], pt[0:32, 16:32])
            phase_end("p4")

        def p5_weights(l, ws):
            wg = [sb([128, KC, D], BF16, "wg%d" % i, ws) for i in range(2)]
            wu = [sb([128, KC, D], BF16, "wu%d" % i, ws) for i in range(2)]
            wd = [sb([128, KC, D], BF16, "wd%d" % i, ws) for i in range(2)]
            for wt, src in ((wg[0], w_gate), (wu[0], w_up), (wd[0], w_down)):
                S.dma("pool", wt.full(), src[l, 0, :, :].rearrange("(kc p) n -> p kc n", p=128))
            return wg, wu, wd

        def p5(l, wts):
            wg, wu, wd = wts
            with ExitStack() as ps:
                xs = [sb([128, 5, D], BF16, "xs%d" % i, ps) for i in range(2)]
                xsT = sb([128, KC, 640], BF16, "xsT", ps)
                actT = sb([128, KC, 640], BF16, "actT", ps)
                sa = [sb([128, 512], F32, "sa%d" % i, ps) for i in range(2)]
                ysb = [sb([128, D], F32, "ysb%d" % i, ps) for i in range(2)]
                zer = sb([128, 2, D], F32, "zer", ps)
                FFN = T(ffn_d, "ffn_d")
                U2 = T(u2tok_d, "u2tok_d")
                S.memset("dve", zer.full(), 0.0)
                for i in range(2 * NT // 256):
                    S.dma("sp", V(FFN, ffn_d[i * 256:(i + 1) * 256, :].rearrange("(a p) d -> p a d", p=128)), zer.full())
                sts = [(0, 128), (1, 128), (2, 128), (3, 128)] + ([(4, 64)] if l == 0 else [])
                nsl = 576 if l == 0 else 512
                cgs = [(0, 512)] + ([(512, 576)] if l == 0 else [])

                def loadw(e_):
                    b_ = e_ % 2
                    for wt, src in ((wg[b_], w_gate), (wu[b_], w_up), (wd[b_], w_down)):
                        S.dma("pool", wt.full(), src[l, e_, :, :].rearrange("(kc p) n -> p kc n", p=128))

                def gathers(e_):
                    x_ = xs[e_ % 2]
                    for (st, np_) in sts:
                        o_ = _ap(x_[0:np_, st, :])
                        ix = _ap(IDXT[0:np_, st, e_:e_ + 1])

                        def fn(eng, o_=o_, ix=ix):
                            return eng.indirect_dma_start(out=o_, out_offset=None, in_=u2tok_d[:, :],
                                                          in_offset=bass.IndirectOffsetOnAxis(ap=ix, axis=0))
                        S.dma("pool", x_[0:np_, st, :], V(U2, None), extra_reads=[IDXT.full()], fn=fn)
                gathers(0)
                ny = 0
                for e_ in range(E):
                    if e_ + 1 < E:
                        loadw(e_ + 1)
                        gathers(e_ + 1)
                    b_ = e_ % 2
                    x_ = xs[b_]
                    for (st, np_) in sts:
                        pt = PSB[st % 2]
                        for kc in range(KC):
                            S.tr(pt[:, kc * 128:kc * 128 + np_], x_[0:np_, st, kc * 128:(kc + 1) * 128], ident_b[0:np_, 0:np_])
                        S.copy("act" if st % 2 == 0 else "dve", xsT[:, :, st * 128:st * 128 + np_],
                               pt.full().re("p (k t) -> p k t", t=128)[:, :, 0:np_])
                    for fc in range(KC):
                        for (c0, c1) in cgs:
                            pa, pu = PSF[(fc % 2) * 2], PSF[(fc % 2) * 2 + 1]
                            w_ = c1 - c0
                            for kc in range(KC):
                                S.mm(pa[:, 0:w_], wg[b_][:, kc, fc * 128:(fc + 1) * 128], xsT[:, kc, c0:c1],
                                     start=(kc == 0), stop=(kc == KC - 1))
                            for kc in range(KC):
                                S.mm(pu[:, 0:w_], wu[b_][:, kc, fc * 128:(fc + 1) * 128], xsT[:, kc, c0:c1],
                                     start=(kc == 0), stop=(kc == KC - 1))
                            s_ = sa[fc % 2]
                            S.act(s_[:, 0:w_], pa[:, 0:w_], AF.Silu)
                            S.tt("dve", actT[:, fc, c0:c1], s_[:, 0:w_], pu[:, 0:w_], ALU.mult)
                    for (st, np_) in sts:
                        y_ = ysb[ny % 2]
                        ny += 1
                        for dh in range(2):
                            py = PSF[4 + dh]
                            for fc in range(KC):
                                S.mm(py[0:np_, :], actT[:, fc, st * 128:st * 128 + np_], wd[b_][:, fc, dh * 512:(dh + 1) * 512],
                                     start=(fc == 0), stop=(fc == KC - 1))
                            S.act(y_[0:np_, dh * 512:(dh + 1) * 512], py[0:np_, :], AF.Copy, scale=SELW[0:np_, st, e_:e_ + 1])
                        yi = _ap(y_[0:np_, :])
                        ix = _ap(IDXT[0:np_, st, e_:e_ + 1])

                        def fn(eng, yi=yi, ix=ix):
                            return eng.indirect_dma_start(out=ffn_d[:, :], out_offset=bass.IndirectOffsetOnAxis(ap=ix, axis=0),
                                                          in_=yi, in_offset=None, compute_op=ALU.add)
                        S.dma("pool", V(FFN, None), y_[0:np_, :], extra_reads=[IDXT.full(), V(FFN, None)], fn=fn)
            phase_end("p5")

        def p6(s, l):
            with ExitStack() as ps:
                ftok2 = [sb([128, 4, D], F32, "ftok%d" % i, ps) for i in range(2)]
                hTc2 = [sb([128, KC, 512], F32, "hTc6%d" % i, ps) for i in range(2)]
                zT2 = [sb([128, KC, 512], F32, "zT6%d" % i, ps) for i in range(2)]
                zsq = [sb([128, 512], F32, "zsq6%d" % i, ps) for i in range(2)]
                mean_sb = sb([128, 512], F32, "mean6", ps)
                var_sb = sb([128, 512], F32, "var6", ps)
                accs = sb([128, 512], F32, "accs6", ps)
                accq = sb([128, 512], F32, "accq6", ps)
                otok = [sb([128, D], F32, "otok%d" % i, ps) for i in range(2)]
                chunks = CH512 if l == 0 else CH512[1:]
                for ci6, (t0, Tn) in enumerate(chunks):
                    ftok, hTc, zT = ftok2[ci6 % 2], hTc2[ci6 % 2], zT2[ci6 % 2]
                    j = 2 if t0 == 0 else s
                    nt = Tn // 128
                    r0 = s * NT + t0
                    S.dma("sp", ftok[:, 0:nt, :], ffn_d[r0:r0 + Tn, :].rearrange("(ti p) d -> p ti d", p=128))
                    S.dma("sp", hTc[:, :, 0:Tn], hT_d[s, :, :, t0:t0 + Tn])
                    for kc in range(KC):
                        pk = PSF[kc % 4]
                        for ti in range(nt):
                            S.tr(pk[:, ti * 128:(ti + 1) * 128], ftok[:, ti, kc * 128:(kc + 1) * 128], ident_f)
                        S.stt(zT[:, kc, 0:Tn], pk[:, 0:Tn], mod(l, 5, kc, j), hTc[:, kc, 0:Tn], ALU.mult, ALU.add)
                    layernorm(zT, Tn, l, 2, (zsq, mean_sb, var_sb, accs, accq))
                    if l < L - 1:
                        S.dma("pool", hT_d[s, :, :, t0:t0 + Tn], zT[:, :, 0:Tn])
                    else:
                        for ti in range(nt):
                            ot = otok[ti % 2]
                            for hf in range(2):
                                po = PSF[hf]
                                for k4 in range(4):
                                    kc = hf * 4 + k4
                                    S.tr(po[:, k4 * 128:(k4 + 1) * 128], zT[:, kc, ti * 128:(ti + 1) * 128], ident_f)
                                S.copy("act" if hf == 0 else "dve", ot[:, hf * 512:(hf + 1) * 512], po.full())
                            S.dma("pool", out[s, t0 - NCX + ti * 128:t0 - NCX + (ti + 1) * 128, :], ot.full())
            phase_end("p6")

        done = False
        for l in range(L):
            p1(l, (0,) if stop_after == "p1" else (0, 1))
            if stop_after == "p1":
                break
            for s in range(2):
                p2(s, l)
                with ExitStack() as ws:
                    wts = p3_weights(l, ws) if stop_after != "p2" else None
                    p2b(s, l)
                    if stop_after == "p2":
                        done = True
                        break
                    p3(s, l, wts)
                if stop_after == "p3":
                    done = True
                    break
            if done:
                break
            with ExitStack() as ws:
                wts5 = p5_weights(l, ws)
                p4(l)
                p5(l, wts5)
            if stop_after == "p5":
                break
            for s in range(2):
                p6(s, l)
            if stop_after == "l0":
                break
        S.barrier()
        S.emit()
        print("bass ops:", S.nops, {e: len(S.items[e]) for e in ENG})
        _CACHE["marks"] = marks
    return nc


_CACHE = {}


def _host_shared(inp):
    f = np.float32
    sh = dict(_consts())
    cols, vcols = _wina_cols()
    w_in = np.asarray(inp["w_in"], f)
    sh["w_inA"] = np.ascontiguousarray(w_in[:, :, cols])
    sh["w_inV"] = np.ascontiguousarray(w_in[:, :, vcols])
    sh["w_inG"] = np.ascontiguousarray(w_in[:, :, 1792:])
    sh["w_mod"] = np.asarray(inp["w_mod"], f)
    sh["b_modT"] = np.ascontiguousarray(np.asarray(inp["b_mod"], f).reshape(L, 48, 128).transpose(2, 0, 1))
    g = np.asarray(inp["qk_gain"], f)
    d_ = np.array([_ORD[p % 64] for p in range(128)])
    gt = np.stack([g[:, 0, d_], g[:, 0, d_], g[:, 1, d_], g[:, 1, d_]], -1)
    sh["gainT"] = np.ascontiguousarray(gt.transpose(1, 0, 2))
    sk = np.asarray(inp["sink_logit"], f).reshape(1, L * 4)
    sh["sinkB"] = np.ascontiguousarray(np.broadcast_to(sk, (128, L * 4)))
    rpb = np.asarray(inp["na_rpb"], f)
    p = np.arange(128)
    kcol = (p % 64)[:, None, None]
    i_ = (p // 64)[:, None, None]
    m_ = np.arange(15)[None, :, None]
    c_ = np.arange(64)[None, None, :]
    a_ = np.clip(14 - m_ + i_, 0, 14) + 0 * c_
    oc = np.clip(kcol - c_ + 15, 0, 30) + 0 * m_
    tb = rpb[:, :, a_, oc]
    sh["tbs"] = np.ascontiguousarray(tb.transpose(0, 2, 1, 3, 4).reshape(L, 128, 4 * 15 * 64))
    wb = np.asarray(inp["w_branch"], f)
    perm = np.concatenate([np.arange(0, 64), np.arange(128, 192), np.arange(64, 128), np.arange(192, 256)])
    wbp = wb.copy()
    for i in (0, 2, 3):
        wbp[:, i] = wb[:, i][:, perm]
    sh["w_brP"] = np.ascontiguousarray(wbp.reshape(L, 1024, D))
    sh["w_out"] = np.asarray(inp["w_out"], f)
    ln = np.stack([inp["ln1_g"], inp["ln1_b"], inp["ln2_g"], inp["ln2_b"]], 1).astype(f)
    sh["lnT"] = np.ascontiguousarray(ln.reshape(L, 4, KC, 128).transpose(3, 0, 1, 2).reshape(128, L * 4 * KC))
    sh["w_router"] = np.asarray(inp["w_router"], f)
    sh["w_gate"] = np.asarray(inp["w_gate"], f)
    sh["w_up"] = np.asarray(inp["w_up"], f)
    sh["w_down"] = np.asarray(inp["w_down"], f)
    return sh


def _host_core(inp, core):
    f = np.float32
    b0 = core * 2
    x = np.asarray(inp["x"], f)
    ctx = np.asarray(inp["ctx"], f)
    c = np.asarray(inp["c"], f)
    cc = np.asarray(inp["c_ctx"], f)
    d = {}
    d["x2"] = np.ascontiguousarray(np.concatenate([ctx[b0:b0 + 2], x[b0:b0 + 2]], axis=1))
    cv = np.stack([c[b0], c[b0 + 1], cc, np.zeros_like(cc)], -1)
    d["cT"] = np.ascontiguousarray(cv.reshape(KC, 128, 4).transpose(1, 0, 2))
    return d


def kernel(**inputs):
    n = 8
    if "nc" not in _CACHE:
        _CACHE["nc"] = build_program(debug=False)
    nc = _CACHE["nc"]
    sh = _host_shared(inputs)
    in_maps = []
    for core in range(n):
        m = dict(sh)
        m.update(_host_core(inputs, core))
        in_maps.append(m)
    res = run_bass_kernel_spmd(nc, in_maps, core_ids=list(range(n)))
    _CACHE["last"] = res
    outs = [np.asarray(r["out"], np.float32) for r in res.results]
    return np.concatenate(outs, axis=0)
```
